# Optimizing a Trainium2 kernel written in Bass

```python
import math
import jax, jax.numpy as jnp
from jax import lax
import numpy as np

D_MODEL = 1024
BATCH = 4
SEQ = 4096
DEPTH = 1
DEC_BATCH = 128
DEC_SEQ = 4
PAST_LEN = 16384
PAGE_SIZE = 128

HG_HEADS = 4
HG_DK = 128
HG_DV = 128
HG_FWIDTH = HG_HEADS * HG_DK
HG_WIDTH = HG_HEADS * HG_DV
HG_CHUNK = 64
N_Q = 8
N_KV = 2
HEAD_DIM = 64
GROUP = N_Q // N_KV
WINDOW = 128
ROT_DIM = HEAD_DIM // 4
ROPE_THETA = 500000.0
SWA_WIDTH = N_Q * HEAD_DIM
D_FF = 4 * D_MODEL
EPS = 1e-6
NEG_INF = -1e30
IN_SPLITS = (HG_FWIDTH, HG_FWIDTH, HG_WIDTH, HG_WIDTH, SWA_WIDTH, N_KV * HEAD_DIM, N_KV * HEAD_DIM, D_MODEL, D_MODEL)
D_IN = sum(IN_SPLITS)

kernel_name = "hgrn2_swa_sink_gated_hybrid_step"


def _rmsnorm(x, w):
    xf = x.astype(jnp.float32)
    y = xf * lax.rsqrt(jnp.mean(xf * xf, axis=-1, keepdims=True) + EPS)
    return (y * w.astype(jnp.float32)).astype(x.dtype)


def _rotary(x, pos):
    half = ROT_DIM // 2
    inv = jnp.exp(-math.log(ROPE_THETA) * jnp.arange(half, dtype=jnp.float32) * (2.0 / ROT_DIM))
    ang = pos.astype(jnp.float32)[:, None] * inv[None, :]
    cos = jnp.cos(ang)[:, None, :].astype(x.dtype)
    sin = jnp.sin(ang)[:, None, :].astype(x.dtype)
    x1 = x[..., :half]
    x2 = x[..., half:ROT_DIM]
    return jnp.concatenate([x1 * cos - x2 * sin, x2 * cos + x1 * sin, x[..., ROT_DIM:]], axis=-1)


def _hgrn_recurrence(q, logf, k, v, s0, chunk):
    B, H, T, _ = q.shape
    n = T // chunk
    def blocks(t):
        return t.reshape(B, H, n, chunk, t.shape[-1]).transpose(2, 0, 1, 3, 4)
    causal = jnp.tril(jnp.ones((chunk, chunk), dtype=bool))[:, :, None]
    def step(S, inp):
        qc, lc, kc, vc = inp
        b = jnp.cumsum(lc, axis=-2)
        o_inter = jnp.einsum('bhcd,bhde->bhce', qc * jnp.exp(b), S)
        diff = b[..., :, None, :] - b[..., None, :, :]
        decay = jnp.exp(jnp.where(causal, diff, -jnp.inf))
        attn = jnp.einsum('bhtd,bhtsd,bhsd->bhts', qc, decay, kc)
        o_intra = jnp.einsum('bhts,bhse->bhte', attn, vc)
        bl = b[..., -1:, :]
        S_new = jnp.exp(bl[..., 0, :])[..., None] * S + jnp.einsum('bhsd,bhse->bhde', kc * jnp.exp(bl - b), vc)
        return S_new, o_inter + o_intra
    S, o = lax.scan(step, s0.astype(jnp.float32), (blocks(q), blocks(logf), blocks(k), blocks(v)))
    o = o.transpose(1, 2, 0, 3, 4).reshape(B, H, T, v.shape[-1])
    return o, S


def _sink_attention(q, k, v, mask, sinks):
    s = jnp.einsum('...qhgd,...khd->...hgqk', q.astype(jnp.float32), k.astype(jnp.float32)) * (HEAD_DIM ** -0.5)
    s = jnp.where(mask, s, NEG_INF)
    sink = jnp.broadcast_to(sinks.astype(jnp.float32).reshape(N_KV, GROUP, 1, 1), s.shape[:-1] + (1,))
    p = jax.nn.softmax(jnp.concatenate([s, sink], axis=-1), axis=-1)[..., :-1]
    return jnp.einsum('...hgqk,...khd->...qhgd', p.astype(v.dtype), v)


def _swa_prompt(q, k, v, sinks):
    B, T = q.shape[:2]
    nb = T // WINDOW
    qb = q.reshape(B, nb, WINDOW, N_KV, GROUP, HEAD_DIM)
    kb = k.reshape(B, nb, WINDOW, N_KV, HEAD_DIM)
    vb = v.reshape(B, nb, WINDOW, N_KV, HEAD_DIM)
    padw = ((0, 0), (1, 0), (0, 0), (0, 0), (0, 0))
    k2 = jnp.concatenate([jnp.pad(kb, padw)[:, :-1], kb], axis=2)
    v2 = jnp.concatenate([jnp.pad(vb, padw)[:, :-1], vb], axis=2)
    i = jnp.arange(WINDOW)[:, None]
    j = jnp.arange(2 * WINDOW)[None, :]
    rel = i + WINDOW - j
    kpos = (jnp.arange(nb)[:, None, None] - 1) * WINDOW + j[None]
    mask = (rel >= 0)[None] & (rel < WINDOW)[None] & (kpos >= 0)
    o = _sink_attention(qb, k2, v2, mask[:, None, None], sinks)
    return o.reshape(B, T, SWA_WIDTH)


def _swa_sample(q, k, v, ck, cv, sinks):
    Bd, Tn = q.shape[:2]
    wb = ck.shape[1]
    kall = jnp.concatenate([ck, k], axis=1)
    vall = jnp.concatenate([cv, v], axis=1)
    qpos = PAST_LEN + jnp.arange(Tn)
    kpos = PAST_LEN - wb + jnp.arange(wb + Tn)
    rel = qpos[:, None] - kpos[None, :]
    mask = (rel >= 0) & (rel < WINDOW)
    o = _sink_attention(q.reshape(Bd, Tn, N_KV, GROUP, HEAD_DIM), kall, vall, mask, sinks)
    return o.reshape(Bd, Tn, SWA_WIDTH), kall[:, -wb:], vall[:, -wb:]


def _layer(x, pos, ck, cv, s0, w_in, lb, hg_norm_w, sinks, w_up_a, w_up_b, w_o, n1, n2, w_ff1, w_ff2):
    B, T, _ = x.shape
    h = _rmsnorm(x, n1)
    z = h @ w_in
    hq, hf, hi, hg, sq, sk, sv, ga, gb = jnp.split(z, np.cumsum(IN_SPLITS)[:-1], axis=-1)
    f = lb + (1.0 - lb) * jax.nn.sigmoid(hf.astype(jnp.float32))
    def heads(t, d):
        return t.reshape(B, T, HG_HEADS, d).transpose(0, 2, 1, 3)
    o_hg, s_new = _hgrn_recurrence(heads(hq.astype(jnp.float32), HG_DK), heads(jnp.log(f), HG_DK),
                                   heads(1.0 - f, HG_DK), heads(hi.astype(jnp.float32), HG_DV),
                                   s0, math.gcd(T, HG_CHUNK))
    o_hg = _rmsnorm(o_hg.transpose(0, 2, 1, 3), hg_norm_w)
    y_a = (o_hg.reshape(B, T, HG_WIDTH).astype(x.dtype) * jax.nn.silu(hg)) @ w_up_a
    q = _rotary(sq.reshape(B, T, N_Q, HEAD_DIM), pos)
    k = _rotary(sk.reshape(B, T, N_KV, HEAD_DIM), pos)
    v = sv.reshape(B, T, N_KV, HEAD_DIM)
    if ck is None:
        o_b = _swa_prompt(q, k, v, sinks)
        nk, nv = k[:, -WINDOW:], v[:, -WINDOW:]
    else:
        o_b, nk, nv = _swa_sample(q, k, v, ck, cv, sinks)
    y_b = o_b @ w_up_b
    m = jax.nn.sigmoid(ga) * y_a + jax.nn.sigmoid(gb) * y_b
    x = x + m @ w_o
    h2 = _rmsnorm(x, n2)
    x = x + jnp.square(jax.nn.relu(h2 @ w_ff1)) @ w_ff2
    return x, nk, nv, s_new.astype(x.dtype)


def setup_inputs(seed: int = 0) -> dict:
    key = jax.random.key(seed)
    ks = jax.random.split(key, 18)
    f32 = jnp.float32
    wb = min(WINDOW, PAST_LEN)
    nrm = lambda k, s, sc: jax.random.normal(k, s, f32) * sc
    return {
        "x_prompt": nrm(ks[0], (BATCH, SEQ, D_MODEL), 1.0),
        "x_sample": nrm(ks[1], (DEC_BATCH, DEC_SEQ, D_MODEL), 1.0),
        "cache_swa_k": nrm(ks[2], (DEPTH, DEC_BATCH, wb, N_KV, HEAD_DIM), 1.0),
        "cache_swa_v": nrm(ks[3], (DEPTH, DEC_BATCH, wb, N_KV, HEAD_DIM), 1.0),
        "state_hgrn": nrm(ks[4], (DEPTH, DEC_BATCH, HG_HEADS, HG_DK, HG_DV), 0.3),
        "w_in": nrm(ks[5], (DEPTH, D_MODEL, D_IN), D_MODEL ** -0.5),
        "hgrn_lb_logits": nrm(ks[6], (DEPTH + 1, HG_FWIDTH), 0.5),
        "hgrn_norm_w": 1.0 + nrm(ks[7], (DEPTH, HG_HEADS, HG_DV), 0.02),
        "sinks": nrm(ks[8], (DEPTH, N_Q), 1.0),
        "w_up_a": nrm(ks[9], (DEPTH, HG_WIDTH, D_MODEL), HG_WIDTH ** -0.5),
        "w_up_b": nrm(ks[10], (DEPTH, SWA_WIDTH, D_MODEL), SWA_WIDTH ** -0.5),
        "w_o": nrm(ks[11], (DEPTH, D_MODEL, D_MODEL), D_MODEL ** -0.5),
        "norm1_w": 1.0 + nrm(ks[12], (DEPTH, D_MODEL), 0.02),
        "norm2_w": 1.0 + nrm(ks[13], (DEPTH, D_MODEL), 0.02),
        "w_ff1": nrm(ks[14], (DEPTH, D_MODEL, D_FF), D_MODEL ** -0.5),
        "w_ff2": nrm(ks[15], (DEPTH, D_FF, D_MODEL), D_FF ** -0.5),
        "normf_w": 1.0 + nrm(ks[16], (D_MODEL,), 0.02),
    }


def reference(x_prompt, x_sample, cache_swa_k, cache_swa_v, state_hgrn, w_in, hgrn_lb_logits, hgrn_norm_w,
              sinks, w_up_a, w_up_b, w_o, norm1_w, norm2_w, w_ff1, w_ff2, normf_w):
    lb_all = jnp.cumsum(jax.nn.softmax(hgrn_lb_logits.astype(jnp.float32), axis=0), axis=0)
    bp, tp = x_prompt.shape[:2]
    pos_p = jnp.arange(tp)
    pos_s = PAST_LEN + jnp.arange(x_sample.shape[1])
    xp, xs = x_prompt, x_sample
    kp_l, vp_l, sp_l, ks_l, vs_l, ss_l = [], [], [], [], [], []
    for l in range(DEPTH):
        wl = (w_in[l], lb_all[l], hgrn_norm_w[l], sinks[l], w_up_a[l], w_up_b[l], w_o[l],
              norm1_w[l], norm2_w[l], w_ff1[l], w_ff2[l])
        s0 = jnp.zeros((bp, HG_HEADS, HG_DK, HG_DV), jnp.float32)
        xp, kp, vp, sp = _layer(xp, pos_p, None, None, s0, *wl)
        xs, ksn, vsn, ssn = _layer(xs, pos_s, cache_swa_k[l], cache_swa_v[l], state_hgrn[l], *wl)
        kp_l.append(kp); vp_l.append(vp); sp_l.append(sp)
        ks_l.append(ksn); vs_l.append(vsn); ss_l.append(ssn)
    y_prompt = _rmsnorm(xp, normf_w)
    y_sample = _rmsnorm(xs, normf_w)
    return (y_prompt, y_sample, jnp.stack(kp_l), jnp.stack(vp_l), jnp.stack(sp_l),
            jnp.stack(ks_l), jnp.stack(vs_l), jnp.stack(ss_l))
```

```python
import math
import numpy as np
from contextlib import ExitStack
import concourse.bass as bass
import concourse.mybir as mybir
from concourse.bass_utils import run_bass_kernel_spmd

F32 = mybir.dt.float32
BF16 = mybir.dt.bfloat16
AF = mybir.ActivationFunctionType
ALU = mybir.AluOpType

ENGS = ("pe", "act", "dve", "pool", "sp")
LABELS = None

D = 1024
NPRE = 2048
NMAIN = 2048
NSAMP = 64
NX = NPRE + NMAIN + NSAMP
G = 256
NCS = 128 + NMAIN + NSAMP
NSEQ = 16
EPS = 1e-6
NEG = -30000.0
PAST_LEN = 16384


class Tok:
    __slots__ = ("w", "r", "name", "excl")

    def __init__(self, name="", excl=False):
        self.w = None
        self.r = []
        self.name = name
        self.excl = excl


class Op:
    __slots__ = ("eng", "fn", "deps", "needs", "cnt", "key", "is_dma", "waits")

    def __init__(self, eng, fn, is_dma, key):
        self.eng = eng
        self.fn = fn
        self.is_dma = is_dma
        self.key = key
        self.deps = []
        self.needs = False
        self.cnt = None
        self.waits = None


class Sched:
    def __init__(self, nc, same_engine_sync=("act", "pool", "dve")):
        self.nc = nc
        self.ops = {e: [] for e in ENGS}
        self.dma_cnt = {}
        self.same = set(same_engine_sync)
        self.skip_same_waw = True
        self.final_waits = []
        self.last_dma = {}

    def barrier(self):
        lasts = []
        for e in ENGS:
            comp = [o for o in self.ops[e] if (not o.is_dma) and o.fn is not None]
            if comp:
                lasts.append(comp[-1])
        lasts += list(self.last_dma.values())
        for e in ENGS:
            op = Op(e, None, False, None)
            op.deps = [d for d in lasts]
            for d in lasts:
                d.needs = True
            self.ops[e].append(op)

    def add(self, eng, fn, reads=(), writes=(), dma_key=None):
        op = Op(eng, fn, dma_key is not None, dma_key)
        if LABELS is not None:
            import sys as _s
            f = _s._getframe(2)
            lab = []
            for _ in range(3):
                if f is None:
                    break
                lab.append("%s:%d" % (f.f_code.co_name, f.f_lineno))
                f = f.f_back
            LABELS.setdefault(eng, []).append("/".join(lab))
        writes = list(writes) + [t for t in reads if t.excl]
        reads = [t for t in reads if not t.excl]
        deps = {}
        for t in reads:
            if t.w is not None:
                deps[id(t.w)] = t.w
        for t in writes:
            if t.w is not None:
                if not (self.skip_same_waw and eng in ("act", "dve") and (not t.w.is_dma) and t.w.eng == eng
                        and dma_key is None and not t.excl):
                    deps[id(t.w)] = t.w
            for r in t.r:
                deps[id(r)] = r
        dl = []
        for d in deps.values():
            if d is op:
                continue
            if (not d.is_dma) and d.eng == eng and eng not in self.same:
                continue
            dl.append(d)
            d.needs = True
        op.deps = dl
        for t in reads:
            t.r.append(op)
        for t in writes:
            t.w = op
            t.r = []
        if op.is_dma:
            c = self.dma_cnt.get(dma_key, 0) + 16
            self.dma_cnt[dma_key] = c
            op.cnt = c
            self.last_dma[dma_key] = op
        self.ops[eng].append(op)
        return op

    def finalize_and_emit(self, es):
        nc = self.nc
        esem = {e: es.enter_context(nc.semaphore("c_" + e)) for e in ENGS}
        dsem = {k: es.enter_context(nc.semaphore("d_%d" % i)) for i, k in enumerate(self.dma_cnt)}
        for e in ENGS:
            c = 0
            for op in self.ops[e]:
                if not op.is_dma and op.needs and op.fn is not None:
                    c += 1
                    op.cnt = c
        for e in ENGS:
            waited = {}
            for op in self.ops[e]:
                w = {}
                for d in op.deps:
                    key = ("d", d.key) if d.is_dma else ("e", d.eng)
                    if d.cnt > w.get(key, 0):
                        w[key] = d.cnt
                ws = []
                for key, v in w.items():
                    if waited.get(key, 0) >= v:
                        continue
                    waited[key] = v
                    ws.append((dsem[key[1]] if key[0] == "d" else esem[key[1]], v))
                op.waits = ws
        final = [(dsem[k], self.dma_cnt[k]) for k in self.final_waits]
        block = es.enter_context(nc.Block())

        def emit(e, eng):
            for op in self.ops[e]:
                for (sm, v) in op.waits:
                    eng.wait_ge(sm, v)
                if op.fn is None:
                    continue
                ins = op.fn(eng)
                if op.is_dma:
                    ins.then_inc(dsem[op.key], 16)
                elif op.needs:
                    ins.then_inc(esem[e], 1)
            if e == "sp":
                for (sm, v) in final:
                    eng.wait_ge(sm, v)

        @block.tensor
        def _(eng):
            emit("pe", eng)

        @block.scalar
        def _(eng):
            emit("act", eng)

        @block.vector
        def _(eng):
            emit("dve", eng)

        @block.gpsimd
        def _(eng):
            emit("pool", eng)

        @block.sync
        def _(eng):
            emit("sp", eng)


class Arena:
    def __init__(self, big, nelem):
        self.big = big
        self.n = nelem
        self.off = 0

    def reset(self):
        self.off = 0

    def alloc(self, shape, dt):
        n = 1
        for s in shape[1:]:
            n *= s
        nb = n * 2 if dt == F32 else n
        nb = (nb + 1) // 2 * 2
        assert self.off + nb <= self.n, ("arena overflow", self.off, nb, self.n)
        v = self.big[0:shape[0], self.off:self.off + nb]
        self.off += nb
        if dt == F32:
            v = v.bitcast(F32)
        v = v[:, 0:n]
        if len(shape) == 3:
            v = v.rearrange("p (a b) -> p a b", a=shape[1])
        elif len(shape) == 4:
            v = v.rearrange("p (a b c) -> p a b c", a=shape[1], b=shape[2])
        return v


class _Stop(Exception):
    pass


ABL = ""


def build_program(stop=None, counts=None, count_only=False):
    nc = bass.Bass("TRN2", target_bir_lowering=False)

    def din(name, shape):
        return nc.dram_tensor(name, list(shape), F32, kind="ExternalInput").ap()

    def dout(name, shape):
        return nc.dram_tensor(name, list(shape), F32, kind="ExternalOutput").ap()

    xT = din("xT", [D, NX])
    cs_d = din("cs", [128, 2, NCS])
    w_in = din("w_in", [D, 4864])
    w_ua = din("w_up_a", [512, D])
    w_ub = din("w_up_b", [512, D])
    w_o = din("w_o", [D, D])
    w_f1 = din("w_ff1", [D, 4096])
    w_f2 = din("w_ff2", [4096, D])
    lbl_d = din("lbl", [128, 8])
    hgnw_d = din("hgnw", [128, 4])
    ncol_d = din("ncols", [128, 24])
    sinks_d = din("sinks", [1, 8])
    mhg_d = din("m_hg", [128, 128])
    mhgs_d = din("m_hg_s", [64, 64])
    negm_d = din("negm", [128, 3, 128])
    negsc_d = din("negm_sc", [128, NSEQ, 64])
    negsn_d = din("negm_sn", [64, 64])
    ind_d = din("ind16", [64, NSEQ])
    ident_d = din("ident", [128, 128])
    perm_d = din("perm", [128, 128])
    ckT_d = din("ckT", [NSEQ, 2, 64, 128])
    ck_d = din("ck", [NSEQ, 128, 128])
    cv_d = din("cv", [NSEQ, 128, 128])
    s0_d = din("s0", [NSEQ, 4, 128, 128])

    NTO = NMAIN + NSAMP
    y_d = dout("y", [NTO, D])
    nk_d = dout("nk", [128, 128])
    nv_d = dout("nv", [128, 128])
    ns_d = dout("ns", [4, 128, 128])
    nks_d = dout("nks", [NSEQ, 128, 128])
    nvs_d = dout("nvs", [NSEQ, 128, 128])
    nss_d = dout("nss", [NSEQ, 4, 128, 128])
    x1s = nc.dram_tensor("x1s", [D, NTO], F32, kind="Internal").ap()
    ogs_d = nc.dram_tensor("ogs", [512, NTO], BF16, kind="Internal").ap()
    obs_d = nc.dram_tensor("obs", [512, NTO], BF16, kind="Internal").ap()

    es = ExitStack()
    with es:
        S = Sched(nc)

        def sb(name, shape, dt=F32):
            return es.enter_context(nc.sbuf_tensor("s_" + name, list(shape), dt))

        def MM(out, lhsT, rhs, st, sp, R, W, skip=False):
            S.add("pe", lambda e: e.matmul(out, lhsT=lhsT, rhs=rhs, start=st, stop=sp, skip_group_check=skip), R, W)

        def TR(out, in_, idn, R, W):
            S.add("pe", lambda e: e.transpose(out, in_, idn), R, W)

        def ACT(out, in_, func, R, W, scale=1.0, bias=0.0):
            S.add("act", lambda e: e.activation(out=out, in_=in_, func=func, bias=bias, scale=scale), R, W)

        def TS(eng, out, in0, s1, s2, op0, op1, R, W):
            S.add(eng, lambda e: e.tensor_scalar(out=out, in0=in0, scalar1=s1, scalar2=s2, op0=op0, op1=op1), R, W)

        def TT(eng, out, in0, in1, op, R, W):
            S.add(eng, lambda e: e.tensor_tensor(out=out, in0=in0, in1=in1, op=op), R, W)

        def STT(out, in0, scalar, in1, op0, op1, R, W):
            S.add("dve", lambda e: e.scalar_tensor_tensor(out=out, in0=in0, scalar=scalar, in1=in1, op0=op0, op1=op1), R, W)

        def CP(eng, out, in_, R, W):
            if eng == "act":
                ACT(out, in_, AF.Copy, R, W)
            else:
                S.add(eng, lambda e: e.tensor_copy(out=out, in_=in_), R, W)

        def DMA(eng, out, in_, R, W, key):
            S.add(eng, lambda e: e.dma_start(out=out, in_=in_), R, W, dma_key=key)

        def MEMSET(eng, ap, val, W):
            S.add(eng, lambda e: e.memset(ap, val), (), W)

        def finish():
            if count_only:
                return
            S.final_waits = [k for k in S.dma_cnt if k.startswith("o_")]
            print("ops", {e: len(S.ops[e]) for e in ENGS}, "dma keys", len(S.dma_cnt))
            S.finalize_and_emit(es)

        banks = [es.enter_context(nc.psum_tensor("psb%d" % i, [128, 512], F32)) for i in range(8)]
        wide = [(banks[i], Tok("pw%d" % i, True)) for i in range(3)]
        plong = (banks[3], Tok("plong", True))
        nt_ = [Tok("pn%d" % i, True) for i in range(4, 8)]
        narrow = [(banks[i][:, 0:256], nt_[i - 4]) for i in range(4, 8)] + [(banks[i][:, 256:512], nt_[i - 4]) for i in range(4, 8)]
        narrow_all = list(narrow)
        ctr = {"w": 0, "n": 0}

        def PW():
            b, t = wide[ctr["w"] % len(wide)]
            ctr["w"] += 1
            return b, t

        def PN():
            a, t = narrow[ctr["n"] % len(narrow)]
            ctr["n"] += 1
            return a, t

        id32 = sb("id32", [128, 128]); t_id32 = Tok()
        id16 = sb("id16", [128, 128], BF16); t_id16 = Tok()
        ones16 = sb("ones16", [128, 128], BF16); t_ones = Tok()
        ones16w = sb("ones16w", [128, 256], BF16)
        perm16 = sb("perm16", [128, 128], BF16); t_perm = Tok()
        lbl = sb("lbl", [128, 8]); lb = sb("lb", [128, 4]); omlb = sb("omlb", [128, 4]); t_lb = Tok()
        hgnw = sb("hgnw", [128, 4]); t_hgnw = Tok()
        ncol = sb("ncol", [128, 24]); t_ncol = Tok()
        esink = sb("esink", [128, 8]); t_esink = Tok()
        mhg = sb("mhg", [128, 4, 128], BF16); t_mhg = Tok()
        mhgs = sb("mhgs", [64, 4, 64], BF16); t_mhgs = Tok()
        negm = sb("negm", [128, 3, 4, 128], BF16); t_negm = Tok()
        negsc = sb("negsc", [128, NSEQ, 64], BF16); t_negsc = Tok()
        negsn = sb("negsn", [64, 4, 64], BF16); t_negsn = Tok()
        ind = sb("ind", [64, NSEQ]); t_ind = Tok()
        zeros = sb("zeros", [128, 64]); t_zeros = Tok()
        rmask = sb("rmask", [128, G]); t_rmask = Tok()
        epsc = sb("epsc", [128, 1]); t_epsc = Tok()

        DMA("sp", id32[:], ident_d, [], [t_id32], "id32")
        DMA("pool", id16[:], ident_d, [], [t_id16], "id16")
        DMA("pool", perm16[:], perm_d, [], [t_perm], "perm16")
        MEMSET("pool", ones16[:], 1.0, [t_ones])
        MEMSET("pool", ones16w[:], 1.0, [t_ones])
        MEMSET("pool", zeros[:], 0.0, [t_zeros])
        MEMSET("pool", rmask[:], 1.0, [t_rmask])
        MEMSET("pool", rmask[:, :].rearrange("p (c t) -> p c t", t=64)[:, :, 0:1], 0.0, [t_rmask])
        MEMSET("pool", epsc[:], EPS, [t_epsc])
        DMA("sp", lbl[:], lbl_d, [], [t_lb], "lbl")
        DMA("sp", hgnw[:], hgnw_d, [], [t_hgnw], "hgnw")
        DMA("sp", ncol[:], ncol_d, [], [t_ncol], "ncol")
        DMA("sp", esink[:], sinks_d.partition_broadcast(128), [], [t_esink], "esink")
        DMA("sp", ind[:], ind_d, [], [t_ind], "ind")
        TT("dve", lb[:], lbl[:, 0:4], lbl[:, 4:8], ALU.subtract, [t_lb], [t_lb])
        ACT(lb[:], lb[:], AF.Sigmoid, [t_lb], [t_lb])
        TS("dve", omlb[:], lb[:], -1.0, 1.0, ALU.mult, ALU.add, [t_lb], [t_lb])
        ACT(esink[:], esink[:], AF.Exp, [t_esink], [t_esink])

        if stop == "consts":
            finish()
            return nc
        BIGN = 98304
        big = sb("big", [128, BIGN], BF16)
        A = Arena(big, BIGN)
        rr = {}

        def slots(name, shape, dt, n):
            return [(A.alloc(shape, dt), Tok("%s%d" % (name, i))) for i in range(n)]

        def nxt(pool, name):
            i = rr.get(name, 0)
            rr[name] = i + 1
            return pool[i % len(pool)]

        def rstd_from(ps_ap, ps_tok, out_ap, out_tok, scale, n):
            ACT(out_ap, ps_ap, AF.Ln, [ps_tok, t_epsc], [out_tok], scale=scale, bias=epsc[:, 0:1])
            ACT(out_ap, out_ap, AF.Exp, [out_tok], [out_tok], scale=-0.5)

        w_in_v = w_in.rearrange("(k p) c -> p k c", p=128)

        NFM = 24 * 128
        wfm = A.alloc([128, 8, NFM], BF16)
        wtm = A.alloc([128, 8, 640], BF16)
        t_w = {n: Tok("w_" + n) for n in ("hf", "hi", "sv", "k", "hq", "hg", "sq", "rot")}
        WALL = list(t_w.values())

        def wload(dst, src, name):
            DMA("pool", dst, src, [], [t_w[name]], "w_" + name)

        wload(wfm[:, :, 4 * 128:8 * 128], w_in_v[:, :, 512:1024], "hf")
        wload(wtm[:, :, 0:512], w_in_v[:, :, 1024:1536], "hi")
        wload(wtm[:, :, 512:640], w_in_v[:, :, 2688:2816], "sv")
        kd = wfm[:, :, 16 * 128:18 * 128].rearrange("p k (j h c) -> p k j h c", j=2, h=2)
        ksrc = w_in_v[:, :, 2560:2688].rearrange("p k (j c) -> p k j c", j=2)
        for h in range(2):
            for j in range(2):
                wload(kd[:, :, j, h, :], ksrc[:, :, j, :], "k")
        for g in range(4):
            DMA("pool", mhg[:, g, :], mhg_d, [], [t_mhg], "mhg")
            DMA("pool", mhgs[:, g, :], mhgs_d, [], [t_mhgs], "mhgs")
            DMA("pool", negm[:, :, g, :], negm_d, [], [t_negm], "negm")
            DMA("pool", negsn[:, g, :], negsn_d, [], [t_negsn], "negsn")
        DMA("pool", negsc[:], negsc_d, [], [t_negsc], "negsc")
        wload(wfm[:, :, 0:4 * 128], w_in_v[:, :, 0:512], "hq")
        wload(wfm[:, :, 8 * 128:12 * 128], w_in_v[:, :, 1536:2048], "hg")
        wload(wfm[:, :, 12 * 128:16 * 128], w_in_v[:, :, 2048:2560], "sq")
        def emit_rot_copies():
            for (src0, dst0, nh, nm) in ((12 * 128, 18 * 128, 8, "sq"), (16 * 128, 22 * 128, 4, "k")):
                sv_ = wfm[:, :, src0:src0 + nh * 64].rearrange("p k (h c) -> p k h c", c=64)
                dv_ = wfm[:, :, dst0:dst0 + nh * 64].rearrange("p k (h c) -> p k h c", c=64)
                for k in range(8):
                    CP("pool", dv_[:, k, :, 0:8], sv_[:, k, :, 8:16], [t_w[nm]], [t_w["rot"]])
                    CP("pool", dv_[:, k, :, 8:16], sv_[:, k, :, 0:8], [t_w[nm]], [t_w["rot"]])
                    CP("pool", dv_[:, k, :, 16:64], sv_[:, k, :, 16:64], [t_w[nm]], [t_w["rot"]])

        if stop == "wA1":
            finish()
            return nc
        NKC = 128 + NMAIN + NSAMP
        k2t = A.alloc([128, 2, NKC], BF16)
        t_k2t = [Tok("k2t%d" % i) for i in range(18)]
        vaug = A.alloc([128, 18, 2, 65], BF16)
        t_vaug = [Tok("vaug%d" % i) for i in range(18)]
        MEMSET("pool", vaug[:], 1.0, t_vaug)
        S32 = A.alloc([128, 4, 128], F32); t_S32 = [Tok() for _ in range(4)]
        Sbf2 = [A.alloc([128, 4, 128], BF16) for _ in range(2)]
        t_Sbf2 = [[Tok() for _ in range(4)] for _ in range(2)]
        MEMSET("pool", S32[:], 0.0, t_S32)
        for p_ in range(2):
            MEMSET("pool", Sbf2[p_][:], 0.0, t_Sbf2[p_])
        chunk_ctr = {"n": 0}

        xs = slots("xs", [128, 8, G], F32, 1)
        xsq = slots("xsq", [128, G], BF16, 2)
        rstd = slots("rstd", [128, G], F32, 2)
        hTs = slots("hT", [128, 8, G], BF16, 2)
        fbs = slots("fb", [128, G], F32, 4)
        kbs = slots("kb", [128, G], F32, 4)
        Pbs = slots("Pb", [128, 4, G], F32, 2)
        t_P_all = [[Tok("P%d_%d" % (s_, i)) for i in range(4)] for s_ in range(2)]
        t_ke_all = [[Tok("ke%d_%d" % (s_, i)) for i in range(4)] for s_ in range(2)]
        t_den_all = [[Tok("den%d_%d" % (s_, i)) for i in range(2)] for s_ in range(2)]
        rPs = slots("rP", [128, G], F32, 2)
        bbs = slots("bb", [128, G], F32, 2)
        qds = slots("qd", [128, 4, G], BF16, 2)
        kes = slots("ke", [128, 4, G], BF16, 2)
        ke2s = slots("ke2", [128, 4, G], BF16, 2)
        sgs = slots("sg", [128, 4, G], BF16, 2)
        sgts = slots("sgt", [128, G], BF16, 1)
        vts = slots("vt", [128, 512], BF16, 2)
        v32s = slots("v32", [128, 128], F32, 2)
        kets = slots("ket", [128, 4, 128], BF16, 2)
        ams = slots("am", [128, 4, 128], BF16, 1)
        o32s = slots("o32", [128, 4, 128], F32, 2)
        osqs = slots("osq", [128, 4, 128], BF16, 1)
        rso = slots("rso", [128, 512], F32, 1)
        ogs = slots("og", [128, 4, G], BF16, 2)
        css = slots("cs", [128, 2, G], F32, 2)
        zbs = slots("zb", [128, G], BF16, 2)
        rt1 = slots("rt1", [128, G], F32, 2)
        rt2 = slots("rt2", [128, G], F32, 2)
        kr32 = slots("kr32", [128, 2, G], F32, 1)
        qrs = slots("qr", [128, 4, 2, G], BF16, 2)
        for q_ in qrs:
            MEMSET("pool", q_[0][:], 0.0, [q_[1]])
        pTs = slots("pT", [128, 512], BF16, 3)
        dens = slots("den", [128, 16], F32, 2)
        obt = slots("obt", [128, 512], BF16, 2)
        t_ob_all = [[Tok("ob%d_%d" % (s_, i)) for i in range(8)] for s_ in range(2)]
        obTs = slots("obT", [128, 4, G], BF16, 2)
        st_k = slots("stk", [128, 128], F32, 1)
        s0s = slots("s0f", [128, 4, 128], F32, 3)
        qd32 = (A.alloc([128, 4, NSAMP], F32), Tok("qd32"))
        sns = slots("sn", [128, 4, 128], F32, 2)
        kms = slots("km", [64, 4, 128], BF16, 2)
        kcs = slots("kc", [128, 2, 128], BF16, 3)
        vcs = slots("vc", [128, 2, 65], BF16, 3)
        for v_ in vcs:
            MEMSET("pool", v_[0][:], 1.0, [v_[1]])
        print("A1 arena bytes", A.off * 2)

        def load_x(src, col0, nt, R):
            xt, t_x = nxt(xs, "xs")
            DMA("sp", xt[:, :, 0:nt], src[:, col0:col0 + nt].rearrange("(k p) n -> p k n", p=128), R, [t_x], "xs%d" % (rr["xs"] % len(xs)))
            return xt, t_x

        def norm_h(xt, t_x, nt, ncolbase):
            pb, t_pb = PN()
            for k in range(8):
                q, t_q = nxt(xsq, "xsq")
                ACT(q[:, 0:nt], xt[:, k, 0:nt], AF.Square, [t_x], [t_q])
                MM(pb[:, 0:nt], ones16[:], q[:, 0:nt], k == 0, k == 7, [t_ones, t_q], [t_pb])
            r, t_r = nxt(rstd, "rstd")
            rstd_from(pb[:, 0:nt], t_pb, r[:, 0:nt], t_r, 1.0 / D, nt)
            h, t_h = nxt(hTs, "hT")
            for k in range(8):
                STT(h[:, k, 0:nt], xt[:, k, 0:nt], ncol[:, ncolbase + k:ncolbase + k + 1], r[:, 0:nt], ALU.mult, ALU.mult,
                    [t_x, t_ncol, t_r], [t_h])
            return h, t_h

        def norm_h_gen(xt, t_x, nt, ncolbase, res):
            pb, t_pb = plong
            for k in range(8):
                q, t_q = nxt(xsq, "xsq")
                ACT(q[:, 0:nt], xt[:, k, 0:nt], AF.Square, [t_x], [t_q])
                MM(pb[:, 0:nt], ones16[:], q[:, 0:nt], k == 0, k == 7, [t_ones, t_q], [t_pb])
                if k % 2 == 1:
                    yield
            r, t_r = nxt(rstd, "rstd")
            rstd_from(pb[:, 0:nt], t_pb, r[:, 0:nt], t_r, 1.0 / D, nt)
            yield
            h, t_h = nxt(hTs, "hT")
            for k in range(8):
                STT(h[:, k, 0:nt], xt[:, k, 0:nt], ncol[:, ncolbase + k:ncolbase + k + 1], r[:, 0:nt], ALU.mult, ALU.mult,
                    [t_x, t_ncol, t_r], [t_h])
                if k % 4 == 3:
                    yield
            res["h"] = h
            res["t_h"] = t_h

        NWARM = 0

        def proj_fm(h, t_h, chunk, nt, wt):
            pb, t_pb = PN()
            for _ in range(NWARM):
                MM(pb[:, 0:256], ones16[:], ones16w[:, 0:256], True, True, [t_ones], [t_pb])
            for k in range(8):
                MM(pb[:, 0:nt], wfm[:, k, chunk * 128:(chunk + 1) * 128], h[:, k, 0:nt], k == 0, k == 7, wt + [t_h], [t_pb])
            return pb, t_pb

        def gates_a(h, t_h, nt, hd):
            pb, t_pb = proj_fm(h, t_h, 4 + hd, nt, [t_w["hf"]])
            f, t_f = nxt(fbs, "fb")
            ACT(f[:, 0:nt], pb[:, 0:nt], AF.Sigmoid, [t_pb], [t_f])
            ACT(f[:, 0:nt], f[:, 0:nt], AF.Identity, [t_f, t_lb], [t_f], scale=omlb[:, hd:hd + 1], bias=lb[:, hd:hd + 1])
            kk, t_kk = nxt(kbs, "kb")
            TS("pool", kk[:, 0:nt], f[:, 0:nt], -1.0, 1.0, ALU.mult, ALU.add, [t_f], [t_kk])
            return f, t_f, kk, t_kk

        def gates_b(fk, nt, samp, hd, Pt, t_P, ke, t_ke, ke2, t_ke2):
            f, t_f, kk, t_kk = fk
            rp, t_rp = nxt(rPs, "rP")
            bb, t_bb = nxt(bbs, "bb")
            ACT(rp[:, 0:nt], f[:, 0:nt], AF.Ln, [t_f], [t_rp])
            if not samp:
                S.add("dve", lambda e, rp=rp, bb=bb: e.tensor_tensor_scan(
                    out=bb[:, 0:nt], data0=rmask[:, 0:nt], data1=rp[:, 0:nt],
                    initial=0.0, op0=ALU.mult, op1=ALU.add), [t_rp, t_rmask], [t_bb])
            else:
                lv = rp[:, 0:nt].rearrange("p (s t) -> p s t", t=4)
                bv = bb[:, 0:nt].rearrange("p (s t) -> p s t", t=4)
                CP("dve", bv[:, :, 0:1], lv[:, :, 0:1], [t_rp], [t_bb])
                for j in range(1, 4):
                    TT("dve", bv[:, :, j:j + 1], bv[:, :, j - 1:j], lv[:, :, j:j + 1], ALU.add, [t_rp, t_bb], [t_bb])
            ACT(Pt[:, hd, 0:nt], bb[:, 0:nt], AF.Exp, [t_bb], [t_P[hd]])
            ACT(rp[:, 0:nt], bb[:, 0:nt], AF.Exp, [t_bb], [t_rp], scale=-1.0)
            TT("pool", ke[:, hd, 0:nt], kk[:, 0:nt], rp[:, 0:nt], ALU.mult, [t_kk, t_rp], [t_ke[hd]])
            if not samp:
                ncq = nt // 64
                plb = Pt[:, hd, 0:nt].rearrange("p (c t) -> p c t", t=64)[:, :, 63:64].broadcast_to([128, ncq, 64])
                TT("dve", ke2[:, hd, 0:nt].rearrange("p (c t) -> p c t", t=64), ke[:, hd, 0:nt].rearrange("p (c t) -> p c t", t=64),
                   plb, ALU.mult, [t_ke[hd], t_P[hd]], [t_ke2])

        def proj_tm(h, t_h, tcol, ntk):
            pw, t_pw = PW()
            for k in range(8):
                MM(pw[0:ntk, 0:512], h[:, k, tcol:tcol + ntk], wtm[:, k, 0:512], k == 0, k == 7, [t_w["hi"], t_h], [t_pw])
            vt, t_vt = nxt(vts, "vt")
            CP("dve", vt[0:ntk, 0:512], pw[0:ntk, 0:512], [t_pw], [t_vt])
            return vt, t_vt

        def proj_sv(h, t_h, tcol, ntk, tile_idx, want32):
            pn, t_pn = PN()
            for k in range(8):
                MM(pn[0:ntk, 0:128], h[:, k, tcol:tcol + ntk], wtm[:, k, 512:640], k == 0, k == 7, [t_w["sv"], t_h], [t_pn])
            CP("dve", vaug[0:ntk, tile_idx, :, 0:64], pn[0:ntk, 0:128].rearrange("p (j c) -> p j c", j=2), [t_pn], [t_vaug[tile_idx]])
            v32 = None
            if want32:
                v32t, t_v32 = nxt(v32s, "v32")
                CP("act", v32t[0:ntk, :], pn[0:ntk, 0:128], [t_pn], [t_v32])
                v32 = (v32t, t_v32)
            return v32

        def state_U(ket, t_ket, vt, t_vt):
            pus = []
            for c in range(2):
                pu, t_pu = PW()
                for hd in range(4):
                    MM(pu[:, hd * 128:(hd + 1) * 128], ket[c * 64:(c + 1) * 64, hd, :], vt[c * 64:(c + 1) * 64, hd * 128:(hd + 1) * 128],
                       True, True, [t_ket, t_vt], [t_pu])
                pus.append((pu, t_pu))
            return pus

        def state_chain(c, pus, Pt, t_P, tcol, cast):
            n = chunk_ctr["n"]
            chunk_ctr["n"] = n + 1
            pu, t_pu = pus[c]
            for hd in range(4):
                col = tcol + c * 64 + 63
                STT(S32[:, hd, :], S32[:, hd, :], Pt[:, hd, col:col + 1], pu[:, hd * 128:(hd + 1) * 128], ALU.mult, ALU.add,
                    [t_S32[hd], t_P[hd], t_pu], [t_S32[hd]])
            if cast:
                p_ = (n + 1) % 2
                CP("dve", Sbf2[p_][:, :, :], S32[:, :, :], t_S32, t_Sbf2[p_])

        def ke_transpose(ke, t_ke, tcol, ntk):
            pw, t_pw = PW()
            pwb = pw[:].bitcast(BF16)
            for hd in range(4):
                TR(pwb[0:ntk, hd * 128:(hd + 1) * 128], ke[:, hd, tcol:tcol + ntk], id16[:],
                   [t_ke[hd] if isinstance(t_ke, list) else t_ke, t_id16], [t_pw])
            ket, t_ket = nxt(kets, "ket")
            CP("dve", ket[0:ntk, :, :], pwb[0:ntk, 0:512].rearrange("p (h d) -> p h d", h=4), [t_pw], [t_ket])
            return ket, t_ket

        def rotary_chunk(h, t_h, ch_a, ch_b, wa, cst, t_cs, c0, nt, out_ap, out_toks, hsl=None):
            hv = h if hsl is None else h[:, :, hsl[0]:hsl[1]]
            pa, t_pa = proj_fm(hv, t_h, ch_a, nt, [t_w[wa]])
            zb, t_zb = nxt(zbs, "zb")
            CP("dve", zb[:, 0:nt], pa[:, 0:nt], [t_pa], [t_zb])
            pb, t_pb = PN()
            MM(pb[:, 0:nt], perm16[:], zb[:, 0:nt], True, True, [t_perm, t_zb], [t_pb])
            a, t_a = nxt(rt1, "rt1")
            b, t_b = nxt(rt2, "rt2")
            TT("dve", a[:, 0:nt], pa[:, 0:nt], cst[:, 0, c0:c0 + nt], ALU.mult, [t_pa, t_cs], [t_a])
            TT("dve", b[:, 0:nt], pb[:, 0:nt], cst[:, 1, c0:c0 + nt], ALU.mult, [t_pb, t_cs], [t_b])
            if isinstance(out_ap, tuple):
                TT("pool", out_ap[0], a[0:64, 0:nt], b[0:64, 0:nt], ALU.add, [t_a, t_b], out_toks)
                TT("pool", out_ap[1], a[64:128, 0:nt], b[64:128, 0:nt], ALU.add, [t_a, t_b], out_toks)
            else:
                TT("pool", out_ap, a[:, 0:nt], b[:, 0:nt], ALU.add, [t_a, t_b], out_toks)

        def load_cs(col0, nt):
            cst, t_cs = nxt(css, "cs")
            DMA("sp", cst[:, :, 0:nt], cs_d[:, :, col0:col0 + nt], [], [t_cs], "cs%d" % (rr["cs"] % 2))
            return cst, t_cs

        def k_chunk(h, t_h, cst, t_cs, nt, kcol0, tiles, j, hsl=None):
            kr, t_kr = kr32[0]
            rotary_chunk(h, t_h, 16 + j, 22 + j, "k", cst, t_cs, 0, nt, kr[:, j, 0:nt], [t_kr], hsl)
            CP("pool", k2t[:, j, kcol0:kcol0 + nt], kr[:, j, 0:nt], [t_kr], [t_k2t[t] for t in tiles])
            return kr, t_kr

        def attention(qr, t_qr, qcol, nq, blocks, obT, t_obT, ocol, po_bank=None, po_bank2=None, preloaded=None):
            ob, _ = nxt(obt, "obt")
            t_obh = t_ob_all[(rr["obt"] - 1) % 2]
            den, _ = nxt(dens, "den")
            t_den2 = t_den_all[(rr["den"] - 1) % 2]
            nb = len(blocks)
            pos = [po_bank if po_bank is not None else plong]
            if po_bank2 is not None:
                pos.append(po_bank2)
                items = [(j, bi) for bi in range(nb) for j in range(2)]
            else:
                pos.append(pos[0])
                items = [(j, bi) for j in range(2) for bi in range(nb)]
            loaded = dict(preloaded or {})

            def ensure_loaded(idx):
                if idx < len(items):
                    b0 = items[idx][1]
                    for bi in range(b0, min(nb, b0 + 3)):
                        if bi not in loaded:
                            loaded[bi] = blocks[bi][0]()

            def scores(idx):
                j, bi = items[idx]
                _, mask_ap, nk = blocks[bi]
                kaps, vaps, btoks = loaded[bi]
                ps_, t_ps = PW()
                MM(ps_[0:nk, 0:4 * nq].rearrange("p (g q) -> p g q", g=4), id16[0:nk, 0:nk], mask_ap, True, False,
                   [t_id16, t_negm, t_negsc, t_negsn], [t_ps], skip=True)
                MM(ps_[0:nk, 0:4 * nq].rearrange("p (g q) -> p g q", g=4), kaps[j],
                   qr[:, 2 * j:2 * j + 2, :, qcol:qcol + nq].rearrange("p c h q -> p (c h) q"),
                   False, True, btoks + [t_qr], [t_ps], skip=True)
                pT, t_pT = nxt(pTs, "pT")
                ACT(pT[0:nk, 0:4 * nq], ps_[0:nk, 0:4 * nq], AF.Exp, [t_ps], [t_pT], scale=0.125)
                return pT, t_pT, vaps[j], btoks, nk

            ensure_loaded(0)
            nxt_s = scores(0)
            yield
            for idx, (j, bi) in enumerate(items):
                pT, t_pT, vap, btoks, nk = nxt_s
                if idx + 1 < len(items):
                    nxt_s = scores(idx + 1)
                po, t_po = pos[j]
                pov = po[0:nq, 0:260].rearrange("p (g c) -> p g c", c=65)
                for gq in range(4):
                    MM(po[0:nq, gq * 65:(gq + 1) * 65], pT[0:nk, gq * nq:(gq + 1) * nq], vap, bi == 0 and gq == 0, bi == nb - 1,
                       [t_pT] + btoks, [t_po], skip=True)
                ensure_loaded(idx + 1)
                if bi == nb - 1:
                    TT("dve", den[0:nq, 4 * j:4 * j + 4], pov[:, :, 64], esink[0:nq, 4 * j:4 * j + 4], ALU.add, [t_po, t_esink], [t_den2[j]])
                    S.add("dve", lambda e, j=j, den=den: e.reciprocal(out=den[0:nq, 8 + 4 * j:12 + 4 * j], in_=den[0:nq, 4 * j:4 * j + 4]),
                          [t_den2[j]], [t_den2[j]])
                    TT("dve", ob[0:nq, 4 * j * 64:(4 * j + 4) * 64].rearrange("p (g c) -> p g c", c=64), pov[:, :, 0:64],
                       den[0:nq, 8 + 4 * j:12 + 4 * j].unsqueeze(2).broadcast_to([nq, 4, 64]), ALU.mult,
                       [t_po, t_den2[j]], [t_obh[4 * j + g_] for g_ in range(4)])
                yield
            pw, t_pw = PW()
            pwb = pw[:].bitcast(BF16)
            for c in range(4):
                TR(pwb[:, c * nq:(c + 1) * nq], ob[0:nq, c * 128:(c + 1) * 128], id16[0:nq, 0:nq], [t_obh[2 * c], t_obh[2 * c + 1], t_id16], [t_pw])
            CP("act", obT[:, :, ocol:ocol + nq], pwb[:, 0:4 * nq].rearrange("p (c q) -> p c q", c=4), [t_pw], [t_obT])
            yield

        def hgrn_out(o32, t_o32, osq, t_osq, sg, t_sg, og, t_og, tcol, ntk, pso, t_pso):
            pov = pso[:, 0:4 * ntk].rearrange("p (h t) -> p h t", h=4)
            ACT(o32[:, :, 0:ntk], pov, AF.Copy, [t_pso], [t_o32])
            TT("pool", osq[:, :, 0:ntk], o32[:, :, 0:ntk], o32[:, :, 0:ntk], ALU.mult, [t_o32], [t_osq])
            yield
            pss, t_pss = PW()
            MM(pss[:, 0:4 * ntk].rearrange("p (h t) -> p h t", h=4), ones16[:], osq[:, :, 0:ntk], True, True, [t_ones, t_osq], [t_pss])
            r, t_r = nxt(rso, "rso")
            rstd_from(pss[:, 0:4 * ntk], t_pss, r[:, 0:4 * ntk], t_r, 1.0 / 128, 4 * ntk)
            TT("dve", o32[:, :, 0:ntk], o32[:, :, 0:ntk], r[:, 0:4 * ntk].rearrange("p (h t) -> p h t", h=4), ALU.mult, [t_o32, t_r], [t_o32])
            TT("pool", og[:, :, tcol:tcol + ntk], o32[:, :, 0:ntk], sg[:, :, tcol:tcol + ntk], ALU.mult, [t_o32, t_sg], [t_og])

        def out_kv(kr, t_kr, v32, tcol, ntk, samp):
            v32t, t_v32 = v32
            pw, t_pw = PW()
            for j in range(2):
                TR(pw[0:ntk, j * 128:(j + 1) * 128], kr[:, j, tcol:tcol + ntk], id32[:], [t_kr, t_id32], [t_pw])
            stk, t_stk = st_k[0]
            CP("dve", stk[0:ntk, :].rearrange("p (j c) -> p j c", j=2), pw[0:ntk, 0:256].rearrange("p (j c) -> p j c", j=2)[:, :, 0:64],
               [t_pw], [t_stk])
            if not samp:
                DMA("sp", nk_d, stk[:, :], [t_stk], [], "o_nk")
                DMA("sp", nv_d, v32t[:, :], [t_v32], [], "o_nv")
            else:
                for s in range(NSEQ):
                    DMA("sp", nks_d[s, 124:128, :], stk[4 * s:4 * s + 4, :], [t_stk], [], "o_nk")
                    DMA("sp", nvs_d[s, 124:128, :], v32t[4 * s:4 * s + 4, :], [t_v32], [], "o_nv")

        def sample_states(pso, t_pso, qd, t_qd, ket, t_ket, vt, t_vt, Pt, t_P, deferred=()):
            deferred = list(deferred)
            sfl = dict(sample_pre["sf"])

            def load_s(s_):
                if s_ < NSEQ and s_ not in sfl:
                    sf_, t_sf_ = nxt(s0s, "s0f")
                    DMA("sp", sf_[:], s0_d[s_].rearrange("h d e -> d h e"), [], [t_sf_], "s0f%d" % (rr["s0f"] % 3))
                    sfl[s_] = (sf_, t_sf_)
            load_s(0)
            load_s(1)
            for s in range(NSEQ):
                if deferred and s % 3 == 2:
                    deferred.pop(0)()
                sf, t_sf = sfl.pop(s)
                for hd in range(4):
                    MM(pso[:, hd * 64 + s * 4:hd * 64 + s * 4 + 4], sf[:, hd, :], qd32[0][:, hd, s * 4:s * 4 + 4], False, s == NSEQ - 1,
                       [t_sf, qd32[1]], [t_pso], skip=True)
                km, t_km = nxt(kms, "km")
                TS("dve", km[:, :, :], ket[0:64, :, :], ind[:, s:s + 1], None, ALU.mult, ALU.bypass, [t_ket, t_ind], [t_km])
                pu, t_pu = PW()
                for hd in range(4):
                    MM(pu[:, hd * 128:(hd + 1) * 128], km[:, hd, :], vt[0:64, hd * 128:(hd + 1) * 128], True, True, [t_km, t_vt], [t_pu])
                sn, t_sn = nxt(sns, "sn")
                for hd in range(4):
                    pl = Pt[:, hd, s * 4 + 3:s * 4 + 4]
                    ACT(sn[:, hd, :], pu[:, hd * 128:(hd + 1) * 128], AF.Copy, [t_pu, t_P[hd]], [t_sn], scale=pl)
                    STT(sn[:, hd, :], sf[:, hd, :], pl, sn[:, hd, :], ALU.mult, ALU.add, [t_sf, t_P[hd], t_sn], [t_sn])
                DMA("act", nss_d[s].rearrange("h d e -> d h e"), sn[:], [t_sn], [], "o_nss%d" % (rr["sn"] % 2))
                load_s(s + 2)
                yield
            while deferred:
                deferred.pop(0)()

        sample_pre = {"kv": {}, "sf": {}}

        def sample_preload():
            for s_ in range(3):
                sample_pre["kv"][s_] = cached_loader(s_)()
            for s_ in range(2):
                sf_, t_sf_ = nxt(s0s, "s0f")
                DMA("sp", sf_[:], s0_d[s_].rearrange("h d e -> d h e"), [], [t_sf_], "s0f%d" % (rr["s0f"] % 3))
                sample_pre["sf"][s_] = (sf_, t_sf_)

        def cached_loader(s):
            def load():
                kc, t_kc = nxt(kcs, "kc")
                vc, t_vc = nxt(vcs, "vc")
                key = "kc%d" % (rr["kc"] % 3)
                for hf_ in range(2):
                    DMA("pool", kc[hf_ * 64:(hf_ + 1) * 64, :, :], ckT_d[s].rearrange("j c k -> c j k"), [], [t_kc], key)
                DMA("pool", vc[:, :, 0:64], cv_d[s].rearrange("k (j c) -> k j c", j=2), [], [t_vc], "vc%d" % (rr["vc"] % 3))
                return [kc[:, j_, :] for j_ in range(2)], [vc[:, j_, :] for j_ in range(2)], [t_kc, t_vc]
            return load

        def sample_attention(qr, t_qr, obT, t_obT, po_bank, po_bank2):
            blocks = []

            def cached_loader_unused(s):
                def load():
                    kc, t_kc = nxt(kcs, "kc")
                    vc, t_vc = nxt(vcs, "vc")
                    key = "kc%d" % (rr["kc"] % 3)
                    for hf_ in range(2):
                        DMA("pool", kc[hf_ * 64:(hf_ + 1) * 64, :, :], ckT_d[s].rearrange("j c k -> c j k"), [], [t_kc], key)
                    DMA("pool", vc[:, :, 0:64], cv_d[s].rearrange("k (j c) -> k j c", j=2), [], [t_vc], "vc%d" % (rr["vc"] % 3))
                    return [kc[:, j_, :] for j_ in range(2)], [vc[:, j_, :] for j_ in range(2)], [t_kc, t_vc]
                return load
            for s in range(NSEQ):
                blocks.append((cached_loader(s), negsc[:, s, :].unsqueeze(1).broadcast_to([128, 4, 64]), 128))
                DMA("sp", nks_d[s, 0:124, :], ck_d[s, 4:128, :], [], [], "o_nkc")
                DMA("sp", nvs_d[s, 0:124, :], cv_d[s, 4:128, :], [], [], "o_nkc")
            blocks.append((lambda: ([k2t[:, j_, 128 + NMAIN:128 + NMAIN + 64] for j_ in range(2)], [vaug[0:64, 17, j_, :] for j_ in range(2)],
                                    [t_k2t[17], t_vaug[17]]), negsn[:, :, :], 64))
            yield from attention(qr, t_qr, 0, 64, blocks, obT, t_obT, 0, po_bank, po_bank2, sample_pre["kv"])

        NG = NPRE // G
        NGM = NMAIN // G
        jobs = [("pre", g) for g in range(NG)] + [("main", g) for g in range(NGM)] + [("samp", 0)]

        def job_x(job):
            kind, g = job
            if kind == "pre":
                return (g * G, G)
            if kind == "main":
                return (NPRE + g * G, G)
            return (NPRE + NMAIN, NSAMP)

        def stage1(ji, ctx):
            kind, gi = jobs[ji]
            samp = kind == "samp"
            nt = NSAMP if samp else G
            xt, t_x = xq.pop(0)
            h, t_h = norm_h(xt, t_x, nt, 0)
            if ji + 1 < len(jobs):
                xq.append(load_x(xT, *job_x(jobs[ji + 1]), []))
            ctx.update(h=h, t_h=t_h, nt=nt, samp=samp, kind=kind, gi=gi)
            if samp:
                sample_preload()
            yield
            Pt, _ = nxt(Pbs, "Pb")
            ke, _ = nxt(kes, "ke")
            t_P = t_P_all[(rr["Pb"] - 1) % 2]
            t_ke = t_ke_all[(rr["ke"] - 1) % 2]
            ke2, t_ke2 = nxt(ke2s, "ke2")
            ctx.update(Pt=Pt, t_P=t_P, ke=ke, t_ke=t_ke, ke2=ke2, t_ke2=t_ke2)
            fks = []
            sg = None
            if kind != "pre":
                sg, t_sg = nxt(sgs, "sg")
            for hd in range(4):
                fks.append(gates_a(h, t_h, nt, hd))
            if kind != "pre":
                for hd in range(4):
                    pg, t_pg = proj_fm(h, t_h, 8 + hd, nt, [t_w["hg"]])
                    sgt, t_sgt = nxt(sgts, "sgt")
                    ACT(sgt[:, 0:nt], pg[:, 0:nt], AF.Sigmoid, [t_pg], [t_sgt])
                    STT(sg[:, hd, 0:nt], pg[:, 0:nt], hgnw[:, hd:hd + 1], sgt[:, 0:nt], ALU.mult, ALU.mult, [t_pg, t_hgnw, t_sgt], [t_sg])
            yield
            for hd in range(4):
                gates_b(fks[hd], nt, samp, hd, Pt, t_P, ke, t_ke, ke2, t_ke2)
                yield
            if kind == "pre":
                if gi == NG - 1:
                    cst, t_cs = load_cs(0, 128)
                    for j in range(2):
                        k_chunk(h, t_h, cst, t_cs, 128, 0, [0], j, hsl=(G - 128, G))
                        yield
                    proj_sv(h, t_h, G - 128, 128, 0, False)
                    yield
                return
            cscol0 = 128 + (NMAIN if samp else gi * G)
            cst, t_cs = load_cs(cscol0, nt)
            qd, t_qd = nxt(qds, "qd")
            for hd in range(4):
                pb, t_pb = proj_fm(h, t_h, hd, nt, [t_w["hq"]])
                TT("dve", qd[:, hd, 0:nt], pb[:, 0:nt], Pt[:, hd, 0:nt], ALU.mult, [t_pb, t_P[hd]], [t_qd])
                if samp:
                    TT("dve", qd32[0][:, hd, 0:nt], pb[:, 0:nt], Pt[:, hd, 0:nt], ALU.mult, [t_pb, t_P[hd]], [qd32[1]])
                if hd % 2 == 1:
                    yield
            qr, t_qr = nxt(qrs, "qr")
            for c in range(4):
                rotary_chunk(h, t_h, 12 + c, 18 + c, "sq", cst, t_cs, 0, nt, (qr[0:64, c, 0, 0:nt], qr[64:128, c, 1, 0:nt]), [t_qr])
                yield
            tiles = [17] if samp else [1 + 2 * gi, 2 + 2 * gi]
            for j in range(2):
                kr, t_kr = k_chunk(h, t_h, cst, t_cs, nt, cscol0, tiles, j)
                yield
            v32 = None
            lastg_ = (not samp) and gi == NGM - 1
            for t in range(1 if samp else G // 128):
                ntk = nt if samp else 128
                w32 = samp or (lastg_ and t == G // 128 - 1)
                r_ = proj_sv(h, t_h, t * 128, ntk, tiles[t], w32)
                if r_ is not None:
                    v32 = r_
                yield
            ctx.update(qd=qd, t_qd=t_qd, sg=sg, t_sg=t_sg, qr=qr, t_qr=t_qr, kr=kr, t_kr=t_kr, tiles=tiles, cscol0=cscol0, v32=v32)

        po_att = (banks[7], nt_[3])
        narrow[:] = [e for e in narrow if e[1] is not nt_[3]]

        def stage2h(ctx):
            kind, gi, samp, nt = ctx["kind"], ctx["gi"], ctx["samp"], ctx["nt"]
            h, t_h, Pt, t_P, ke, t_ke = ctx["h"], ctx["t_h"], ctx["Pt"], ctx["t_P"], ctx["ke"], ctx["t_ke"]
            if kind == "pre":
                for t in range(G // 128):
                    halo = (gi == NG - 1) and t == G // 128 - 1
                    vt, t_vt = proj_tm(h, t_h, t * 128, 128)
                    ket, t_ket = ke_transpose(ctx["ke2"], ctx["t_ke2"], t * 128, 128)
                    yield
                    pus = state_U(ket, t_ket, vt, t_vt)
                    for c in range(2):
                        state_chain(c, pus, Pt, t_P, t * 128, halo and c == 1)
                    yield
                if gi == NG - 1:
                    assert chunk_ctr["n"] % 2 == 0
                return
            if ABL == "nohg" and not samp:
                return
            qd, t_qd, sg, t_sg = ctx["qd"], ctx["t_qd"], ctx["sg"], ctx["t_sg"]
            ocol0 = NMAIN if samp else gi * G
            lastg = (not samp) and gi == NGM - 1
            if samp:
                A2v = Arena(big, BIGN)
                wg_v = A2v.alloc([128, 8, 2048], BF16)
                wua_v = A2v.alloc([128, 4, 1024], BF16)
                wub_v = A2v.alloc([128, 4, 1024], BF16)
                assert A2v.off <= 8 * NFM
                wdead = [t_w[n_] for n_ in ("hq", "hf", "hg", "sq", "k", "rot")]
                w2pre = [
                    lambda: DMA("pool", wg_v[:, :, 0:1024], w_in_v[:, :, 2816:3840], [], wdead, "w2pre"),
                    lambda: DMA("pool", wua_v, w_ua.rearrange("(k p) c -> p k c", p=128), [], wdead, "w2pre"),
                    lambda: DMA("pool", wub_v, w_ub.rearrange("(k p) c -> p k c", p=128), [], wdead, "w2pre"),
                    lambda: DMA("pool", wg_v[:, :, 1024:2048], w_in_v[:, :, 3840:4864], [], wdead, "w2pre"),
                ]
            og, t_og = nxt(ogs, "og")
            ntile = 1 if samp else G // 128
            for t in range(ntile):
                ntk = nt if samp else 128
                tcol = t * 128
                vt, t_vt = proj_tm(h, t_h, tcol, ntk)
                yield
                ket, t_ket = ke_transpose(ke if samp else ctx["ke2"], t_ke if samp else ctx["t_ke2"], tcol, ntk)
                pa, t_pa = PW()
                for hd in range(4):
                    MM(pa[0:ntk, hd * ntk:(hd + 1) * ntk], ke[:, hd, tcol:tcol + ntk], qd[:, hd, tcol:tcol + ntk], True, True, [t_ke[hd], t_qd], [t_pa])
                am, t_am = nxt(ams, "am")
                msk = mhgs[:, :, :] if samp else mhg[:, :, :]
                TT("dve", am[0:ntk, :, 0:ntk], pa[0:ntk, 0:4 * ntk].rearrange("p (h t) -> p h t", h=4), msk, ALU.mult,
                   [t_pa, t_mhg, t_mhgs], [t_am])
                yield
                if not samp:
                    pus = state_U(ket, t_ket, vt, t_vt)
                pso, t_pso = plong
                for hd in range(4):
                    MM(pso[:, hd * ntk:(hd + 1) * ntk], vt[0:ntk, hd * 128:(hd + 1) * 128], am[0:ntk, hd, 0:ntk], hd == 0, False, [t_vt, t_am], [t_pso], skip=True)
                if not samp:
                    for c in range(2):
                        p_ = chunk_ctr["n"] % 2
                        for hd in range(4):
                            MM(pso[:, hd * ntk + c * 64:hd * ntk + (c + 1) * 64], Sbf2[p_][:, hd, :], qd[:, hd, tcol + c * 64:tcol + (c + 1) * 64],
                               False, True, [t_Sbf2[p_][hd], t_qd], [t_pso], skip=True)
                        state_chain(c, pus, Pt, t_P, tcol, True)
                    yield
                else:
                    yield from sample_states(pso, t_pso, qd, t_qd, ket, t_ket, vt, t_vt, Pt, t_P, w2pre)
                o32, t_o32 = nxt(o32s, "o32")
                osq, t_osq = nxt(osqs, "osq")
                yield from hgrn_out(o32, t_o32, osq, t_osq, sg, t_sg, og, t_og, tcol, ntk, pso, t_pso)
                yield
            DMA("sp", ogs_d[:, ocol0:ocol0 + nt].rearrange("(k p) n -> p k n", p=128), og[:, :, 0:nt], [t_og], [], "st_og%d" % (rr["og"] % 2))
            if lastg:
                DMA("sp", ns_d.rearrange("h d e -> d h e"), S32[:], t_S32, [], "o_ns")

        def stage2a(ctx):
            kind, gi, samp, nt = ctx["kind"], ctx["gi"], ctx["samp"], ctx["nt"]
            if kind == "pre" or (ABL == "noatt" and not samp):
                return
            qr, t_qr, kr, t_kr, tiles, cscol0 = ctx["qr"], ctx["t_qr"], ctx["kr"], ctx["t_kr"], ctx["tiles"], ctx["cscol0"]
            ocol0 = NMAIN if samp else gi * G
            lastg = (not samp) and gi == NGM - 1
            obT, t_obT = nxt(obTs, "obT")
            ntile = 1 if samp else G // 128
            for t in range(ntile):
                ntk = nt if samp else 128
                tcol = t * 128
                tidx = tiles[t]
                want32 = samp or (lastg and t == ntile - 1)
                if not samp:
                    kcol = cscol0 + tcol

                    def mk(kc, ti, mi):
                        return (lambda kc=kc, ti=ti: ([k2t[:, j_, kc:kc + 128] for j_ in range(2)], [vaug[:, ti, j_, :] for j_ in range(2)],
                                                      [t_k2t[ti], t_vaug[ti]]),
                                negm[:, mi, :, :], 128)
                    blocks = [mk(kcol - 128, tidx - 1, 2 if tidx == 1 else 1), mk(kcol, tidx, 0)]
                    yield from attention(qr, t_qr, tcol, 128, blocks, obT, t_obT, tcol, po_att)
                else:
                    po_att2 = (banks[6], nt_[2])
                    narrow[:] = [e for e in narrow if e[1] is not nt_[2]]
                    yield from sample_attention(qr, t_qr, obT, t_obT, po_att, po_att2)
                if want32:
                    out_kv(kr, t_kr, ctx["v32"], tcol, ntk, samp)
                    yield
            DMA("sp", obs_d[:, ocol0:ocol0 + nt].rearrange("(k p) n -> p k n", p=128), obT[:, :, 0:nt], [t_obT], [], "st_ob%d" % (rr["obT"] % 2))

        step_counts = {}

        def interleave(named_gens, speed=None):
            gens = [(nm, g) for nm, g in named_gens if g is not None]
            done = {nm: 0 for nm, _ in gens}
            tot = {nm: (counts or {}).get(nm, 1) / (speed or {}).get(nm.split("_")[0], 1.0) for nm, _ in gens}
            live = dict(gens)
            while live:
                nm = min(live, key=lambda n: done[n] / max(tot[n], 1))
                try:
                    next(live[nm])
                    done[nm] += 1
                except StopIteration:
                    del live[nm]
            for nm, _ in gens:
                step_counts[nm] = done[nm]

        def run_pipelined():
            ctxs = [dict() for _ in jobs]
            xq.append(load_x(xT, *job_x(jobs[0]), []))
            for _ in stage1(0, ctxs[0]):
                pass
            for ji in range(len(jobs)):
                g2h = stage2h(ctxs[ji])
                g2a = stage2a(ctxs[ji])
                g1 = stage1(ji + 1, ctxs[ji + 1]) if ji + 1 < len(jobs) else None
                interleave([("a1s2h_%d" % ji, g2h), ("a1s2a_%d" % ji, g2a), ("a1s1_%d" % (ji + 1), g1)], speed={"a1s1": 0.8})
                if stop == "pre" and jobs[ji] == ("pre", NG - 1):
                    return True
            return False

        xq = []
        if run_pipelined():
            finish()
            return nc
        if stop == "A1":
            finish()
            return nc

        S.barrier()
        A.reset()
        rr.clear()
        narrow[:] = narrow_all
        wg = A.alloc([128, 8, 2048], BF16)
        wua = A.alloc([128, 4, 1024], BF16)
        wub = A.alloc([128, 4, 1024], BF16)
        wo = A.alloc([128, 8, 1024], BF16)
        t_w2 = {n: Tok("w2_" + n) for n in ("g", "ua", "ub", "wo")}
        DMA("pool", wo, w_o.rearrange("(k p) c -> p k c", p=128), [], [t_w2["wo"]], "w2_wo")
        xs = slots("xs", [128, 8, G], F32, 2)
        xsq = slots("xsq", [128, G], BF16, 2)
        rstd = slots("rstd", [128, G], F32, 2)
        hTs = slots("hT", [128, 8, G], BF16, 2)
        og2 = slots("og2", [128, 4, G], BF16, 2)
        ob2 = slots("ob2", [128, 4, G], BF16, 2)
        sga = slots("sga", [128, G], F32, 2)
        sgb = slots("sgb", [128, G], F32, 2)
        mt1 = slots("mt1", [128, G], F32, 2)
        mt2 = slots("mt2", [128, G], F32, 2)
        mTs = slots("mT", [128, 8, G], BF16, 2)
        t_mT_all = [[Tok("mT%d_%d" % (s_, i)) for i in range(8)] for s_ in range(2)]
        x1o = slots("x1o", [128, G], F32, 3)
        assert A.off <= BIGN - 8 * 4096, ("phase A2 working set overlaps the wf1 prefetch region", A.off)
        wf1_pre = big[:, BIGN - 8 * 4096:BIGN].rearrange("p (k c) -> p k c", k=8)
        f1v = w_f1.rearrange("(k p) c -> p k c", p=128)
        for q in range(4):
            DMA("pool", wf1_pre[:, :, q * 1024:(q + 1) * 1024], f1v[:, :, q * 1024:(q + 1) * 1024], [], [], "wB%d" % q)

        def load_a2(col0, nt):
            xt, t_x = load_x(xT, NPRE + col0, nt, [])
            og, t_og = nxt(og2, "og2")
            ob, t_ob = nxt(ob2, "ob2")
            DMA("sp", og[:, :, 0:nt], ogs_d[:, col0:col0 + nt].rearrange("(k p) n -> p k n", p=128), [], [t_og], "og2%d" % (rr["og2"] % 2))
            DMA("sp", ob[:, :, 0:nt], obs_d[:, col0:col0 + nt].rearrange("(k p) n -> p k n", p=128), [], [t_ob], "ob2%d" % (rr["ob2"] % 2))
            return (xt, t_x, og, t_og, ob, t_ob)

        def a2_s1(ld, nt, res):
            xt, t_x = ld[0], ld[1]
            yield from norm_h_gen(xt, t_x, nt, 0, res)

        def a2_s2(ld, res, nt, ocol0, nxt_ld, out):
            xt, t_x, og, t_og, obT, t_obT = ld
            h, t_h = res["h"], res["t_h"]
            mT, _ = nxt(mTs, "mT")
            t_mTc = t_mT_all[(rr["mT"] - 1) % 2]
            for jc in range(8):
                pga, t_pga = PN()
                for k in range(8):
                    MM(pga[:, 0:nt], wg[:, k, jc * 128:(jc + 1) * 128], h[:, k, 0:nt], k == 0, k == 7, [t_w2["g"], t_h], [t_pga])
                a, t_a = nxt(sga, "sga")
                ACT(a[:, 0:nt], pga[:, 0:nt], AF.Sigmoid, [t_pga], [t_a])
                pgb, t_pgb = PN()
                for k in range(8):
                    MM(pgb[:, 0:nt], wg[:, k, 1024 + jc * 128:1024 + (jc + 1) * 128], h[:, k, 0:nt], k == 0, k == 7, [t_w2["g"], t_h], [t_pgb])
                b, t_b = nxt(sgb, "sgb")
                ACT(b[:, 0:nt], pgb[:, 0:nt], AF.Sigmoid, [t_pgb], [t_b])
                pya, t_pya = PN()
                for k in range(4):
                    MM(pya[:, 0:nt], wua[:, k, jc * 128:(jc + 1) * 128], og[:, k, 0:nt], k == 0, k == 3, [t_w2["ua"], t_og], [t_pya])
                pyb, t_pyb = PN()
                for k in range(4):
                    MM(pyb[:, 0:nt], wub[:, k, jc * 128:(jc + 1) * 128], obT[:, k, 0:nt], k == 0, k == 3, [t_w2["ub"], t_obT], [t_pyb])
                m1, t_m1 = nxt(mt1, "mt1")
                m2, t_m2 = nxt(mt2, "mt2")
                TT("dve", m1[:, 0:nt], pya[:, 0:nt], a[:, 0:nt], ALU.mult, [t_pya, t_a], [t_m1])
                TT("dve", m2[:, 0:nt], pyb[:, 0:nt], b[:, 0:nt], ALU.mult, [t_pyb, t_b], [t_m2])
                TT("pool", mT[:, jc, 0:nt], m1[:, 0:nt], m2[:, 0:nt], ALU.add, [t_m1, t_m2], [t_mTc[jc]])
                yield
            for jc in range(8):
                px, t_px = PN()
                for k in range(8):
                    MM(px[:, 0:nt], wo[:, k, jc * 128:(jc + 1) * 128], mT[:, k, 0:nt], k == 0, k == 7, [t_w2["wo"], t_mTc[k]], [t_px])
                xo, t_xo = nxt(x1o, "x1o")
                TT("dve", xo[:, 0:nt], px[:, 0:nt], xt[:, jc, 0:nt], ALU.add, [t_px, t_x], [t_xo])
                DMA("sp", x1s[jc * 128:(jc + 1) * 128, ocol0:ocol0 + nt], xo[:, 0:nt], [t_xo], [], "x1o%d" % (rr["x1o"] % 3))
                yield
            out["ld"] = load_a2(*nxt_ld) if nxt_ld is not None else None

        a2_jobs = [(gi * G, G) for gi in range(NGM)] + [(NMAIN, NSAMP)]
        lds = {0: load_a2(*a2_jobs[0])}
        if len(a2_jobs) > 1:
            lds[1] = load_a2(*a2_jobs[1])
        ress = [dict() for _ in a2_jobs]
        for _ in a2_s1(lds[0], a2_jobs[0][1], ress[0]):
            pass
        for ji in range(len(a2_jobs)):
            out = {}
            nl = a2_jobs[ji + 2] if ji + 2 < len(a2_jobs) else None
            g2 = a2_s2(lds[ji], ress[ji], a2_jobs[ji][1], a2_jobs[ji][0], nl, out)
            g1 = a2_s1(lds[ji + 1], a2_jobs[ji + 1][1], ress[ji + 1]) if ji + 1 < len(a2_jobs) else None
            interleave([("a2s2_%d" % ji, g2), ("a2s1_%d" % (ji + 1), g1)], speed={"a2s1": 0.6})
            if out.get("ld") is not None:
                lds[ji + 2] = out["ld"]
        if stop == "A2":
            finish()
            return nc

        S.barrier()
        A.reset()
        rr.clear()
        wf2 = A.alloc([128, 32, 1024], BF16)
        t_wB = [Tok("wB%d" % i) for i in range(8)]
        f2v = w_f2.rearrange("(k p) c -> p k c", p=128)
        for q in range(4):
            DMA("pool", wf2[:, q * 8:(q + 1) * 8, :], f2v[:, q * 8:(q + 1) * 8, :], [], [t_wB[4 + q]], "wB%d" % (4 + q))
        xs = slots("xs", [128, 8, G], F32, 2)
        xsq = slots("xsq", [128, G], BF16, 4)
        rstd = slots("rstd", [128, G], F32, 2)
        hTs = slots("hT", [128, 8, G], BF16, 2)
        aTs = slots("aT", [128, 32, G], BF16, 1)
        t_aTc = [Tok("aT%d" % i) for i in range(32)]
        rl = slots("rl", [128, G], F32, 3)
        x2s = slots("x2", [128, 8, G], F32, 1)
        t_x2c = [Tok("x2_%d" % i) for i in range(8)]
        yts = slots("yt", [128, 1024], F32, 2)
        assert A.off <= BIGN - 8 * 4096, ("phase B working set overlaps wf1", A.off)
        print("B arena bytes", A.off * 2)
        wf1 = wf1_pre

        plong2 = wide[2]
        del wide[2]

        def b_s1(ld, nt, res):
            yield from norm_h_gen(ld[0], ld[1], nt, 8, res)

        def b_s2(ld, res, nt, nxt_ld, out):
            xt, t_x = ld
            h, t_h = res["h"], res["t_h"]
            aT, _ = aTs[0]
            for c in range(32):
                pb, t_pb = PN()
                for k in range(8):
                    MM(pb[:, 0:nt], wf1[:, k, c * 128:(c + 1) * 128], h[:, k, 0:nt], k == 0, k == 7, [t_wB[c // 8], t_h], [t_pb])
                r, t_r = nxt(rl, "rl")
                ACT(r[:, 0:nt], pb[:, 0:nt], AF.Relu, [t_pb], [t_r])
                TT("pool" if c % 2 else "dve", aT[:, c, 0:nt], r[:, 0:nt], r[:, 0:nt], ALU.mult, [t_r], [t_aTc[c]])
                yield
            x2, _ = x2s[0]
            pss, t_pss = plong2
            pend = None
            for jc in range(8):
                pb, t_pb = PN()
                for k in range(32):
                    MM(pb[:, 0:nt], wf2[:, k, jc * 128:(jc + 1) * 128], aT[:, k, 0:nt], k == 0, k == 31, [t_wB[4 + k // 8], t_aTc[k]], [t_pb])
                TT("dve", x2[:, jc, 0:nt], pb[:, 0:nt], xt[:, jc, 0:nt], ALU.add, [t_pb, t_x], [t_x2c[jc]])
                q, t_q = nxt(xsq, "xsq")
                ACT(q[:, 0:nt], x2[:, jc, 0:nt], AF.Square, [t_x2c[jc]], [t_q])
                if pend is not None:
                    MM(pss[:, 0:nt], ones16[:], pend[0][:, 0:nt], pend[2] == 0, False, [t_ones, pend[1]], [t_pss])
                pend = (q, t_q, jc)
                yield
            MM(pss[:, 0:nt], ones16[:], pend[0][:, 0:nt], False, True, [t_ones, pend[1]], [t_pss])
            out["ld"] = load_x(x1s, nxt_ld[0], nxt_ld[1], []) if nxt_ld is not None else None

        def b_s3(col0, nt):
            x2, _ = x2s[0]
            pss, t_pss = plong2
            r, t_r = nxt(rstd, "rstd")
            rstd_from(pss[:, 0:nt], t_pss, r[:, 0:nt], t_r, 1.0 / D, nt)
            yield
            for jc in range(8):
                STT(x2[:, jc, 0:nt], x2[:, jc, 0:nt], ncol[:, 16 + jc:17 + jc], r[:, 0:nt], ALU.mult, ALU.mult, [t_x2c[jc], t_ncol, t_r], [t_x2c[jc]])
                if jc % 4 == 3:
                    yield
            for t in range((nt + 127) // 128):
                ntk = min(128, nt - t * 128)
                yt, t_yt = nxt(yts, "yt")
                for half in range(2):
                    pw, t_pw = PW()
                    for jj in range(4):
                        jc = half * 4 + jj
                        TR(pw[0:ntk, jj * 128:(jj + 1) * 128], x2[:, jc, t * 128:t * 128 + ntk], id32[:], [t_x2c[jc], t_id32], [t_pw])
                    CP("act" if half else "dve", yt[0:ntk, half * 512:(half + 1) * 512], pw[0:ntk, 0:512], [t_pw], [t_yt])
                    yield
                DMA("sp", y_d[col0 + t * 128:col0 + t * 128 + ntk, :], yt[0:ntk, :], [t_yt], [], "o_y%d" % (rr["yt"] % 2))

        b_jobs = [(gi * G, G) for gi in range(NGM)] + [(NMAIN, NSAMP)]
        ldsB = {0: load_x(x1s, b_jobs[0][0], b_jobs[0][1], []), 1: load_x(x1s, b_jobs[1][0], b_jobs[1][1], [])}
        resB = [dict() for _ in b_jobs]
        for _ in b_s1(ldsB[0], b_jobs[0][1], resB[0]):
            pass
        for ji in range(len(b_jobs)):
            out = {}
            nl = b_jobs[ji + 2] if ji + 2 < len(b_jobs) else None
            g2 = b_s2(ldsB[ji], resB[ji], b_jobs[ji][1], nl, out)
            g1 = b_s1(ldsB[ji + 1], b_jobs[ji + 1][1], resB[ji + 1]) if ji + 1 < len(b_jobs) else None
            g3 = b_s3(b_jobs[ji - 1][0], b_jobs[ji - 1][1]) if ji > 0 else None
            interleave([("bs2_%d" % ji, g2), ("bs1_%d" % (ji + 1), g1), ("bs3_%d" % (ji - 1), g3)], speed={"bs3": 0.35, "bs1": 0.6})
            if out.get("ld") is not None:
                ldsB[ji + 2] = out["ld"]
        for _ in b_s3(b_jobs[-1][0], b_jobs[-1][1]):
            pass

        finish()
    if count_only:
        return step_counts
    return nc


def build_two_pass(stop=None):
    counts = build_program(None, None, True)
    return build_program(stop, counts, False)


_NC_CACHE = {}


def _consts():
    half = 8
    inv = np.exp(-math.log(500000.0) * np.arange(half, dtype=np.float32) * np.float32(2.0 / 16)).astype(np.float32)
    s = np.arange(128)
    m_hg = ((s[:, None] <= s[None, :]) & ((s[:, None] // 64) == (s[None, :] // 64))).astype(np.float32)
    s64 = np.arange(64)
    m_hg_s = ((s64[:, None] <= s64[None, :]) & ((s64[:, None] // 4) == (s64[None, :] // 4))).astype(np.float32)
    own = np.where(s[:, None] <= s[None, :], 0.0, NEG).astype(np.float32)
    prev = np.where(s[:, None] > s[None, :], 0.0, NEG).astype(np.float32)
    allneg = np.full((128, 128), NEG, np.float32)
    negsc = np.full((128, NSEQ, 64), NEG, np.float32)
    for q in range(64):
        sq, tq = q // 4, q % 4
        negsc[tq + 1:, sq, q] = 0.0
    negsn = np.where(((s64[:, None] // 4) == (s64[None, :] // 4)) & (s64[:, None] <= s64[None, :]), 0.0, NEG).astype(np.float32)
    ind = (s64[:, None] // 4 == np.arange(NSEQ)[None, :]).astype(np.float32)
    return inv, m_hg, m_hg_s, own, prev, allneg, negsc, negsn, ind


def _perm():
    p = np.zeros((128, 128), np.float32)
    for d in range(128):
        i = d % 64
        s = d + 8 if i < 8 else (d - 8 if i < 16 else d)
        p[s, d] = 1.0
    return p


def _cs_table(pos, inv):
    ang = pos.astype(np.float32)[None, :] * inv[:, None]
    c = np.cos(ang.astype(np.float64)).astype(np.float32)
    sn = np.sin(ang.astype(np.float64)).astype(np.float32)
    n = pos.shape[0]
    C = np.ones((64, n), np.float32)
    Sg = np.zeros((64, n), np.float32)
    C[0:8] = c
    C[8:16] = c
    Sg[0:8] = -sn
    Sg[8:16] = sn
    out = np.stack([np.concatenate([C, C], 0), np.concatenate([Sg, Sg], 0)], axis=1)
    return np.ascontiguousarray(out)


def prepare(x_prompt, x_sample, cache_swa_k, cache_swa_v, state_hgrn, w_in, hgrn_lb_logits, hgrn_norm_w,
            sinks, w_up_a, w_up_b, w_o, norm1_w, norm2_w, w_ff1, w_ff2, normf_w, cores=range(8)):
    f = lambda a: np.ascontiguousarray(np.asarray(a, dtype=np.float32))
    x_prompt, x_sample = f(x_prompt), f(x_sample)
    ck_all, cv_all, s0_all = f(cache_swa_k)[0], f(cache_swa_v)[0], f(state_hgrn)[0]
    inv, m_hg, m_hg_s, own, prev, allneg, negsc, negsn, ind = _consts()
    col = lambda v: np.ascontiguousarray(f(v).reshape(8, 128).T)
    ncols = np.concatenate([col(norm1_w[0]), col(norm2_w[0]), col(normf_w)], axis=1)
    lbl = np.ascontiguousarray(f(hgrn_lb_logits).reshape(2, 4, 128).transpose(2, 0, 1).reshape(128, 8))
    hgnw = np.ascontiguousarray(f(hgrn_norm_w)[0].T)
    shared = {
        "w_in": f(w_in)[0], "w_up_a": f(w_up_a)[0], "w_up_b": f(w_up_b)[0], "w_o": f(w_o)[0],
        "w_ff1": f(w_ff1)[0], "w_ff2": f(w_ff2)[0], "lbl": lbl, "hgnw": hgnw, "ncols": ncols,
        "sinks": f(sinks).reshape(1, 8), "m_hg": m_hg, "m_hg_s": m_hg_s, "negm_sc": negsc, "negm_sn": negsn,
        "ind16": ind, "ident": np.eye(128, dtype=np.float32), "perm": _perm(),
    }
    in_maps = []
    for c in cores:
        b, hf = c // 2, c % 2
        xT = np.zeros((D, NX), np.float32)
        if hf == 1:
            xT[:, 0:NPRE] = x_prompt[b, 0:2048].T
        xT[:, NPRE:NPRE + NMAIN] = x_prompt[b, hf * 2048:(hf + 1) * 2048].T
        xT[:, NPRE + NMAIN:] = x_sample[16 * c:16 * c + 16].reshape(64, D).T
        base = hf * 2048
        pos = np.concatenate([np.arange(base - 128, base + 2048), PAST_LEN + np.tile(np.arange(4), NSEQ)])
        negm = np.stack([own, prev, prev if hf == 1 else allneg], axis=1)
        m = dict(shared)
        m["xT"] = xT
        m["cs"] = _cs_table(pos, inv)
        m["negm"] = np.ascontiguousarray(negm)
        ck = ck_all[16 * c:16 * c + 16]
        m["ckT"] = np.ascontiguousarray(ck.transpose(0, 2, 3, 1))
        m["ck"] = np.ascontiguousarray(ck.reshape(16, 128, 128))
        m["cv"] = np.ascontiguousarray(cv_all[16 * c:16 * c + 16].reshape(16, 128, 128))
        m["s0"] = np.ascontiguousarray(s0_all[16 * c:16 * c + 16])
        in_maps.append(m)
    return in_maps


def assemble(R):
    y_prompt = np.zeros((4, 4096, D), np.float32)
    y_sample = np.zeros((128, 4, D), np.float32)
    nkp = np.zeros((1, 4, 128, 2, 64), np.float32)
    nvp = np.zeros((1, 4, 128, 2, 64), np.float32)
    nsp = np.zeros((1, 4, 4, 128, 128), np.float32)
    nks = np.zeros((1, 128, 128, 2, 64), np.float32)
    nvs = np.zeros((1, 128, 128, 2, 64), np.float32)
    nss = np.zeros((1, 128, 4, 128, 128), np.float32)
    for c in range(8):
        b, hf = c // 2, c % 2
        r = R[c]
        y_prompt[b, hf * 2048:(hf + 1) * 2048] = r["y"][0:2048]
        y_sample[16 * c:16 * c + 16] = r["y"][2048:2112].reshape(16, 4, D)
        if hf == 1:
            nkp[0, b] = r["nk"].reshape(128, 2, 64)
            nvp[0, b] = r["nv"].reshape(128, 2, 64)
            nsp[0, b] = r["ns"]
        nks[0, 16 * c:16 * c + 16] = r["nks"].reshape(16, 128, 2, 64)
        nvs[0, 16 * c:16 * c + 16] = r["nvs"].reshape(16, 128, 2, 64)
        nss[0, 16 * c:16 * c + 16] = r["nss"]
    return (y_prompt, y_sample, nkp, nvp, nsp, nks, nvs, nss)


def kernel(**inputs):
    in_maps = prepare(**inputs)
    if "nc" not in _NC_CACHE:
        _NC_CACHE["nc"] = build_two_pass()
    nc = _NC_CACHE["nc"]
    res = run_bass_kernel_spmd(nc, in_maps, core_ids=list(range(8)))
    return assemble(res.results)
```

```python
import math
import numpy as np
from contextlib import ExitStack
import concourse.bass as bass
import concourse.mybir as mybir
from concourse.bass_utils import run_bass_kernel_spmd

F32 = mybir.dt.float32
BF16 = mybir.dt.bfloat16
AF = mybir.ActivationFunctionType
ALU = mybir.AluOpType

ENGS = ("pe", "act", "dve", "pool", "sp")
LABELS = None

D = 1024
NPRE = 2048
NMAIN = 2048
NSAMP = 64
NX = NPRE + NMAIN + NSAMP
G = 256
NCS = 128 + NMAIN + NSAMP
NSEQ = 16
EPS = 1e-6
NEG = -30000.0
PAST_LEN = 16384


class Tok:
    __slots__ = ("w", "r", "name", "excl")

    def __init__(self, name="", excl=False):
        self.w = None
        self.r = []
        self.name = name
        self.excl = excl


class Op:
    __slots__ = ("eng", "fn", "deps", "needs", "cnt", "key", "is_dma", "waits")

    def __init__(self, eng, fn, is_dma, key):
        self.eng = eng
        self.fn = fn
        self.is_dma = is_dma
        self.key = key
        self.deps = []
        self.needs = False
        self.cnt = None
        self.waits = None


class Sched:
    def __init__(self, nc, same_engine_sync=("act", "pool", "dve")):
        self.nc = nc
        self.ops = {e: [] for e in ENGS}
        self.dma_cnt = {}
        self.same = set(same_engine_sync)
        self.skip_same_waw = True
        self.final_waits = []
        self.last_dma = {}

    def barrier(self):
        lasts = []
        for e in ENGS:
            comp = [o for o in self.ops[e] if (not o.is_dma) and o.fn is not None]
            if comp:
                lasts.append(comp[-1])
        lasts += list(self.last_dma.values())
        for e in ENGS:
            op = Op(e, None, False, None)
            op.deps = [d for d in lasts]
            for d in lasts:
                d.needs = True
            self.ops[e].append(op)

    def add(self, eng, fn, reads=(), writes=(), dma_key=None):
        op = Op(eng, fn, dma_key is not None, dma_key)
        if LABELS is not None:
            import sys as _s
            f = _s._getframe(2)
            lab = []
            for _ in range(3):
                if f is None:
                    break
                lab.append("%s:%d" % (f.f_code.co_name, f.f_lineno))
                f = f.f_back
            LABELS.setdefault(eng, []).append("/".join(lab))
        writes = list(writes) + [t for t in reads if t.excl]
        reads = [t for t in reads if not t.excl]
        deps = {}
        for t in reads:
            if t.w is not None:
                deps[id(t.w)] = t.w
        for t in writes:
            if t.w is not None:
                if not (self.skip_same_waw and eng in ("act", "dve") and (not t.w.is_dma) and t.w.eng == eng
                        and dma_key is None and not t.excl):
                    deps[id(t.w)] = t.w
            for r in t.r:
                deps[id(r)] = r
        dl = []
        for d in deps.values():
            if d is op:
                continue
            if (not d.is_dma) and d.eng == eng and eng not in self.same:
                continue
            dl.append(d)
            d.needs = True
        op.deps = dl
        for t in reads:
            t.r.append(op)
        for t in writes:
            t.w = op
            t.r = []
        if op.is_dma:
            c = self.dma_cnt.get(dma_key, 0) + 16
            self.dma_cnt[dma_key] = c
            op.cnt = c
            self.last_dma[dma_key] = op
        self.ops[eng].append(op)
        return op

    def finalize_and_emit(self, es):
        nc = self.nc
        esem = {e: es.enter_context(nc.semaphore("c_" + e)) for e in ENGS}
        dsem = {k: es.enter_context(nc.semaphore("d_%d" % i)) for i, k in enumerate(self.dma_cnt)}
        for e in ENGS:
            c = 0
            for op in self.ops[e]:
                if not op.is_dma and op.needs and op.fn is not None:
                    c += 1
                    op.cnt = c
        for e in ENGS:
            waited = {}
            for op in self.ops[e]:
                w = {}
                for d in op.deps:
                    key = ("d", d.key) if d.is_dma else ("e", d.eng)
                    if d.cnt > w.get(key, 0):
                        w[key] = d.cnt
                ws = []
                for key, v in w.items():
                    if waited.get(key, 0) >= v:
                        continue
                    waited[key] = v
                    ws.append((dsem[key[1]] if key[0] == "d" else esem[key[1]], v))
                op.waits = ws
        final = [(dsem[k], self.dma_cnt[k]) for k in self.final_waits]
        block = es.enter_context(nc.Block())

        def emit(e, eng):
            for op in self.ops[e]:
                for (sm, v) in op.waits:
                    eng.wait_ge(sm, v)
                if op.fn is None:
                    continue
                ins = op.fn(eng)
                if op.is_dma:
                    ins.then_inc(dsem[op.key], 16)
                elif op.needs:
                    ins.then_inc(esem[e], 1)
            if e == "sp":
                for (sm, v) in final:
                    eng.wait_ge(sm, v)

        @block.tensor
        def _(eng):
            emit("pe", eng)

        @block.scalar
        def _(eng):
            emit("act", eng)

        @block.vector
        def _(eng):
            emit("dve", eng)

        @block.gpsimd
        def _(eng):
            emit("pool", eng)

        @block.sync
        def _(eng):
            emit("sp", eng)


class Arena:
    def __init__(self, big, nelem):
        self.big = big
        self.n = nelem
        self.off = 0

    def reset(self):
        self.off = 0

    def alloc(self, shape, dt):
        n = 1
        for s in shape[1:]:
            n *= s
        nb = n * 2 if dt == F32 else n
        nb = (nb + 1) // 2 * 2
        assert self.off + nb <= self.n, ("arena overflow", self.off, nb, self.n)
        v = self.big[0:shape[0], self.off:self.off + nb]
        self.off += nb
        if dt == F32:
            v = v.bitcast(F32)
        v = v[:, 0:n]
        if len(shape) == 3:
            v = v.rearrange("p (a b) -> p a b", a=shape[1])
        elif len(shape) == 4:
            v = v.rearrange("p (a b c) -> p a b c", a=shape[1], b=shape[2])
        return v


class _Stop(Exception):
    pass


ABL = ""


def build_program(stop=None, counts=None, count_only=False):
    nc = bass.Bass("TRN2", target_bir_lowering=False)

    def din(name, shape):
        return nc.dram_tensor(name, list(shape), F32, kind="ExternalInput").ap()

    def dout(name, shape):
        return nc.dram_tensor(name, list(shape), F32, kind="ExternalOutput").ap()

    xT = din("xT", [D, NX])
    cs_d = din("cs", [128, 2, NCS])
    w_in = din("w_in", [D, 4864])
    w_ua = din("w_up_a", [512, D])
    w_ub = din("w_up_b", [512, D])
    w_o = din("w_o", [D, D])
    w_f1 = din("w_ff1", [D, 4096])
    w_f2 = din("w_ff2", [4096, D])
    lbl_d = din("lbl", [128, 8])
    hgnw_d = din("hgnw", [128, 4])
    ncol_d = din("ncols", [128, 24])
    sinks_d = din("sinks", [1, 8])
    mhg_d = din("m_hg", [128, 128])
    mhgs_d = din("m_hg_s", [64, 64])
    negm_d = din("negm", [128, 3, 128])
    negsc_d = din("negm_sc", [128, NSEQ, 64])
    negsn_d = din("negm_sn", [64, 64])
    ind_d = din("ind16", [64, NSEQ])
    ident_d = din("ident", [128, 128])
    perm_d = din("perm", [128, 128])
    ckT_d = din("ckT", [NSEQ, 2, 64, 128])
    ck_d = din("ck", [NSEQ, 128, 128])
    cv_d = din("cv", [NSEQ, 128, 128])
    s0_d = din("s0", [NSEQ, 4, 128, 128])

    NTO = NMAIN + NSAMP
    y_d = dout("y", [NTO, D])
    nk_d = dout("nk", [128, 128])
    nv_d = dout("nv", [128, 128])
    ns_d = dout("ns", [4, 128, 128])
    nks_d = dout("nks", [NSEQ, 128, 128])
    nvs_d = dout("nvs", [NSEQ, 128, 128])
    nss_d = dout("nss", [NSEQ, 4, 128, 128])
    x1s = nc.dram_tensor("x1s", [D, NTO], F32, kind="Internal").ap()
    ogs_d = nc.dram_tensor("ogs", [512, NTO], BF16, kind="Internal").ap()
    obs_d = nc.dram_tensor("obs", [512, NTO], BF16, kind="Internal").ap()

    es = ExitStack()
    with es:
        S = Sched(nc)

        def sb(name, shape, dt=F32):
            return es.enter_context(nc.sbuf_tensor("s_" + name, list(shape), dt))

        def MM(out, lhsT, rhs, st, sp, R, W, skip=False):
            S.add("pe", lambda e: e.matmul(out, lhsT=lhsT, rhs=rhs, start=st, stop=sp, skip_group_check=skip), R, W)

        def TR(out, in_, idn, R, W):
            S.add("pe", lambda e: e.transpose(out, in_, idn), R, W)

        def ACT(out, in_, func, R, W, scale=1.0, bias=0.0):
            S.add("act", lambda e: e.activation(out=out, in_=in_, func=func, bias=bias, scale=scale), R, W)

        def TS(eng, out, in0, s1, s2, op0, op1, R, W):
            S.add(eng, lambda e: e.tensor_scalar(out=out, in0=in0, scalar1=s1, scalar2=s2, op0=op0, op1=op1), R, W)

        def TT(eng, out, in0, in1, op, R, W):
            S.add(eng, lambda e: e.tensor_tensor(out=out, in0=in0, in1=in1, op=op), R, W)

        def STT(out, in0, scalar, in1, op0, op1, R, W):
            S.add("dve", lambda e: e.scalar_tensor_tensor(out=out, in0=in0, scalar=scalar, in1=in1, op0=op0, op1=op1), R, W)

        def CP(eng, out, in_, R, W):
            if eng == "act":
                ACT(out, in_, AF.Copy, R, W)
            else:
                S.add(eng, lambda e: e.tensor_copy(out=out, in_=in_), R, W)

        def DMA(eng, out, in_, R, W, key):
            S.add(eng, lambda e: e.dma_start(out=out, in_=in_), R, W, dma_key=key)

        def MEMSET(eng, ap, val, W):
            S.add(eng, lambda e: e.memset(ap, val), (), W)

        def finish():
            if count_only:
                return
            S.final_waits = [k for k in S.dma_cnt if k.startswith("o_")]
            print("ops", {e: len(S.ops[e]) for e in ENGS}, "dma keys", len(S.dma_cnt))
            S.finalize_and_emit(es)

        banks = [es.enter_context(nc.psum_tensor("psb%d" % i, [128, 512], F32)) for i in range(8)]
        wide = [(banks[i], Tok("pw%d" % i, True)) for i in range(3)]
        plong = (banks[3], Tok("plong", True))
        nt_ = [Tok("pn%d" % i, True) for i in range(4, 8)]
        narrow = [(banks[i][:, 0:256], nt_[i - 4]) for i in range(4, 8)] + [(banks[i][:, 256:512], nt_[i - 4]) for i in range(4, 8)]
        narrow_all = list(narrow)
        ctr = {"w": 0, "n": 0}

        def PW():
            b, t = wide[ctr["w"] % len(wide)]
            ctr["w"] += 1
            return b, t

        def PN():
            a, t = narrow[ctr["n"] % len(narrow)]
            ctr["n"] += 1
            return a, t

        id32 = sb("id32", [128, 128]); t_id32 = Tok()
        id16 = sb("id16", [128, 128], BF16); t_id16 = Tok()
        ones16 = sb("ones16", [128, 128], BF16); t_ones = Tok()
        ones16w = sb("ones16w", [128, 256], BF16)
        perm16 = sb("perm16", [128, 128], BF16); t_perm = Tok()
        lbl = sb("lbl", [128, 8]); lb = sb("lb", [128, 4]); omlb = sb("omlb", [128, 4]); t_lb = Tok()
        hgnw = sb("hgnw", [128, 4]); t_hgnw = Tok()
        ncol = sb("ncol", [128, 24]); t_ncol = Tok()
        esink = sb("esink", [128, 8]); t_esink = Tok()
        mhg = sb("mhg", [128, 4, 128], BF16); t_mhg = Tok()
        mhgs = sb("mhgs", [64, 4, 64], BF16); t_mhgs = Tok()
        negm = sb("negm", [128, 3, 4, 128], BF16); t_negm = Tok()
        negsc = sb("negsc", [128, NSEQ, 64], BF16); t_negsc = Tok()
        negsn = sb("negsn", [64, 4, 64], BF16); t_negsn = Tok()
        ind = sb("ind", [64, NSEQ]); t_ind = Tok()
        zeros = sb("zeros", [128, 64]); t_zeros = Tok()
        rmask = sb("rmask", [128, G]); t_rmask = Tok()
        epsc = sb("epsc", [128, 1]); t_epsc = Tok()

        DMA("sp", id32[:], ident_d, [], [t_id32], "id32")
        DMA("pool", id16[:], ident_d, [], [t_id16], "id16")
        DMA("pool", perm16[:], perm_d, [], [t_perm], "perm16")
        MEMSET("pool", ones16[:], 1.0, [t_ones])
        MEMSET("pool", ones16w[:], 1.0, [t_ones])
        MEMSET("pool", zeros[:], 0.0, [t_zeros])
        MEMSET("pool", rmask[:], 1.0, [t_rmask])
        MEMSET("pool", rmask[:, :].rearrange("p (c t) -> p c t", t=64)[:, :, 0:1], 0.0, [t_rmask])
        MEMSET("pool", epsc[:], EPS, [t_epsc])
        DMA("sp", lbl[:], lbl_d, [], [t_lb], "lbl")
        DMA("sp", hgnw[:], hgnw_d, [], [t_hgnw], "hgnw")
        DMA("sp", ncol[:], ncol_d, [], [t_ncol], "ncol")
        DMA("sp", esink[:], sinks_d.partition_broadcast(128), [], [t_esink], "esink")
        DMA("sp", ind[:], ind_d, [], [t_ind], "ind")
        TT("dve", lb[:], lbl[:, 0:4], lbl[:, 4:8], ALU.subtract, [t_lb], [t_lb])
        ACT(lb[:], lb[:], AF.Sigmoid, [t_lb], [t_lb])
        TS("dve", omlb[:], lb[:], -1.0, 1.0, ALU.mult, ALU.add, [t_lb], [t_lb])
        ACT(esink[:], esink[:], AF.Exp, [t_esink], [t_esink])

        if stop == "consts":
            finish()
            return nc
        BIGN = 98304
        big = sb("big", [128, BIGN], BF16)
        A = Arena(big, BIGN)
        rr = {}

        def slots(name, shape, dt, n):
            return [(A.alloc(shape, dt), Tok("%s%d" % (name, i))) for i in range(n)]

        def nxt(pool, name):
            i = rr.get(name, 0)
            rr[name] = i + 1
            return pool[i % len(pool)]

        def rstd_from(ps_ap, ps_tok, out_ap, out_tok, scale, n):
            ACT(out_ap, ps_ap, AF.Ln, [ps_tok, t_epsc], [out_tok], scale=scale, bias=epsc[:, 0:1])
            ACT(out_ap, out_ap, AF.Exp, [out_tok], [out_tok], scale=-0.5)

        w_in_v = w_in.rearrange("(k p) c -> p k c", p=128)

        NFM = 24 * 128
        wfm = A.alloc([128, 8, NFM], BF16)
        wtm = A.alloc([128, 8, 640], BF16)
        t_w = {n: Tok("w_" + n) for n in ("hf", "hi", "sv", "k", "hq", "hg", "sq", "rot")}
        WALL = list(t_w.values())

        def wload(dst, src, name):
            DMA("pool", dst, src, [], [t_w[name]], "w_" + name)

        wload(wfm[:, :, 4 * 128:8 * 128], w_in_v[:, :, 512:1024], "hf")
        wload(wtm[:, :, 0:512], w_in_v[:, :, 1024:1536], "hi")
        wload(wtm[:, :, 512:640], w_in_v[:, :, 2688:2816], "sv")
        kd = wfm[:, :, 16 * 128:18 * 128].rearrange("p k (j h c) -> p k j h c", j=2, h=2)
        ksrc = w_in_v[:, :, 2560:2688].rearrange("p k (j c) -> p k j c", j=2)
        for h in range(2):
            for j in range(2):
                wload(kd[:, :, j, h, :], ksrc[:, :, j, :], "k")
        for g in range(4):
            DMA("pool", mhg[:, g, :], mhg_d, [], [t_mhg], "mhg")
            DMA("pool", mhgs[:, g, :], mhgs_d, [], [t_mhgs], "mhgs")
            DMA("pool", negm[:, :, g, :], negm_d, [], [t_negm], "negm")
            DMA("pool", negsn[:, g, :], negsn_d, [], [t_negsn], "negsn")
        DMA("pool", negsc[:], negsc_d, [], [t_negsc], "negsc")
        wload(wfm[:, :, 0:4 * 128], w_in_v[:, :, 0:512], "hq")
        wload(wfm[:, :, 8 * 128:12 * 128], w_in_v[:, :, 1536:2048], "hg")
        wload(wfm[:, :, 12 * 128:16 * 128], w_in_v[:, :, 2048:2560], "sq")
        def emit_rot_copies():
            for (src0, dst0, nh, nm) in ((12 * 128, 18 * 128, 8, "sq"), (16 * 128, 22 * 128, 4, "k")):
                sv_ = wfm[:, :, src0:src0 + nh * 64].rearrange("p k (h c) -> p k h c", c=64)
                dv_ = wfm[:, :, dst0:dst0 + nh * 64].rearrange("p k (h c) -> p k h c", c=64)
                for k in range(8):
                    CP("pool", dv_[:, k, :, 0:8], sv_[:, k, :, 8:16], [t_w[nm]], [t_w["rot"]])
                    CP("pool", dv_[:, k, :, 8:16], sv_[:, k, :, 0:8], [t_w[nm]], [t_w["rot"]])
                    CP("pool", dv_[:, k, :, 16:64], sv_[:, k, :, 16:64], [t_w[nm]], [t_w["rot"]])

        if stop == "wA1":
            finish()
            return nc
        NKC = 128 + NMAIN + NSAMP
        k2t = A.alloc([128, 2, NKC], BF16)
        t_k2t = [Tok("k2t%d" % i) for i in range(18)]
        vaug = A.alloc([128, 18, 2, 65], BF16)
        t_vaug = [Tok("vaug%d" % i) for i in range(18)]
        MEMSET("pool", vaug[:], 1.0, t_vaug)
        S32 = A.alloc([128, 4, 128], F32); t_S32 = [Tok() for _ in range(4)]
        Sbf2 = [A.alloc([128, 4, 128], BF16) for _ in range(2)]
        t_Sbf2 = [[Tok() for _ in range(4)] for _ in range(2)]
        MEMSET("pool", S32[:], 0.0, t_S32)
        for p_ in range(2):
            MEMSET("pool", Sbf2[p_][:], 0.0, t_Sbf2[p_])
        chunk_ctr = {"n": 0}

        xs = slots("xs", [128, 8, G], F32, 1)
        xsq = slots("xsq", [128, G], BF16, 2)
        rstd = slots("rstd", [128, G], F32, 2)
        hTs = slots("hT", [128, 8, G], BF16, 2)
        fbs = slots("fb", [128, G], F32, 4)
        kbs = slots("kb", [128, G], F32, 4)
        Pbs = slots("Pb", [128, 4, G], F32, 2)
        t_P_all = [[Tok("P%d_%d" % (s_, i)) for i in range(4)] for s_ in range(2)]
        t_ke_all = [[Tok("ke%d_%d" % (s_, i)) for i in range(4)] for s_ in range(2)]
        t_den_all = [[Tok("den%d_%d" % (s_, i)) for i in range(2)] for s_ in range(2)]
        rPs = slots("rP", [128, G], F32, 2)
        bbs = slots("bb", [128, G], F32, 2)
        qds = slots("qd", [128, 4, G], BF16, 2)
        kes = slots("ke", [128, 4, G], BF16, 2)
        ke2s = slots("ke2", [128, 4, G], BF16, 2)
        sgs = slots("sg", [128, 4, G], BF16, 2)
        sgts = slots("sgt", [128, G], BF16, 1)
        vts = slots("vt", [128, 512], BF16, 2)
        v32s = slots("v32", [128, 128], F32, 2)
        kets = slots("ket", [128, 4, 128], BF16, 2)
        ams = slots("am", [128, 4, 128], BF16, 1)
        o32s = slots("o32", [128, 4, 128], F32, 2)
        osqs = slots("osq", [128, 4, 128], BF16, 1)
        rso = slots("rso", [128, 512], F32, 1)
        ogs = slots("og", [128, 4, G], BF16, 2)
        css = slots("cs", [128, 2, G], F32, 2)
        zbs = slots("zb", [128, G], BF16, 2)
        rt1 = slots("rt1", [128, G], F32, 2)
        rt2 = slots("rt2", [128, G], F32, 2)
        kr32 = slots("kr32", [128, 2, G], F32, 1)
        qrs = slots("qr", [128, 4, 2, G], BF16, 2)
        for q_ in qrs:
            MEMSET("pool", q_[0][:], 0.0, [q_[1]])
        pTs = slots("pT", [128, 512], BF16, 3)
        dens = slots("den", [128, 16], F32, 2)
        obt = slots("obt", [128, 512], BF16, 2)
        t_ob_all = [[Tok("ob%d_%d" % (s_, i)) for i in range(8)] for s_ in range(2)]
        obTs = slots("obT", [128, 4, G], BF16, 2)
        st_k = slots("stk", [128, 128], F32, 1)
        s0s = slots("s0f", [128, 4, 128], F32, 3)
        qd32 = (A.alloc([128, 4, NSAMP], F32), Tok("qd32"))
        sns = slots("sn", [128, 4, 128], F32, 2)
        kms = slots("km", [64, 4, 128], BF16, 2)
        kcs = slots("kc", [128, 2, 128], BF16, 3)
        vcs = slots("vc", [128, 2, 65], BF16, 3)
        for v_ in vcs:
            MEMSET("pool", v_[0][:], 1.0, [v_[1]])
        print("A1 arena bytes", A.off * 2)

        def load_x(src, col0, nt, R):
            xt, t_x = nxt(xs, "xs")
            DMA("sp", xt[:, :, 0:nt], src[:, col0:col0 + nt].rearrange("(k p) n -> p k n", p=128), R, [t_x], "xs%d" % (rr["xs"] % len(xs)))
            return xt, t_x

        def norm_h(xt, t_x, nt, ncolbase):
            pb, t_pb = PN()
            for k in range(8):
                q, t_q = nxt(xsq, "xsq")
                ACT(q[:, 0:nt], xt[:, k, 0:nt], AF.Square, [t_x], [t_q])
                MM(pb[:, 0:nt], ones16[:], q[:, 0:nt], k == 0, k == 7, [t_ones, t_q], [t_pb])
            r, t_r = nxt(rstd, "rstd")
            rstd_from(pb[:, 0:nt], t_pb, r[:, 0:nt], t_r, 1.0 / D, nt)
            h, t_h = nxt(hTs, "hT")
            for k in range(8):
                STT(h[:, k, 0:nt], xt[:, k, 0:nt], ncol[:, ncolbase + k:ncolbase + k + 1], r[:, 0:nt], ALU.mult, ALU.mult,
                    [t_x, t_ncol, t_r], [t_h])
            return h, t_h

        def norm_h_gen(xt, t_x, nt, ncolbase, res):
            pb, t_pb = plong
            for k in range(8):
                q, t_q = nxt(xsq, "xsq")
                ACT(q[:, 0:nt], xt[:, k, 0:nt], AF.Square, [t_x], [t_q])
                MM(pb[:, 0:nt], ones16[:], q[:, 0:nt], k == 0, k == 7, [t_ones, t_q], [t_pb])
                if k % 2 == 1:
                    yield
            r, t_r = nxt(rstd, "rstd")
            rstd_from(pb[:, 0:nt], t_pb, r[:, 0:nt], t_r, 1.0 / D, nt)
            yield
            h, t_h = nxt(hTs, "hT")
            for k in range(8):
                STT(h[:, k, 0:nt], xt[:, k, 0:nt], ncol[:, ncolbase + k:ncolbase + k + 1], r[:, 0:nt], ALU.mult, ALU.mult,
                    [t_x, t_ncol, t_r], [t_h])
                if k % 4 == 3:
                    yield
            res["h"] = h
            res["t_h"] = t_h

        NWARM = 0

        def proj_fm(h, t_h, chunk, nt, wt):
            pb, t_pb = PN()
            for _ in range(NWARM):
                MM(pb[:, 0:256], ones16[:], ones16w[:, 0:256], True, True, [t_ones], [t_pb])
            for k in range(8):
                MM(pb[:, 0:nt], wfm[:, k, chunk * 128:(chunk + 1) * 128], h[:, k, 0:nt], k == 0, k == 7, wt + [t_h], [t_pb])
            return pb, t_pb

        def gates_a(h, t_h, nt, hd):
            pb, t_pb = proj_fm(h, t_h, 4 + hd, nt, [t_w["hf"]])
            f, t_f = nxt(fbs, "fb")
            ACT(f[:, 0:nt], pb[:, 0:nt], AF.Sigmoid, [t_pb], [t_f])
            ACT(f[:, 0:nt], f[:, 0:nt], AF.Identity, [t_f, t_lb], [t_f], scale=omlb[:, hd:hd + 1], bias=lb[:, hd:hd + 1])
            kk, t_kk = nxt(kbs, "kb")
            TS("pool", kk[:, 0:nt], f[:, 0:nt], -1.0, 1.0, ALU.mult, ALU.add, [t_f], [t_kk])
            return f, t_f, kk, t_kk

        def gates_b(fk, nt, samp, hd, Pt, t_P, ke, t_ke, ke2, t_ke2):
            f, t_f, kk, t_kk = fk
            rp, t_rp = nxt(rPs, "rP")
            bb, t_bb = nxt(bbs, "bb")
            ACT(rp[:, 0:nt], f[:, 0:nt], AF.Ln, [t_f], [t_rp])
            if not samp:
                S.add("dve", lambda e, rp=rp, bb=bb: e.tensor_tensor_scan(
                    out=bb[:, 0:nt], data0=rmask[:, 0:nt], data1=rp[:, 0:nt],
                    initial=0.0, op0=ALU.mult, op1=ALU.add), [t_rp, t_rmask], [t_bb])
            else:
                lv = rp[:, 0:nt].rearrange("p (s t) -> p s t", t=4)
                bv = bb[:, 0:nt].rearrange("p (s t) -> p s t", t=4)
                CP("dve", bv[:, :, 0:1], lv[:, :, 0:1], [t_rp], [t_bb])
                for j in range(1, 4):
                    TT("dve", bv[:, :, j:j + 1], bv[:, :, j - 1:j], lv[:, :, j:j + 1], ALU.add, [t_rp, t_bb], [t_bb])
            ACT(Pt[:, hd, 0:nt], bb[:, 0:nt], AF.Exp, [t_bb], [t_P[hd]])
            ACT(rp[:, 0:nt], bb[:, 0:nt], AF.Exp, [t_bb], [t_rp], scale=-1.0)
            TT("pool", ke[:, hd, 0:nt], kk[:, 0:nt], rp[:, 0:nt], ALU.mult, [t_kk, t_rp], [t_ke[hd]])
            if not samp:
                ncq = nt // 64
                plb = Pt[:, hd, 0:nt].rearrange("p (c t) -> p c t", t=64)[:, :, 63:64].broadcast_to([128, ncq, 64])
                TT("dve", ke2[:, hd, 0:nt].rearrange("p (c t) -> p c t", t=64), ke[:, hd, 0:nt].rearrange("p (c t) -> p c t", t=64),
                   plb, ALU.mult, [t_ke[hd], t_P[hd]], [t_ke2])

        def proj_tm(h, t_h, tcol, ntk):
            pw, t_pw = PW()
            for k in range(8):
                MM(pw[0:ntk, 0:512], h[:, k, tcol:tcol + ntk], wtm[:, k, 0:512], k == 0, k == 7, [t_w["hi"], t_h], [t_pw])
            vt, t_vt = nxt(vts, "vt")
            CP("dve", vt[0:ntk, 0:512], pw[0:ntk, 0:512], [t_pw], [t_vt])
            return vt, t_vt

        def proj_sv(h, t_h, tcol, ntk, tile_idx, want32):
            pn, t_pn = PN()
            for k in range(8):
                MM(pn[0:ntk, 0:128], h[:, k, tcol:tcol + ntk], wtm[:, k, 512:640], k == 0, k == 7, [t_w["sv"], t_h], [t_pn])
            CP("dve", vaug[0:ntk, tile_idx, :, 0:64], pn[0:ntk, 0:128].rearrange("p (j c) -> p j c", j=2), [t_pn], [t_vaug[tile_idx]])
            v32 = None
            if want32:
                v32t, t_v32 = nxt(v32s, "v32")
                CP("act", v32t[0:ntk, :], pn[0:ntk, 0:128], [t_pn], [t_v32])
                v32 = (v32t, t_v32)
            return v32

        def state_U(ket, t_ket, vt, t_vt):
            pus = []
            for c in range(2):
                pu, t_pu = PW()
                for hd in range(4):
                    MM(pu[:, hd * 128:(hd + 1) * 128], ket[c * 64:(c + 1) * 64, hd, :], vt[c * 64:(c + 1) * 64, hd * 128:(hd + 1) * 128],
                       True, True, [t_ket, t_vt], [t_pu])
                pus.append((pu, t_pu))
            return pus

        def state_chain(c, pus, Pt, t_P, tcol, cast):
            n = chunk_ctr["n"]
            chunk_ctr["n"] = n + 1
            pu, t_pu = pus[c]
            for hd in range(4):
                col = tcol + c * 64 + 63
                STT(S32[:, hd, :], S32[:, hd, :], Pt[:, hd, col:col + 1], pu[:, hd * 128:(hd + 1) * 128], ALU.mult, ALU.add,
                    [t_S32[hd], t_P[hd], t_pu], [t_S32[hd]])
            if cast:
                p_ = (n + 1) % 2
                CP("dve", Sbf2[p_][:, :, :], S32[:, :, :], t_S32, t_Sbf2[p_])

        def ke_transpose(ke, t_ke, tcol, ntk):
            pw, t_pw = PW()
            pwb = pw[:].bitcast(BF16)
            for hd in range(4):
                TR(pwb[0:ntk, hd * 128:(hd + 1) * 128], ke[:, hd, tcol:tcol + ntk], id16[:],
                   [t_ke[hd] if isinstance(t_ke, list) else t_ke, t_id16], [t_pw])
            ket, t_ket = nxt(kets, "ket")
            CP("dve", ket[0:ntk, :, :], pwb[0:ntk, 0:512].rearrange("p (h d) -> p h d", h=4), [t_pw], [t_ket])
            return ket, t_ket

        def rotary_chunk(h, t_h, ch_a, ch_b, wa, cst, t_cs, c0, nt, out_ap, out_toks, hsl=None):
            hv = h if hsl is None else h[:, :, hsl[0]:hsl[1]]
            pa, t_pa = proj_fm(hv, t_h, ch_a, nt, [t_w[wa]])
            zb, t_zb = nxt(zbs, "zb")
            CP("dve", zb[:, 0:nt], pa[:, 0:nt], [t_pa], [t_zb])
            pb, t_pb = PN()
            MM(pb[:, 0:nt], perm16[:], zb[:, 0:nt], True, True, [t_perm, t_zb], [t_pb])
            a, t_a = nxt(rt1, "rt1")
            b, t_b = nxt(rt2, "rt2")
            TT("dve", a[:, 0:nt], pa[:, 0:nt], cst[:, 0, c0:c0 + nt], ALU.mult, [t_pa, t_cs], [t_a])
            TT("dve", b[:, 0:nt], pb[:, 0:nt], cst[:, 1, c0:c0 + nt], ALU.mult, [t_pb, t_cs], [t_b])
            if isinstance(out_ap, tuple):
                TT("pool", out_ap[0], a[0:64, 0:nt], b[0:64, 0:nt], ALU.add, [t_a, t_b], out_toks)
                TT("pool", out_ap[1], a[64:128, 0:nt], b[64:128, 0:nt], ALU.add, [t_a, t_b], out_toks)
            else:
                TT("pool", out_ap, a[:, 0:nt], b[:, 0:nt], ALU.add, [t_a, t_b], out_toks)

        def load_cs(col0, nt):
            cst, t_cs = nxt(css, "cs")
            DMA("sp", cst[:, :, 0:nt], cs_d[:, :, col0:col0 + nt], [], [t_cs], "cs%d" % (rr["cs"] % 2))
            return cst, t_cs

        def k_chunk(h, t_h, cst, t_cs, nt, kcol0, tiles, j, hsl=None):
            kr, t_kr = kr32[0]
            rotary_chunk(h, t_h, 16 + j, 22 + j, "k", cst, t_cs, 0, nt, kr[:, j, 0:nt], [t_kr], hsl)
            CP("pool", k2t[:, j, kcol0:kcol0 + nt], kr[:, j, 0:nt], [t_kr], [t_k2t[t] for t in tiles])
            return kr, t_kr

        def attention(qr, t_qr, qcol, nq, blocks, obT, t_obT, ocol, po_bank=None, po_bank2=None, preloaded=None):
            ob, _ = nxt(obt, "obt")
            t_obh = t_ob_all[(rr["obt"] - 1) % 2]
            den, _ = nxt(dens, "den")
            t_den2 = t_den_all[(rr["den"] - 1) % 2]
            nb = len(blocks)
            pos = [po_bank if po_bank is not None else plong]
            if po_bank2 is not None:
                pos.append(po_bank2)
                items = [(j, bi) for bi in range(nb) for j in range(2)]
            else:
                pos.append(pos[0])
                items = [(j, bi) for j in range(2) for bi in range(nb)]
            loaded = dict(preloaded or {})

            def ensure_loaded(idx):
                if idx < len(items):
                    b0 = items[idx][1]
                    for bi in range(b0, min(nb, b0 + 3)):
                        if bi not in loaded:
                            loaded[bi] = blocks[bi][0]()

            def scores(idx):
                j, bi = items[idx]
                _, mask_ap, nk = blocks[bi]
                kaps, vaps, btoks = loaded[bi]
                ps_, t_ps = PW()
                MM(ps_[0:nk, 0:4 * nq].rearrange("p (g q) -> p g q", g=4), id16[0:nk, 0:nk], mask_ap, True, False,
                   [t_id16, t_negm, t_negsc, t_negsn], [t_ps], skip=True)
                MM(ps_[0:nk, 0:4 * nq].rearrange("p (g q) -> p g q", g=4), kaps[j],
                   qr[:, 2 * j:2 * j + 2, :, qcol:qcol + nq].rearrange("p c h q -> p (c h) q"),
                   False, True, btoks + [t_qr], [t_ps], skip=True)
                pT, t_pT = nxt(pTs, "pT")
                ACT(pT[0:nk, 0:4 * nq], ps_[0:nk, 0:4 * nq], AF.Exp, [t_ps], [t_pT], scale=0.125)
                return pT, t_pT, vaps[j], btoks, nk

            ensure_loaded(0)
            nxt_s = scores(0)
            yield
            for idx, (j, bi) in enumerate(items):
                pT, t_pT, vap, btoks, nk = nxt_s
                if idx + 1 < len(items):
                    nxt_s = scores(idx + 1)
                po, t_po = pos[j]
                pov = po[0:nq, 0:260].rearrange("p (g c) -> p g c", c=65)
                for gq in range(4):
                    MM(po[0:nq, gq * 65:(gq + 1) * 65], pT[0:nk, gq * nq:(gq + 1) * nq], vap, bi == 0 and gq == 0, bi == nb - 1,
                       [t_pT] + btoks, [t_po], skip=True)
                ensure_loaded(idx + 1)
                if bi == nb - 1:
                    TT("dve", den[0:nq, 4 * j:4 * j + 4], pov[:, :, 64], esink[0:nq, 4 * j:4 * j + 4], ALU.add, [t_po, t_esink], [t_den2[j]])
                    S.add("dve", lambda e, j=j, den=den: e.reciprocal(out=den[0:nq, 8 + 4 * j:12 + 4 * j], in_=den[0:nq, 4 * j:4 * j + 4]),
                          [t_den2[j]], [t_den2[j]])
                    TT("dve", ob[0:nq, 4 * j * 64:(4 * j + 4) * 64].rearrange("p (g c) -> p g c", c=64), pov[:, :, 0:64],
                       den[0:nq, 8 + 4 * j:12 + 4 * j].unsqueeze(2).broadcast_to([nq, 4, 64]), ALU.mult,
                       [t_po, t_den2[j]], [t_obh[4 * j + g_] for g_ in range(4)])
                yield
            pw, t_pw = PW()
            pwb = pw[:].bitcast(BF16)
            for c in range(4):
                TR(pwb[:, c * nq:(c + 1) * nq], ob[0:nq, c * 128:(c + 1) * 128], id16[0:nq, 0:nq], [t_obh[2 * c], t_obh[2 * c + 1], t_id16], [t_pw])
            CP("act", obT[:, :, ocol:ocol + nq], pwb[:, 0:4 * nq].rearrange("p (c q) -> p c q", c=4), [t_pw], [t_obT])
            yield

        def hgrn_out(o32, t_o32, osq, t_osq, sg, t_sg, og, t_og, tcol, ntk, pso, t_pso):
            pov = pso[:, 0:4 * ntk].rearrange("p (h t) -> p h t", h=4)
            ACT(o32[:, :, 0:ntk], pov, AF.Copy, [t_pso], [t_o32])
            ACT(osq[:, :, 0:ntk], pov, AF.Square, [t_pso], [t_osq])
            yield
            pss, t_pss = PW()
            MM(pss[:, 0:4 * ntk].rearrange("p (h t) -> p h t", h=4), ones16[:], osq[:, :, 0:ntk], True, True, [t_ones, t_osq], [t_pss])
            r, t_r = nxt(rso, "rso")
            rstd_from(pss[:, 0:4 * ntk], t_pss, r[:, 0:4 * ntk], t_r, 1.0 / 128, 4 * ntk)
            TT("dve", o32[:, :, 0:ntk], o32[:, :, 0:ntk], r[:, 0:4 * ntk].rearrange("p (h t) -> p h t", h=4), ALU.mult, [t_o32, t_r], [t_o32])
            TT("pool", og[:, :, tcol:tcol + ntk], o32[:, :, 0:ntk], sg[:, :, tcol:tcol + ntk], ALU.mult, [t_o32, t_sg], [t_og])

        def out_kv(kr, t_kr, v32, tcol, ntk, samp):
            v32t, t_v32 = v32
            pw, t_pw = PW()
            for j in range(2):
                TR(pw[0:ntk, j * 128:(j + 1) * 128], kr[:, j, tcol:tcol + ntk], id32[:], [t_kr, t_id32], [t_pw])
            stk, t_stk = st_k[0]
            CP("dve", stk[0:ntk, :].rearrange("p (j c) -> p j c", j=2), pw[0:ntk, 0:256].rearrange("p (j c) -> p j c", j=2)[:, :, 0:64],
               [t_pw], [t_stk])
            if not samp:
                DMA("sp", nk_d, stk[:, :], [t_stk], [], "o_nk")
                DMA("sp", nv_d, v32t[:, :], [t_v32], [], "o_nv")
            else:
                for s in range(NSEQ):
                    DMA("sp", nks_d[s, 124:128, :], stk[4 * s:4 * s + 4, :], [t_stk], [], "o_nk")
                    DMA("sp", nvs_d[s, 124:128, :], v32t[4 * s:4 * s + 4, :], [t_v32], [], "o_nv")

        def sample_states(pso, t_pso, qd, t_qd, ket, t_ket, vt, t_vt, Pt, t_P, deferred=()):
            deferred = list(deferred)
            sfl = dict(sample_pre["sf"])

            def load_s(s_):
                if s_ < NSEQ and s_ not in sfl:
                    sf_, t_sf_ = nxt(s0s, "s0f")
                    DMA("sp", sf_[:], s0_d[s_].rearrange("h d e -> d h e"), [], [t_sf_], "s0f%d" % (rr["s0f"] % 3))
                    sfl[s_] = (sf_, t_sf_)
            load_s(0)
            load_s(1)
            for s in range(NSEQ):
                if deferred and s % 3 == 2:
                    deferred.pop(0)()
                sf, t_sf = sfl.pop(s)
                for hd in range(4):
                    MM(pso[:, hd * 64 + s * 4:hd * 64 + s * 4 + 4], sf[:, hd, :], qd32[0][:, hd, s * 4:s * 4 + 4], False, s == NSEQ - 1,
                       [t_sf, qd32[1]], [t_pso], skip=True)
                km, t_km = nxt(kms, "km")
                TS("dve", km[:, :, :], ket[0:64, :, :], ind[:, s:s + 1], None, ALU.mult, ALU.bypass, [t_ket, t_ind], [t_km])
                pu, t_pu = PW()
                for hd in range(4):
                    MM(pu[:, hd * 128:(hd + 1) * 128], km[:, hd, :], vt[0:64, hd * 128:(hd + 1) * 128], True, True, [t_km, t_vt], [t_pu])
                sn, t_sn = nxt(sns, "sn")
                for hd in range(4):
                    pl = Pt[:, hd, s * 4 + 3:s * 4 + 4]
                    ACT(sn[:, hd, :], pu[:, hd * 128:(hd + 1) * 128], AF.Copy, [t_pu, t_P[hd]], [t_sn], scale=pl)
                    STT(sn[:, hd, :], sf[:, hd, :], pl, sn[:, hd, :], ALU.mult, ALU.add, [t_sf, t_P[hd], t_sn], [t_sn])
                DMA("act", nss_d[s].rearrange("h d e -> d h e"), sn[:], [t_sn], [], "o_nss%d" % (rr["sn"] % 2))
                load_s(s + 2)
                yield
            while deferred:
                deferred.pop(0)()

        sample_pre = {"kv": {}, "sf": {}}

        def sample_preload():
            for s_ in range(3):
                sample_pre["kv"][s_] = cached_loader(s_)()
            for s_ in range(2):
                sf_, t_sf_ = nxt(s0s, "s0f")
                DMA("sp", sf_[:], s0_d[s_].rearrange("h d e -> d h e"), [], [t_sf_], "s0f%d" % (rr["s0f"] % 3))
                sample_pre["sf"][s_] = (sf_, t_sf_)

        def cached_loader(s):
            def load():
                kc, t_kc = nxt(kcs, "kc")
                vc, t_vc = nxt(vcs, "vc")
                key = "kc%d" % (rr["kc"] % 3)
                for hf_ in range(2):
                    DMA("pool", kc[hf_ * 64:(hf_ + 1) * 64, :, :], ckT_d[s].rearrange("j c k -> c j k"), [], [t_kc], key)
                DMA("pool", vc[:, :, 0:64], cv_d[s].rearrange("k (j c) -> k j c", j=2), [], [t_vc], "vc%d" % (rr["vc"] % 3))
                return [kc[:, j_, :] for j_ in range(2)], [vc[:, j_, :] for j_ in range(2)], [t_kc, t_vc]
            return load

        def sample_attention(qr, t_qr, obT, t_obT, po_bank, po_bank2):
            blocks = []

            def cached_loader_unused(s):
                def load():
                    kc, t_kc = nxt(kcs, "kc")
                    vc, t_vc = nxt(vcs, "vc")
                    key = "kc%d" % (rr["kc"] % 3)
                    for hf_ in range(2):
                        DMA("pool", kc[hf_ * 64:(hf_ + 1) * 64, :, :], ckT_d[s].rearrange("j c k -> c j k"), [], [t_kc], key)
                    DMA("pool", vc[:, :, 0:64], cv_d[s].rearrange("k (j c) -> k j c", j=2), [], [t_vc], "vc%d" % (rr["vc"] % 3))
                    return [kc[:, j_, :] for j_ in range(2)], [vc[:, j_, :] for j_ in range(2)], [t_kc, t_vc]
                return load
            for s in range(NSEQ):
                blocks.append((cached_loader(s), negsc[:, s, :].unsqueeze(1).broadcast_to([128, 4, 64]), 128))
                DMA("sp", nks_d[s, 0:124, :], ck_d[s, 4:128, :], [], [], "o_nkc")
                DMA("sp", nvs_d[s, 0:124, :], cv_d[s, 4:128, :], [], [], "o_nkc")
            blocks.append((lambda: ([k2t[:, j_, 128 + NMAIN:128 + NMAIN + 64] for j_ in range(2)], [vaug[0:64, 17, j_, :] for j_ in range(2)],
                                    [t_k2t[17], t_vaug[17]]), negsn[:, :, :], 64))
            yield from attention(qr, t_qr, 0, 64, blocks, obT, t_obT, 0, po_bank, po_bank2, sample_pre["kv"])

        NG = NPRE // G
        NGM = NMAIN // G
        jobs = [("pre", g) for g in range(NG)] + [("main", g) for g in range(NGM)] + [("samp", 0)]

        def job_x(job):
            kind, g = job
            if kind == "pre":
                return (g * G, G)
            if kind == "main":
                return (NPRE + g * G, G)
            return (NPRE + NMAIN, NSAMP)

        def stage1(ji, ctx):
            kind, gi = jobs[ji]
            samp = kind == "samp"
            nt = NSAMP if samp else G
            xt, t_x = xq.pop(0)
            h, t_h = norm_h(xt, t_x, nt, 0)
            if ji + 1 < len(jobs):
                xq.append(load_x(xT, *job_x(jobs[ji + 1]), []))
            ctx.update(h=h, t_h=t_h, nt=nt, samp=samp, kind=kind, gi=gi)
            if samp:
                sample_preload()
            yield
            Pt, _ = nxt(Pbs, "Pb")
            ke, _ = nxt(kes, "ke")
            t_P = t_P_all[(rr["Pb"] - 1) % 2]
            t_ke = t_ke_all[(rr["ke"] - 1) % 2]
            ke2, t_ke2 = nxt(ke2s, "ke2")
            ctx.update(Pt=Pt, t_P=t_P, ke=ke, t_ke=t_ke, ke2=ke2, t_ke2=t_ke2)
            fks = []
            sg = None
            if kind != "pre":
                sg, t_sg = nxt(sgs, "sg")
            for hd in range(4):
                fks.append(gates_a(h, t_h, nt, hd))
            if kind != "pre":
                for hd in range(4):
                    pg, t_pg = proj_fm(h, t_h, 8 + hd, nt, [t_w["hg"]])
                    sgt, t_sgt = nxt(sgts, "sgt")
                    ACT(sgt[:, 0:nt], pg[:, 0:nt], AF.Sigmoid, [t_pg], [t_sgt])
                    STT(sg[:, hd, 0:nt], pg[:, 0:nt], hgnw[:, hd:hd + 1], sgt[:, 0:nt], ALU.mult, ALU.mult, [t_pg, t_hgnw, t_sgt], [t_sg])
            yield
            for hd in range(4):
                gates_b(fks[hd], nt, samp, hd, Pt, t_P, ke, t_ke, ke2, t_ke2)
                yield
            if kind == "pre":
                if gi == NG - 1:
                    cst, t_cs = load_cs(0, 128)
                    for j in range(2):
                        k_chunk(h, t_h, cst, t_cs, 128, 0, [0], j, hsl=(G - 128, G))
                        yield
                    proj_sv(h, t_h, G - 128, 128, 0, False)
                    yield
                return
            cscol0 = 128 + (NMAIN if samp else gi * G)
            cst, t_cs = load_cs(cscol0, nt)
            qd, t_qd = nxt(qds, "qd")
            for hd in range(4):
                pb, t_pb = proj_fm(h, t_h, hd, nt, [t_w["hq"]])
                TT("dve", qd[:, hd, 0:nt], pb[:, 0:nt], Pt[:, hd, 0:nt], ALU.mult, [t_pb, t_P[hd]], [t_qd])
                if samp:
                    TT("dve", qd32[0][:, hd, 0:nt], pb[:, 0:nt], Pt[:, hd, 0:nt], ALU.mult, [t_pb, t_P[hd]], [qd32[1]])
                if hd % 2 == 1:
                    yield
            qr, t_qr = nxt(qrs, "qr")
            for c in range(4):
                rotary_chunk(h, t_h, 12 + c, 18 + c, "sq", cst, t_cs, 0, nt, (qr[0:64, c, 0, 0:nt], qr[64:128, c, 1, 0:nt]), [t_qr])
                yield
            tiles = [17] if samp else [1 + 2 * gi, 2 + 2 * gi]
            for j in range(2):
                kr, t_kr = k_chunk(h, t_h, cst, t_cs, nt, cscol0, tiles, j)
                yield
            v32 = None
            lastg_ = (not samp) and gi == NGM - 1
            for t in range(1 if samp else G // 128):
                ntk = nt if samp else 128
                w32 = samp or (lastg_ and t == G // 128 - 1)
                r_ = proj_sv(h, t_h, t * 128, ntk, tiles[t], w32)
                if r_ is not None:
                    v32 = r_
                yield
            ctx.update(qd=qd, t_qd=t_qd, sg=sg, t_sg=t_sg, qr=qr, t_qr=t_qr, kr=kr, t_kr=t_kr, tiles=tiles, cscol0=cscol0, v32=v32)

        po_att = (banks[7], nt_[3])
        narrow[:] = [e for e in narrow if e[1] is not nt_[3]]

        def stage2h(ctx):
            kind, gi, samp, nt = ctx["kind"], ctx["gi"], ctx["samp"], ctx["nt"]
            h, t_h, Pt, t_P, ke, t_ke = ctx["h"], ctx["t_h"], ctx["Pt"], ctx["t_P"], ctx["ke"], ctx["t_ke"]
            if kind == "pre":
                for t in range(G // 128):
                    halo = (gi == NG - 1) and t == G // 128 - 1
                    vt, t_vt = proj_tm(h, t_h, t * 128, 128)
                    ket, t_ket = ke_transpose(ctx["ke2"], ctx["t_ke2"], t * 128, 128)
                    yield
                    pus = state_U(ket, t_ket, vt, t_vt)
                    for c in range(2):
                        state_chain(c, pus, Pt, t_P, t * 128, halo and c == 1)
                    yield
                if gi == NG - 1:
                    assert chunk_ctr["n"] % 2 == 0
                return
            if ABL == "nohg" and not samp:
                return
            qd, t_qd, sg, t_sg = ctx["qd"], ctx["t_qd"], ctx["sg"], ctx["t_sg"]
            ocol0 = NMAIN if samp else gi * G
            lastg = (not samp) and gi == NGM - 1
            if samp:
                A2v = Arena(big, BIGN)
                wg_v = A2v.alloc([128, 8, 2048], BF16)
                wua_v = A2v.alloc([128, 4, 1024], BF16)
                wub_v = A2v.alloc([128, 4, 1024], BF16)
                assert A2v.off <= 8 * NFM
                wdead = [t_w[n_] for n_ in ("hq", "hf", "hg", "sq", "k", "rot")]
                w2pre = [
                    lambda: DMA("pool", wg_v[:, :, 0:1024], w_in_v[:, :, 2816:3840], [], wdead, "w2pre"),
                    lambda: DMA("pool", wua_v, w_ua.rearrange("(k p) c -> p k c", p=128), [], wdead, "w2pre"),
                    lambda: DMA("pool", wub_v, w_ub.rearrange("(k p) c -> p k c", p=128), [], wdead, "w2pre"),
                    lambda: DMA("pool", wg_v[:, :, 1024:2048], w_in_v[:, :, 3840:4864], [], wdead, "w2pre"),
                ]
            og, t_og = nxt(ogs, "og")
            ntile = 1 if samp else G // 128
            for t in range(ntile):
                ntk = nt if samp else 128
                tcol = t * 128
                vt, t_vt = proj_tm(h, t_h, tcol, ntk)
                yield
                ket, t_ket = ke_transpose(ke if samp else ctx["ke2"], t_ke if samp else ctx["t_ke2"], tcol, ntk)
                pa, t_pa = PW()
                for hd in range(4):
                    MM(pa[0:ntk, hd * ntk:(hd + 1) * ntk], ke[:, hd, tcol:tcol + ntk], qd[:, hd, tcol:tcol + ntk], True, True, [t_ke[hd], t_qd], [t_pa])
                am, t_am = nxt(ams, "am")
                msk = mhgs[:, :, :] if samp else mhg[:, :, :]
                TT("dve", am[0:ntk, :, 0:ntk], pa[0:ntk, 0:4 * ntk].rearrange("p (h t) -> p h t", h=4), msk, ALU.mult,
                   [t_pa, t_mhg, t_mhgs], [t_am])
                yield
                if not samp:
                    pus = state_U(ket, t_ket, vt, t_vt)
                pso, t_pso = plong
                for hd in range(4):
                    MM(pso[:, hd * ntk:(hd + 1) * ntk], vt[0:ntk, hd * 128:(hd + 1) * 128], am[0:ntk, hd, 0:ntk], hd == 0, False, [t_vt, t_am], [t_pso], skip=True)
                if not samp:
                    for c in range(2):
                        p_ = chunk_ctr["n"] % 2
                        for hd in range(4):
                            MM(pso[:, hd * ntk + c * 64:hd * ntk + (c + 1) * 64], Sbf2[p_][:, hd, :], qd[:, hd, tcol + c * 64:tcol + (c + 1) * 64],
                               False, True, [t_Sbf2[p_][hd], t_qd], [t_pso], skip=True)
                        state_chain(c, pus, Pt, t_P, tcol, True)
                    yield
                else:
                    yield from sample_states(pso, t_pso, qd, t_qd, ket, t_ket, vt, t_vt, Pt, t_P, w2pre)
                o32, t_o32 = nxt(o32s, "o32")
                osq, t_osq = nxt(osqs, "osq")
                yield from hgrn_out(o32, t_o32, osq, t_osq, sg, t_sg, og, t_og, tcol, ntk, pso, t_pso)
                yield
            DMA("sp", ogs_d[:, ocol0:ocol0 + nt].rearrange("(k p) n -> p k n", p=128), og[:, :, 0:nt], [t_og], [], "st_og%d" % (rr["og"] % 2))
            if lastg:
                DMA("sp", ns_d.rearrange("h d e -> d h e"), S32[:], t_S32, [], "o_ns")

        def stage2a(ctx):
            kind, gi, samp, nt = ctx["kind"], ctx["gi"], ctx["samp"], ctx["nt"]
            if kind == "pre" or (ABL == "noatt" and not samp):
                return
            qr, t_qr, kr, t_kr, tiles, cscol0 = ctx["qr"], ctx["t_qr"], ctx["kr"], ctx["t_kr"], ctx["tiles"], ctx["cscol0"]
            ocol0 = NMAIN if samp else gi * G
            lastg = (not samp) and gi == NGM - 1
            obT, t_obT = nxt(obTs, "obT")
            ntile = 1 if samp else G // 128
            for t in range(ntile):
                ntk = nt if samp else 128
                tcol = t * 128
                tidx = tiles[t]
                want32 = samp or (lastg and t == ntile - 1)
                if not samp:
                    kcol = cscol0 + tcol

                    def mk(kc, ti, mi):
                        return (lambda kc=kc, ti=ti: ([k2t[:, j_, kc:kc + 128] for j_ in range(2)], [vaug[:, ti, j_, :] for j_ in range(2)],
                                                      [t_k2t[ti], t_vaug[ti]]),
                                negm[:, mi, :, :], 128)
                    blocks = [mk(kcol - 128, tidx - 1, 2 if tidx == 1 else 1), mk(kcol, tidx, 0)]
                    yield from attention(qr, t_qr, tcol, 128, blocks, obT, t_obT, tcol, po_att)
                else:
                    po_att2 = (banks[6], nt_[2])
                    narrow[:] = [e for e in narrow if e[1] is not nt_[2]]
                    yield from sample_attention(qr, t_qr, obT, t_obT, po_att, po_att2)
                if want32:
                    out_kv(kr, t_kr, ctx["v32"], tcol, ntk, samp)
                    yield
            DMA("sp", obs_d[:, ocol0:ocol0 + nt].rearrange("(k p) n -> p k n", p=128), obT[:, :, 0:nt], [t_obT], [], "st_ob%d" % (rr["obT"] % 2))

        step_counts = {}

        def interleave(named_gens, speed=None):
            gens = [(nm, g) for nm, g in named_gens if g is not None]
            done = {nm: 0 for nm, _ in gens}
            tot = {nm: (counts or {}).get(nm, 1) / (speed or {}).get(nm.split("_")[0], 1.0) for nm, _ in gens}
            live = dict(gens)
            while live:
                nm = min(live, key=lambda n: done[n] / max(tot[n], 1))
                try:
                    next(live[nm])
                    done[nm] += 1
                except StopIteration:
                    del live[nm]
            for nm, _ in gens:
                step_counts[nm] = done[nm]

        def run_pipelined():
            ctxs = [dict() for _ in jobs]
            xq.append(load_x(xT, *job_x(jobs[0]), []))
            for _ in stage1(0, ctxs[0]):
                pass
            for ji in range(len(jobs)):
                g2h = stage2h(ctxs[ji])
                g2a = stage2a(ctxs[ji])
                g1 = stage1(ji + 1, ctxs[ji + 1]) if ji + 1 < len(jobs) else None
                interleave([("a1s2h_%d" % ji, g2h), ("a1s2a_%d" % ji, g2a), ("a1s1_%d" % (ji + 1), g1)], speed={"a1s1": 0.85, "a1s2h": 0.7})
                if stop == "pre" and jobs[ji] == ("pre", NG - 1):
                    return True
            return False

        xq = []
        if run_pipelined():
            finish()
            return nc
        if stop == "A1":
            finish()
            return nc

        S.barrier()
        A.reset()
        rr.clear()
        narrow[:] = narrow_all
        wg = A.alloc([128, 8, 2048], BF16)
        wua = A.alloc([128, 4, 1024], BF16)
        wub = A.alloc([128, 4, 1024], BF16)
        wo = A.alloc([128, 8, 1024], BF16)
        t_w2 = {n: Tok("w2_" + n) for n in ("g", "ua", "ub", "wo")}
        DMA("pool", wo, w_o.rearrange("(k p) c -> p k c", p=128), [], [t_w2["wo"]], "w2_wo")
        xs = slots("xs", [128, 8, G], F32, 2)
        xsq = slots("xsq", [128, G], BF16, 2)
        rstd = slots("rstd", [128, G], F32, 2)
        hTs = slots("hT", [128, 8, G], BF16, 2)
        og2 = slots("og2", [128, 4, G], BF16, 2)
        ob2 = slots("ob2", [128, 4, G], BF16, 2)
        sga = slots("sga", [128, G], F32, 2)
        sgb = slots("sgb", [128, G], F32, 2)
        mt1 = slots("mt1", [128, G], F32, 2)
        mt2 = slots("mt2", [128, G], F32, 2)
        mTs = slots("mT", [128, 8, G], BF16, 2)
        t_mT_all = [[Tok("mT%d_%d" % (s_, i)) for i in range(8)] for s_ in range(2)]
        x1o = slots("x1o", [128, G], F32, 3)
        assert A.off <= BIGN - 8 * 4096, ("phase A2 working set overlaps the wf1 prefetch region", A.off)
        wf1_pre = big[:, BIGN - 8 * 4096:BIGN].rearrange("p (k c) -> p k c", k=8)
        f1v = w_f1.rearrange("(k p) c -> p k c", p=128)
        for q in range(4):
            DMA("pool", wf1_pre[:, :, q * 1024:(q + 1) * 1024], f1v[:, :, q * 1024:(q + 1) * 1024], [], [], "wB%d" % q)

        def load_a2(col0, nt):
            xt, t_x = load_x(xT, NPRE + col0, nt, [])
            og, t_og = nxt(og2, "og2")
            ob, t_ob = nxt(ob2, "ob2")
            DMA("sp", og[:, :, 0:nt], ogs_d[:, col0:col0 + nt].rearrange("(k p) n -> p k n", p=128), [], [t_og], "og2%d" % (rr["og2"] % 2))
            DMA("sp", ob[:, :, 0:nt], obs_d[:, col0:col0 + nt].rearrange("(k p) n -> p k n", p=128), [], [t_ob], "ob2%d" % (rr["ob2"] % 2))
            return (xt, t_x, og, t_og, ob, t_ob)

        def a2_s1(ld, nt, res):
            xt, t_x = ld[0], ld[1]
            yield from norm_h_gen(xt, t_x, nt, 0, res)

        def a2_s2(ld, res, nt, ocol0, nxt_ld, out):
            xt, t_x, og, t_og, obT, t_obT = ld
            h, t_h = res["h"], res["t_h"]
            mT, _ = nxt(mTs, "mT")
            t_mTc = t_mT_all[(rr["mT"] - 1) % 2]
            for jc in range(8):
                pga, t_pga = PN()
                for k in range(8):
                    MM(pga[:, 0:nt], wg[:, k, jc * 128:(jc + 1) * 128], h[:, k, 0:nt], k == 0, k == 7, [t_w2["g"], t_h], [t_pga])
                a, t_a = nxt(sga, "sga")
                ACT(a[:, 0:nt], pga[:, 0:nt], AF.Sigmoid, [t_pga], [t_a])
                pgb, t_pgb = PN()
                for k in range(8):
                    MM(pgb[:, 0:nt], wg[:, k, 1024 + jc * 128:1024 + (jc + 1) * 128], h[:, k, 0:nt], k == 0, k == 7, [t_w2["g"], t_h], [t_pgb])
                b, t_b = nxt(sgb, "sgb")
                ACT(b[:, 0:nt], pgb[:, 0:nt], AF.Sigmoid, [t_pgb], [t_b])
                pya, t_pya = PN()
                for k in range(4):
                    MM(pya[:, 0:nt], wua[:, k, jc * 128:(jc + 1) * 128], og[:, k, 0:nt], k == 0, k == 3, [t_w2["ua"], t_og], [t_pya])
                pyb, t_pyb = PN()
                for k in range(4):
                    MM(pyb[:, 0:nt], wub[:, k, jc * 128:(jc + 1) * 128], obT[:, k, 0:nt], k == 0, k == 3, [t_w2["ub"], t_obT], [t_pyb])
                m1, t_m1 = nxt(mt1, "mt1")
                m2, t_m2 = nxt(mt2, "mt2")
                TT("dve", m1[:, 0:nt], pya[:, 0:nt], a[:, 0:nt], ALU.mult, [t_pya, t_a], [t_m1])
                TT("dve", m2[:, 0:nt], pyb[:, 0:nt], b[:, 0:nt], ALU.mult, [t_pyb, t_b], [t_m2])
                TT("pool", mT[:, jc, 0:nt], m1[:, 0:nt], m2[:, 0:nt], ALU.add, [t_m1, t_m2], [t_mTc[jc]])
                yield
            for jc in range(8):
                px, t_px = PN()
                for k in range(8):
                    MM(px[:, 0:nt], wo[:, k, jc * 128:(jc + 1) * 128], mT[:, k, 0:nt], k == 0, k == 7, [t_w2["wo"], t_mTc[k]], [t_px])
                xo, t_xo = nxt(x1o, "x1o")
                TT("dve", xo[:, 0:nt], px[:, 0:nt], xt[:, jc, 0:nt], ALU.add, [t_px, t_x], [t_xo])
                DMA("sp", x1s[jc * 128:(jc + 1) * 128, ocol0:ocol0 + nt], xo[:, 0:nt], [t_xo], [], "x1o%d" % (rr["x1o"] % 3))
                yield
            out["ld"] = load_a2(*nxt_ld) if nxt_ld is not None else None

        a2_jobs = [(gi * G, G) for gi in range(NGM)] + [(NMAIN, NSAMP)]
        lds = {0: load_a2(*a2_jobs[0])}
        if len(a2_jobs) > 1:
            lds[1] = load_a2(*a2_jobs[1])
        ress = [dict() for _ in a2_jobs]
        for _ in a2_s1(lds[0], a2_jobs[0][1], ress[0]):
            pass
        for ji in range(len(a2_jobs)):
            out = {}
            nl = a2_jobs[ji + 2] if ji + 2 < len(a2_jobs) else None
            g2 = a2_s2(lds[ji], ress[ji], a2_jobs[ji][1], a2_jobs[ji][0], nl, out)
            g1 = a2_s1(lds[ji + 1], a2_jobs[ji + 1][1], ress[ji + 1]) if ji + 1 < len(a2_jobs) else None
            interleave([("a2s2_%d" % ji, g2), ("a2s1_%d" % (ji + 1), g1)], speed={"a2s1": 0.6})
            if out.get("ld") is not None:
                lds[ji + 2] = out["ld"]
        if stop == "A2":
            finish()
            return nc

        S.barrier()
        A.reset()
        rr.clear()
        wf2 = A.alloc([128, 32, 1024], BF16)
        t_wB = [Tok("wB%d" % i) for i in range(8)]
        f2v = w_f2.rearrange("(k p) c -> p k c", p=128)
        for q in range(4):
            DMA("pool", wf2[:, q * 8:(q + 1) * 8, :], f2v[:, q * 8:(q + 1) * 8, :], [], [t_wB[4 + q]], "wB%d" % (4 + q))
        xs = slots("xs", [128, 8, G], F32, 2)
        xsq = slots("xsq", [128, G], BF16, 4)
        rstd = slots("rstd", [128, G], F32, 2)
        hTs = slots("hT", [128, 8, G], BF16, 2)
        aTs = slots("aT", [128, 32, G], BF16, 1)
        t_aTc = [Tok("aT%d" % i) for i in range(32)]
        rl = slots("rl", [128, G], F32, 3)
        x2s = slots("x2", [128, 8, G], F32, 1)
        t_x2c = [Tok("x2_%d" % i) for i in range(8)]
        yts = slots("yt", [128, 1024], F32, 2)
        assert A.off <= BIGN - 8 * 4096, ("phase B working set overlaps wf1", A.off)
        print("B arena bytes", A.off * 2)
        wf1 = wf1_pre

        plong2 = wide[2]
        del wide[2]

        def b_s1(ld, nt, res):
            yield from norm_h_gen(ld[0], ld[1], nt, 8, res)

        def b_s2(ld, res, nt, nxt_ld, out):
            xt, t_x = ld
            h, t_h = res["h"], res["t_h"]
            aT, _ = aTs[0]
            for c in range(32):
                pb, t_pb = PN()
                for k in range(8):
                    MM(pb[:, 0:nt], wf1[:, k, c * 128:(c + 1) * 128], h[:, k, 0:nt], k == 0, k == 7, [t_wB[c // 8], t_h], [t_pb])
                r, t_r = nxt(rl, "rl")
                ACT(r[:, 0:nt], pb[:, 0:nt], AF.Relu, [t_pb], [t_r])
                TT("pool" if c % 2 else "dve", aT[:, c, 0:nt], r[:, 0:nt], r[:, 0:nt], ALU.mult, [t_r], [t_aTc[c]])
                yield
            x2, _ = x2s[0]
            pss, t_pss = plong2
            pend = None
            for jc in range(8):
                pb, t_pb = PN()
                for k in range(32):
                    MM(pb[:, 0:nt], wf2[:, k, jc * 128:(jc + 1) * 128], aT[:, k, 0:nt], k == 0, k == 31, [t_wB[4 + k // 8], t_aTc[k]], [t_pb])
                TT("dve", x2[:, jc, 0:nt], pb[:, 0:nt], xt[:, jc, 0:nt], ALU.add, [t_pb, t_x], [t_x2c[jc]])
                q, t_q = nxt(xsq, "xsq")
                ACT(q[:, 0:nt], x2[:, jc, 0:nt], AF.Square, [t_x2c[jc]], [t_q])
                if pend is not None:
                    MM(pss[:, 0:nt], ones16[:], pend[0][:, 0:nt], pend[2] == 0, False, [t_ones, pend[1]], [t_pss])
                pend = (q, t_q, jc)
                yield
            MM(pss[:, 0:nt], ones16[:], pend[0][:, 0:nt], False, True, [t_ones, pend[1]], [t_pss])
            out["ld"] = load_x(x1s, nxt_ld[0], nxt_ld[1], []) if nxt_ld is not None else None

        def b_s3(col0, nt):
            x2, _ = x2s[0]
            pss, t_pss = plong2
            r, t_r = nxt(rstd, "rstd")
            rstd_from(pss[:, 0:nt], t_pss, r[:, 0:nt], t_r, 1.0 / D, nt)
            yield
            for jc in range(8):
                STT(x2[:, jc, 0:nt], x2[:, jc, 0:nt], ncol[:, 16 + jc:17 + jc], r[:, 0:nt], ALU.mult, ALU.mult, [t_x2c[jc], t_ncol, t_r], [t_x2c[jc]])
                if jc % 4 == 3:
                    yield
            for t in range((nt + 127) // 128):
                ntk = min(128, nt - t * 128)
                yt, t_yt = nxt(yts, "yt")
                for half in range(2):
                    pw, t_pw = PW()
                    for jj in range(4):
                        jc = half * 4 + jj
                        TR(pw[0:ntk, jj * 128:(jj + 1) * 128], x2[:, jc, t * 128:t * 128 + ntk], id32[:], [t_x2c[jc], t_id32], [t_pw])
                    CP("act" if half else "dve", yt[0:ntk, half * 512:(half + 1) * 512], pw[0:ntk, 0:512], [t_pw], [t_yt])
                    yield
                DMA("sp", y_d[col0 + t * 128:col0 + t * 128 + ntk, :], yt[0:ntk, :], [t_yt], [], "o_y%d" % (rr["yt"] % 2))

        b_jobs = [(gi * G, G) for gi in range(NGM)] + [(NMAIN, NSAMP)]
        ldsB = {0: load_x(x1s, b_jobs[0][0], b_jobs[0][1], []), 1: load_x(x1s, b_jobs[1][0], b_jobs[1][1], [])}
        resB = [dict() for _ in b_jobs]
        for _ in b_s1(ldsB[0], b_jobs[0][1], resB[0]):
            pass
        for ji in range(len(b_jobs)):
            out = {}
            nl = b_jobs[ji + 2] if ji + 2 < len(b_jobs) else None
            g2 = b_s2(ldsB[ji], resB[ji], b_jobs[ji][1], nl, out)
            g1 = b_s1(ldsB[ji + 1], b_jobs[ji + 1][1], resB[ji + 1]) if ji + 1 < len(b_jobs) else None
            g3 = b_s3(b_jobs[ji - 1][0], b_jobs[ji - 1][1]) if ji > 0 else None
            interleave([("bs2_%d" % ji, g2), ("bs1_%d" % (ji + 1), g1), ("bs3_%d" % (ji - 1), g3)], speed={"bs3": 0.35, "bs1": 0.6})
            if out.get("ld") is not None:
                ldsB[ji + 2] = out["ld"]
        for _ in b_s3(b_jobs[-1][0], b_jobs[-1][1]):
            pass

        finish()
    if count_only:
        return step_counts
    return nc


def build_two_pass(stop=None):
    counts = build_program(None, None, True)
    return build_program(stop, counts, False)


_NC_CACHE = {}


def _consts():
    half = 8
    inv = np.exp(-math.log(500000.0) * np.arange(half, dtype=np.float32) * np.float32(2.0 / 16)).astype(np.float32)
    s = np.arange(128)
    m_hg = ((s[:, None] <= s[None, :]) & ((s[:, None] // 64) == (s[None, :] // 64))).astype(np.float32)
    s64 = np.arange(64)
    m_hg_s = ((s64[:, None] <= s64[None, :]) & ((s64[:, None] // 4) == (s64[None, :] // 4))).astype(np.float32)
    own = np.where(s[:, None] <= s[None, :], 0.0, NEG).astype(np.float32)
    prev = np.where(s[:, None] > s[None, :], 0.0, NEG).astype(np.float32)
    allneg = np.full((128, 128), NEG, np.float32)
    negsc = np.full((128, NSEQ, 64), NEG, np.float32)
    for q in range(64):
        sq, tq = q // 4, q % 4
        negsc[tq + 1:, sq, q] = 0.0
    negsn = np.where(((s64[:, None] // 4) == (s64[None, :] // 4)) & (s64[:, None] <= s64[None, :]), 0.0, NEG).astype(np.float32)
    ind = (s64[:, None] // 4 == np.arange(NSEQ)[None, :]).astype(np.float32)
    return inv, m_hg, m_hg_s, own, prev, allneg, negsc, negsn, ind


def _perm():
    p = np.zeros((128, 128), np.float32)
    for d in range(128):
        i = d % 64
        s = d + 8 if i < 8 else (d - 8 if i < 16 else d)
        p[s, d] = 1.0
    return p


def _cs_table(pos, inv):
    ang = pos.astype(np.float32)[None, :] * inv[:, None]
    c = np.cos(ang.astype(np.float64)).astype(np.float32)
    sn = np.sin(ang.astype(np.float64)).astype(np.float32)
    n = pos.shape[0]
    C = np.ones((64, n), np.float32)
    Sg = np.zeros((64, n), np.float32)
    C[0:8] = c
    C[8:16] = c
    Sg[0:8] = -sn
    Sg[8:16] = sn
    out = np.stack([np.concatenate([C, C], 0), np.concatenate([Sg, Sg], 0)], axis=1)
    return np.ascontiguousarray(out)


def prepare(x_prompt, x_sample, cache_swa_k, cache_swa_v, state_hgrn, w_in, hgrn_lb_logits, hgrn_norm_w,
            sinks, w_up_a, w_up_b, w_o, norm1_w, norm2_w, w_ff1, w_ff2, normf_w, cores=range(8)):
    f = lambda a: np.ascontiguousarray(np.asarray(a, dtype=np.float32))
    x_prompt, x_sample = f(x_prompt), f(x_sample)
    ck_all, cv_all, s0_all = f(cache_swa_k)[0], f(cache_swa_v)[0], f(state_hgrn)[0]
    inv, m_hg, m_hg_s, own, prev, allneg, negsc, negsn, ind = _consts()
    col = lambda v: np.ascontiguousarray(f(v).reshape(8, 128).T)
    ncols = np.concatenate([col(norm1_w[0]), col(norm2_w[0]), col(normf_w)], axis=1)
    lbl = np.ascontiguousarray(f(hgrn_lb_logits).reshape(2, 4, 128).transpose(2, 0, 1).reshape(128, 8))
    hgnw = np.ascontiguousarray(f(hgrn_norm_w)[0].T)
    shared = {
        "w_in": f(w_in)[0], "w_up_a": f(w_up_a)[0], "w_up_b": f(w_up_b)[0], "w_o": f(w_o)[0],
        "w_ff1": f(w_ff1)[0], "w_ff2": f(w_ff2)[0], "lbl": lbl, "hgnw": hgnw, "ncols": ncols,
        "sinks": f(sinks).reshape(1, 8), "m_hg": m_hg, "m_hg_s": m_hg_s, "negm_sc": negsc, "negm_sn": negsn,
        "ind16": ind, "ident": np.eye(128, dtype=np.float32), "perm": _perm(),
    }
    in_maps = []
    for c in cores:
        b, hf = c // 2, c % 2
        xT = np.zeros((D, NX), np.float32)
        if hf == 1:
            xT[:, 0:NPRE] = x_prompt[b, 0:2048].T
        xT[:, NPRE:NPRE + NMAIN] = x_prompt[b, hf * 2048:(hf + 1) * 2048].T
        xT[:, NPRE + NMAIN:] = x_sample[16 * c:16 * c + 16].reshape(64, D).T
        base = hf * 2048
        pos = np.concatenate([np.arange(base - 128, base + 2048), PAST_LEN + np.tile(np.arange(4), NSEQ)])
        negm = np.stack([own, prev, prev if hf == 1 else allneg], axis=1)
        m = dict(shared)
        m["xT"] = xT
        m["cs"] = _cs_table(pos, inv)
        m["negm"] = np.ascontiguousarray(negm)
        ck = ck_all[16 * c:16 * c + 16]
        m["ckT"] = np.ascontiguousarray(ck.transpose(0, 2, 3, 1))
        m["ck"] = np.ascontiguousarray(ck.reshape(16, 128, 128))
        m["cv"] = np.ascontiguousarray(cv_all[16 * c:16 * c + 16].reshape(16, 128, 128))
        m["s0"] = np.ascontiguousarray(s0_all[16 * c:16 * c + 16])
        in_maps.append(m)
    return in_maps


def assemble(R):
    y_prompt = np.zeros((4, 4096, D), np.float32)
    y_sample = np.zeros((128, 4, D), np.float32)
    nkp = np.zeros((1, 4, 128, 2, 64), np.float32)
    nvp = np.zeros((1, 4, 128, 2, 64), np.float32)
    nsp = np.zeros((1, 4, 4, 128, 128), np.float32)
    nks = np.zeros((1, 128, 128, 2, 64), np.float32)
    nvs = np.zeros((1, 128, 128, 2, 64), np.float32)
    nss = np.zeros((1, 128, 4, 128, 128), np.float32)
    for c in range(8):
        b, hf = c // 2, c % 2
        r = R[c]
        y_prompt[b, hf * 2048:(hf + 1) * 2048] = r["y"][0:2048]
        y_sample[16 * c:16 * c + 16] = r["y"][2048:2112].reshape(16, 4, D)
        if hf == 1:
            nkp[0, b] = r["nk"].reshape(128, 2, 64)
            nvp[0, b] = r["nv"].reshape(128, 2, 64)
            nsp[0, b] = r["ns"]
        nks[0, 16 * c:16 * c + 16] = r["nks"].reshape(16, 128, 2, 64)
        nvs[0, 16 * c:16 * c + 16] = r["nvs"].reshape(16, 128, 2, 64)
        nss[0, 16 * c:16 * c + 16] = r["nss"]
    return (y_prompt, y_sample, nkp, nvp, nsp, nks, nvs, nss)


def kernel(**inputs):
    in_maps = prepare(**inputs)
    if "nc" not in _NC_CACHE:
        _NC_CACHE["nc"] = build_two_pass()
    nc = _NC_CACHE["nc"]
    res = run_bass_kernel_spmd(nc, in_maps, core_ids=list(range(8)))
    return assemble(res.results)
```

```python
import math
import numpy as np
from contextlib import ExitStack
import concourse.bass as bass
import concourse.mybir as mybir
from concourse.bass_utils import run_bass_kernel_spmd

F32 = mybir.dt.float32
BF16 = mybir.dt.bfloat16
AF = mybir.ActivationFunctionType
ALU = mybir.AluOpType

ENGS = ("pe", "act", "dve", "pool", "sp")
LABELS = None

D = 1024
NPRE = 2048
NMAIN = 2048
NSAMP = 64
NX = NPRE + NMAIN + NSAMP
G = 256
NCS = 128 + NMAIN + NSAMP
NSEQ = 16
EPS = 1e-6
NEG = -30000.0
PAST_LEN = 16384


class Tok:
    __slots__ = ("w", "r", "name", "excl")

    def __init__(self, name="", excl=False):
        self.w = None
        self.r = []
        self.name = name
        self.excl = excl


class Op:
    __slots__ = ("eng", "fn", "deps", "needs", "cnt", "key", "is_dma", "waits")

    def __init__(self, eng, fn, is_dma, key):
        self.eng = eng
        self.fn = fn
        self.is_dma = is_dma
        self.key = key
        self.deps = []
        self.needs = False
        self.cnt = None
        self.waits = None


class Sched:
    def __init__(self, nc, same_engine_sync=("act", "pool", "dve")):
        self.nc = nc
        self.ops = {e: [] for e in ENGS}
        self.dma_cnt = {}
        self.same = set(same_engine_sync)
        self.skip_same_waw = True
        self.final_waits = []
        self.last_dma = {}

    def barrier(self):
        lasts = []
        for e in ENGS:
            comp = [o for o in self.ops[e] if (not o.is_dma) and o.fn is not None]
            if comp:
                lasts.append(comp[-1])
        lasts += list(self.last_dma.values())
        for e in ENGS:
            op = Op(e, None, False, None)
            op.deps = [d for d in lasts]
            for d in lasts:
                d.needs = True
            self.ops[e].append(op)

    def add(self, eng, fn, reads=(), writes=(), dma_key=None):
        op = Op(eng, fn, dma_key is not None, dma_key)
        if LABELS is not None:
            import sys as _s
            f = _s._getframe(2)
            lab = []
            for _ in range(3):
                if f is None:
                    break
                lab.append("%s:%d" % (f.f_code.co_name, f.f_lineno))
                f = f.f_back
            LABELS.setdefault(eng, []).append("/".join(lab))
        writes = list(writes) + [t for t in reads if t.excl]
        reads = [t for t in reads if not t.excl]
        deps = {}
        for t in reads:
            if t.w is not None:
                deps[id(t.w)] = t.w
        for t in writes:
            if t.w is not None:
                if not (self.skip_same_waw and eng in ("act", "dve") and (not t.w.is_dma) and t.w.eng == eng
                        and dma_key is None and not t.excl):
                    deps[id(t.w)] = t.w
            for r in t.r:
                deps[id(r)] = r
        dl = []
        for d in deps.values():
            if d is op:
                continue
            if (not d.is_dma) and d.eng == eng and eng not in self.same:
                continue
            dl.append(d)
            d.needs = True
        op.deps = dl
        for t in reads:
            t.r.append(op)
        for t in writes:
            t.w = op
            t.r = []
        if op.is_dma:
            c = self.dma_cnt.get(dma_key, 0) + 16
            self.dma_cnt[dma_key] = c
            op.cnt = c
            self.last_dma[dma_key] = op
        self.ops[eng].append(op)
        return op

    def finalize_and_emit(self, es):
        nc = self.nc
        esem = {e: es.enter_context(nc.semaphore("c_" + e)) for e in ENGS}
        dsem = {k: es.enter_context(nc.semaphore("d_%d" % i)) for i, k in enumerate(self.dma_cnt)}
        for e in ENGS:
            c = 0
            for op in self.ops[e]:
                if not op.is_dma and op.needs and op.fn is not None:
                    c += 1
                    op.cnt = c
        for e in ENGS:
            waited = {}
            for op in self.ops[e]:
                w = {}
                for d in op.deps:
                    key = ("d", d.key) if d.is_dma else ("e", d.eng)
                    if d.cnt > w.get(key, 0):
                        w[key] = d.cnt
                ws = []
                for key, v in w.items():
                    if waited.get(key, 0) >= v:
                        continue
                    waited[key] = v
                    ws.append((dsem[key[1]] if key[0] == "d" else esem[key[1]], v))
                op.waits = ws
        final = [(dsem[k], self.dma_cnt[k]) for k in self.final_waits]
        block = es.enter_context(nc.Block())

        def emit(e, eng):
            for op in self.ops[e]:
                for (sm, v) in op.waits:
                    eng.wait_ge(sm, v)
                if op.fn is None:
                    continue
                ins = op.fn(eng)
                if op.is_dma:
                    ins.then_inc(dsem[op.key], 16)
                elif op.needs:
                    ins.then_inc(esem[e], 1)
            if e == "sp":
                for (sm, v) in final:
                    eng.wait_ge(sm, v)

        @block.tensor
        def _(eng):
            emit("pe", eng)

        @block.scalar
        def _(eng):
            emit("act", eng)

        @block.vector
        def _(eng):
            emit("dve", eng)

        @block.gpsimd
        def _(eng):
            emit("pool", eng)

        @block.sync
        def _(eng):
            emit("sp", eng)


class Arena:
    def __init__(self, big, nelem):
        self.big = big
        self.n = nelem
        self.off = 0

    def reset(self):
        self.off = 0

    def alloc(self, shape, dt):
        n = 1
        for s in shape[1:]:
            n *= s
        nb = n * 2 if dt == F32 else n
        nb = (nb + 1) // 2 * 2
        assert self.off + nb <= self.n, ("arena overflow", self.off, nb, self.n)
        v = self.big[0:shape[0], self.off:self.off + nb]
        self.off += nb
        if dt == F32:
            v = v.bitcast(F32)
        v = v[:, 0:n]
        if len(shape) == 3:
            v = v.rearrange("p (a b) -> p a b", a=shape[1])
        elif len(shape) == 4:
            v = v.rearrange("p (a b c) -> p a b c", a=shape[1], b=shape[2])
        return v


class _Stop(Exception):
    pass


ABL = ""


def build_program(stop=None, counts=None, count_only=False):
    nc = bass.Bass("TRN2", target_bir_lowering=False)

    def din(name, shape):
        return nc.dram_tensor(name, list(shape), F32, kind="ExternalInput").ap()

    def dout(name, shape):
        return nc.dram_tensor(name, list(shape), F32, kind="ExternalOutput").ap()

    xT = din("xT", [D, NX])
    cs_d = din("cs", [128, 2, NCS])
    w_in = din("w_in", [D, 4864])
    w_ua = din("w_up_a", [512, D])
    w_ub = din("w_up_b", [512, D])
    w_o = din("w_o", [D, D])
    w_f1 = din("w_ff1", [D, 4096])
    w_f2 = din("w_ff2", [4096, D])
    lbl_d = din("lbl", [128, 8])
    hgnw_d = din("hgnw", [128, 4])
    ncol_d = din("ncols", [128, 24])
    sinks_d = din("sinks", [1, 8])
    mhg_d = din("m_hg", [128, 128])
    mhgs_d = din("m_hg_s", [64, 64])
    negm_d = din("negm", [128, 3, 128])
    negsc_d = din("negm_sc", [128, NSEQ, 64])
    negsn_d = din("negm_sn", [64, 64])
    ind_d = din("ind16", [64, NSEQ])
    ident_d = din("ident", [128, 128])
    perm_d = din("perm", [128, 128])
    ckT_d = din("ckT", [NSEQ, 2, 64, 128])
    ck_d = din("ck", [NSEQ, 128, 128])
    cv_d = din("cv", [NSEQ, 128, 128])
    s0_d = din("s0", [NSEQ, 4, 128, 128])

    NTO = NMAIN + NSAMP
    y_d = dout("y", [NTO, D])
    nk_d = dout("nk", [128, 128])
    nv_d = dout("nv", [128, 128])
    ns_d = dout("ns", [4, 128, 128])
    nks_d = dout("nks", [NSEQ, 128, 128])
    nvs_d = dout("nvs", [NSEQ, 128, 128])
    nss_d = dout("nss", [NSEQ, 4, 128, 128])
    x1s = nc.dram_tensor("x1s", [D, NTO], F32, kind="Internal").ap()
    ogs_d = nc.dram_tensor("ogs", [512, NTO], BF16, kind="Internal").ap()
    obs_d = nc.dram_tensor("obs", [512, NTO], BF16, kind="Internal").ap()

    es = ExitStack()
    with es:
        S = Sched(nc)

        def sb(name, shape, dt=F32):
            return es.enter_context(nc.sbuf_tensor("s_" + name, list(shape), dt))

        def MM(out, lhsT, rhs, st, sp, R, W, skip=False):
            S.add("pe", lambda e: e.matmul(out, lhsT=lhsT, rhs=rhs, start=st, stop=sp, skip_group_check=skip), R, W)

        def TR(out, in_, idn, R, W):
            S.add("pe", lambda e: e.transpose(out, in_, idn), R, W)

        def ACT(out, in_, func, R, W, scale=1.0, bias=0.0):
            S.add("act", lambda e: e.activation(out=out, in_=in_, func=func, bias=bias, scale=scale), R, W)

        def TS(eng, out, in0, s1, s2, op0, op1, R, W):
            S.add(eng, lambda e: e.tensor_scalar(out=out, in0=in0, scalar1=s1, scalar2=s2, op0=op0, op1=op1), R, W)

        def TT(eng, out, in0, in1, op, R, W):
            S.add(eng, lambda e: e.tensor_tensor(out=out, in0=in0, in1=in1, op=op), R, W)

        def STT(out, in0, scalar, in1, op0, op1, R, W):
            S.add("dve", lambda e: e.scalar_tensor_tensor(out=out, in0=in0, scalar=scalar, in1=in1, op0=op0, op1=op1), R, W)

        def CP(eng, out, in_, R, W):
            if eng == "act":
                ACT(out, in_, AF.Copy, R, W)
            else:
                S.add(eng, lambda e: e.tensor_copy(out=out, in_=in_), R, W)

        def DMA(eng, out, in_, R, W, key):
            S.add(eng, lambda e: e.dma_start(out=out, in_=in_), R, W, dma_key=key)

        def MEMSET(eng, ap, val, W):
            S.add(eng, lambda e: e.memset(ap, val), (), W)

        def finish():
            if count_only:
                return
            S.final_waits = [k for k in S.dma_cnt if k.startswith("o_")]
            print("ops", {e: len(S.ops[e]) for e in ENGS}, "dma keys", len(S.dma_cnt))
            S.finalize_and_emit(es)

        banks = [es.enter_context(nc.psum_tensor("psb%d" % i, [128, 512], F32)) for i in range(8)]
        wide = [(banks[i], Tok("pw%d" % i, True)) for i in range(3)]
        plong = (banks[3], Tok("plong", True))
        nt_ = [Tok("pn%d" % i, True) for i in range(4, 8)]
        narrow = [(banks[i][:, 0:256], nt_[i - 4]) for i in range(4, 8)] + [(banks[i][:, 256:512], nt_[i - 4]) for i in range(4, 8)]
        narrow_all = list(narrow)
        ctr = {"w": 0, "n": 0}

        def PW():
            b, t = wide[ctr["w"] % len(wide)]
            ctr["w"] += 1
            return b, t

        def PN():
            a, t = narrow[ctr["n"] % len(narrow)]
            ctr["n"] += 1
            return a, t

        id32 = sb("id32", [128, 128]); t_id32 = Tok()
        id16 = sb("id16", [128, 128], BF16); t_id16 = Tok()
        ones16 = sb("ones16", [128, 128], BF16); t_ones = Tok()
        ones16w = sb("ones16w", [128, 256], BF16)
        perm16 = sb("perm16", [128, 128], BF16); t_perm = Tok()
        lbl = sb("lbl", [128, 8]); lb = sb("lb", [128, 4]); omlb = sb("omlb", [128, 4]); t_lb = Tok()
        hgnw = sb("hgnw", [128, 4]); t_hgnw = Tok()
        ncol = sb("ncol", [128, 24]); t_ncol = Tok()
        esink = sb("esink", [128, 8]); t_esink = Tok()
        mhg = sb("mhg", [128, 4, 128], BF16); t_mhg = Tok()
        mhgs = sb("mhgs", [64, 4, 64], BF16); t_mhgs = Tok()
        negm = sb("negm", [128, 3, 4, 128], BF16); t_negm = Tok()
        negsc = sb("negsc", [128, NSEQ, 64], BF16); t_negsc = Tok()
        negsn = sb("negsn", [64, 4, 64], BF16); t_negsn = Tok()
        ind = sb("ind", [64, NSEQ]); t_ind = Tok()
        zeros = sb("zeros", [128, 64]); t_zeros = Tok()
        rmask = sb("rmask", [128, G]); t_rmask = Tok()
        epsc = sb("epsc", [128, 1]); t_epsc = Tok()

        DMA("sp", id32[:], ident_d, [], [t_id32], "id32")
        DMA("pool", id16[:], ident_d, [], [t_id16], "id16")
        DMA("pool", perm16[:], perm_d, [], [t_perm], "perm16")
        MEMSET("pool", ones16[:], 1.0, [t_ones])
        MEMSET("pool", ones16w[:], 1.0, [t_ones])
        MEMSET("pool", zeros[:], 0.0, [t_zeros])
        MEMSET("pool", rmask[:], 1.0, [t_rmask])
        MEMSET("pool", rmask[:, :].rearrange("p (c t) -> p c t", t=64)[:, :, 0:1], 0.0, [t_rmask])
        MEMSET("pool", epsc[:], EPS, [t_epsc])
        DMA("sp", lbl[:], lbl_d, [], [t_lb], "lbl")
        DMA("sp", hgnw[:], hgnw_d, [], [t_hgnw], "hgnw")
        DMA("sp", ncol[:], ncol_d, [], [t_ncol], "ncol")
        DMA("sp", esink[:], sinks_d.partition_broadcast(128), [], [t_esink], "esink")
        DMA("sp", ind[:], ind_d, [], [t_ind], "ind")
        TT("dve", lb[:], lbl[:, 0:4], lbl[:, 4:8], ALU.subtract, [t_lb], [t_lb])
        ACT(lb[:], lb[:], AF.Sigmoid, [t_lb], [t_lb])
        TS("dve", omlb[:], lb[:], -1.0, 1.0, ALU.mult, ALU.add, [t_lb], [t_lb])
        ACT(esink[:], esink[:], AF.Exp, [t_esink], [t_esink])

        if stop == "consts":
            finish()
            return nc
        BIGN = 98304
        big = sb("big", [128, BIGN], BF16)
        A = Arena(big, BIGN)
        rr = {}

        def slots(name, shape, dt, n):
            return [(A.alloc(shape, dt), Tok("%s%d" % (name, i))) for i in range(n)]

        def nxt(pool, name):
            i = rr.get(name, 0)
            rr[name] = i + 1
            return pool[i % len(pool)]

        def rstd_from(ps_ap, ps_tok, out_ap, out_tok, scale, n):
            ACT(out_ap, ps_ap, AF.Ln, [ps_tok, t_epsc], [out_tok], scale=scale, bias=epsc[:, 0:1])
            ACT(out_ap, out_ap, AF.Exp, [out_tok], [out_tok], scale=-0.5)

        w_in_v = w_in.rearrange("(k p) c -> p k c", p=128)

        NFM = 24 * 128
        wfm = A.alloc([128, 8, NFM], BF16)
        wtm = A.alloc([128, 8, 640], BF16)
        t_w = {n: Tok("w_" + n) for n in ("hf", "hi", "sv", "k", "hq", "hg", "sq", "rot")}
        WALL = list(t_w.values())

        def wload(dst, src, name):
            DMA("pool", dst, src, [], [t_w[name]], "w_" + name)

        wload(wfm[:, :, 4 * 128:8 * 128], w_in_v[:, :, 512:1024], "hf")
        wload(wtm[:, :, 0:512], w_in_v[:, :, 1024:1536], "hi")
        wload(wtm[:, :, 512:640], w_in_v[:, :, 2688:2816], "sv")
        kd = wfm[:, :, 16 * 128:18 * 128].rearrange("p k (j h c) -> p k j h c", j=2, h=2)
        ksrc = w_in_v[:, :, 2560:2688].rearrange("p k (j c) -> p k j c", j=2)
        for h in range(2):
            for j in range(2):
                wload(kd[:, :, j, h, :], ksrc[:, :, j, :], "k")
        for g in range(4):
            DMA("pool", mhg[:, g, :], mhg_d, [], [t_mhg], "mhg")
            DMA("pool", mhgs[:, g, :], mhgs_d, [], [t_mhgs], "mhgs")
            DMA("pool", negm[:, :, g, :], negm_d, [], [t_negm], "negm")
            DMA("pool", negsn[:, g, :], negsn_d, [], [t_negsn], "negsn")
        DMA("pool", negsc[:], negsc_d, [], [t_negsc], "negsc")
        wload(wfm[:, :, 0:4 * 128], w_in_v[:, :, 0:512], "hq")
        wload(wfm[:, :, 8 * 128:12 * 128], w_in_v[:, :, 1536:2048], "hg")
        wload(wfm[:, :, 12 * 128:16 * 128], w_in_v[:, :, 2048:2560], "sq")
        def emit_rot_copies():
            for (src0, dst0, nh, nm) in ((12 * 128, 18 * 128, 8, "sq"), (16 * 128, 22 * 128, 4, "k")):
                sv_ = wfm[:, :, src0:src0 + nh * 64].rearrange("p k (h c) -> p k h c", c=64)
                dv_ = wfm[:, :, dst0:dst0 + nh * 64].rearrange("p k (h c) -> p k h c", c=64)
                for k in range(8):
                    CP("pool", dv_[:, k, :, 0:8], sv_[:, k, :, 8:16], [t_w[nm]], [t_w["rot"]])
                    CP("pool", dv_[:, k, :, 8:16], sv_[:, k, :, 0:8], [t_w[nm]], [t_w["rot"]])
                    CP("pool", dv_[:, k, :, 16:64], sv_[:, k, :, 16:64], [t_w[nm]], [t_w["rot"]])

        if stop == "wA1":
            finish()
            return nc
        NKC = 128 + NMAIN + NSAMP
        k2t = A.alloc([128, 2, NKC], BF16)
        t_k2t = [Tok("k2t%d" % i) for i in range(18)]
        vaug = A.alloc([128, 18, 2, 65], BF16)
        t_vaug = [Tok("vaug%d" % i) for i in range(18)]
        MEMSET("pool", vaug[:], 1.0, t_vaug)
        S32 = A.alloc([128, 4, 128], F32); t_S32 = [Tok() for _ in range(4)]
        Sbf2 = [A.alloc([128, 4, 128], BF16) for _ in range(2)]
        t_Sbf2 = [[Tok() for _ in range(4)] for _ in range(2)]
        MEMSET("pool", S32[:], 0.0, t_S32)
        for p_ in range(2):
            MEMSET("pool", Sbf2[p_][:], 0.0, t_Sbf2[p_])
        chunk_ctr = {"n": 0}

        xs = slots("xs", [128, 8, G], F32, 1)
        xsq = slots("xsq", [128, G], BF16, 2)
        rstd = slots("rstd", [128, G], F32, 2)
        hTs = slots("hT", [128, 8, G], BF16, 2)
        fbs = slots("fb", [128, G], F32, 4)
        kbs = slots("kb", [128, G], F32, 4)
        Pbs = slots("Pb", [128, 4, G], F32, 2)
        t_P_all = [[Tok("P%d_%d" % (s_, i)) for i in range(4)] for s_ in range(2)]
        t_ke_all = [[Tok("ke%d_%d" % (s_, i)) for i in range(4)] for s_ in range(2)]
        t_den_all = [[Tok("den%d_%d" % (s_, i)) for i in range(2)] for s_ in range(2)]
        rPs = slots("rP", [128, G], F32, 2)
        bbs = slots("bb", [128, G], F32, 2)
        qds = slots("qd", [128, 4, G], BF16, 2)
        kes = slots("ke", [128, 4, G], BF16, 2)
        ke2s = slots("ke2", [128, 4, G], BF16, 2)
        sgs = slots("sg", [128, 4, G], BF16, 2)
        sgts = slots("sgt", [128, G], BF16, 1)
        vts = slots("vt", [128, 512], BF16, 2)
        v32s = slots("v32", [128, 128], F32, 2)
        kets = slots("ket", [128, 4, 128], BF16, 2)
        ams = slots("am", [128, 4, 128], BF16, 1)
        o32s = slots("o32", [128, 4, 128], F32, 2)
        osqs = slots("osq", [128, 4, 128], BF16, 1)
        rso = slots("rso", [128, 512], F32, 1)
        ogs = slots("og", [128, 4, G], BF16, 2)
        css = slots("cs", [128, 2, G], F32, 2)
        zbs = slots("zb", [128, G], BF16, 2)
        rt1 = slots("rt1", [128, G], F32, 2)
        rt2 = slots("rt2", [128, G], F32, 2)
        kr32 = slots("kr32", [128, 2, G], F32, 1)
        qrs = slots("qr", [128, 4, 2, G], BF16, 2)
        for q_ in qrs:
            MEMSET("pool", q_[0][:], 0.0, [q_[1]])
        pTs = slots("pT", [128, 512], BF16, 3)
        dens = slots("den", [128, 16], F32, 2)
        obt = slots("obt", [128, 512], BF16, 2)
        t_ob_all = [[Tok("ob%d_%d" % (s_, i)) for i in range(8)] for s_ in range(2)]
        obTs = slots("obT", [128, 4, G], BF16, 2)
        st_k = slots("stk", [128, 128], F32, 1)
        s0s = slots("s0f", [128, 4, 128], F32, 3)
        qd32 = (A.alloc([128, 4, NSAMP], F32), Tok("qd32"))
        sns = slots("sn", [128, 4, 128], F32, 2)
        kms = slots("km", [64, 4, 128], BF16, 2)
        kcs = slots("kc", [128, 2, 128], BF16, 3)
        vcs = slots("vc", [128, 2, 65], BF16, 3)
        for v_ in vcs:
            MEMSET("pool", v_[0][:], 1.0, [v_[1]])
        print("A1 arena bytes", A.off * 2)

        def load_x(src, col0, nt, R):
            xt, t_x = nxt(xs, "xs")
            DMA("sp", xt[:, :, 0:nt], src[:, col0:col0 + nt].rearrange("(k p) n -> p k n", p=128), R, [t_x], "xs%d" % (rr["xs"] % len(xs)))
            return xt, t_x

        def norm_h(xt, t_x, nt, ncolbase):
            pb, t_pb = PN()
            for k in range(8):
                q, t_q = nxt(xsq, "xsq")
                ACT(q[:, 0:nt], xt[:, k, 0:nt], AF.Square, [t_x], [t_q])
                MM(pb[:, 0:nt], ones16[:], q[:, 0:nt], k == 0, k == 7, [t_ones, t_q], [t_pb])
            r, t_r = nxt(rstd, "rstd")
            rstd_from(pb[:, 0:nt], t_pb, r[:, 0:nt], t_r, 1.0 / D, nt)
            h, t_h = nxt(hTs, "hT")
            for k in range(8):
                STT(h[:, k, 0:nt], xt[:, k, 0:nt], ncol[:, ncolbase + k:ncolbase + k + 1], r[:, 0:nt], ALU.mult, ALU.mult,
                    [t_x, t_ncol, t_r], [t_h])
            return h, t_h

        def norm_h_gen(xt, t_x, nt, ncolbase, res):
            pb, t_pb = plong
            for k in range(8):
                q, t_q = nxt(xsq, "xsq")
                ACT(q[:, 0:nt], xt[:, k, 0:nt], AF.Square, [t_x], [t_q])
                MM(pb[:, 0:nt], ones16[:], q[:, 0:nt], k == 0, k == 7, [t_ones, t_q], [t_pb])
                if k % 2 == 1:
                    yield
            r, t_r = nxt(rstd, "rstd")
            rstd_from(pb[:, 0:nt], t_pb, r[:, 0:nt], t_r, 1.0 / D, nt)
            yield
            h, t_h = nxt(hTs, "hT")
            for k in range(8):
                STT(h[:, k, 0:nt], xt[:, k, 0:nt], ncol[:, ncolbase + k:ncolbase + k + 1], r[:, 0:nt], ALU.mult, ALU.mult,
                    [t_x, t_ncol, t_r], [t_h])
                if k % 4 == 3:
                    yield
            res["h"] = h
            res["t_h"] = t_h

        NWARM = 0

        def proj_fm(h, t_h, chunk, nt, wt):
            pb, t_pb = PN()
            for _ in range(NWARM):
                MM(pb[:, 0:256], ones16[:], ones16w[:, 0:256], True, True, [t_ones], [t_pb])
            for k in range(8):
                MM(pb[:, 0:nt], wfm[:, k, chunk * 128:(chunk + 1) * 128], h[:, k, 0:nt], k == 0, k == 7, wt + [t_h], [t_pb])
            return pb, t_pb

        def gates_a(h, t_h, nt, hd):
            pb, t_pb = proj_fm(h, t_h, 4 + hd, nt, [t_w["hf"]])
            f, t_f = nxt(fbs, "fb")
            ACT(f[:, 0:nt], pb[:, 0:nt], AF.Sigmoid, [t_pb], [t_f])
            ACT(f[:, 0:nt], f[:, 0:nt], AF.Identity, [t_f, t_lb], [t_f], scale=omlb[:, hd:hd + 1], bias=lb[:, hd:hd + 1])
            kk, t_kk = nxt(kbs, "kb")
            TS("pool", kk[:, 0:nt], f[:, 0:nt], -1.0, 1.0, ALU.mult, ALU.add, [t_f], [t_kk])
            return f, t_f, kk, t_kk

        def gates_b(fk, nt, samp, hd, Pt, t_P, ke, t_ke, ke2, t_ke2):
            f, t_f, kk, t_kk = fk
            rp, t_rp = nxt(rPs, "rP")
            bb, t_bb = nxt(bbs, "bb")
            ACT(rp[:, 0:nt], f[:, 0:nt], AF.Ln, [t_f], [t_rp])
            if not samp:
                S.add("dve", lambda e, rp=rp, bb=bb: e.tensor_tensor_scan(
                    out=bb[:, 0:nt], data0=rmask[:, 0:nt], data1=rp[:, 0:nt],
                    initial=0.0, op0=ALU.mult, op1=ALU.add), [t_rp, t_rmask], [t_bb])
            else:
                lv = rp[:, 0:nt].rearrange("p (s t) -> p s t", t=4)
                bv = bb[:, 0:nt].rearrange("p (s t) -> p s t", t=4)
                CP("dve", bv[:, :, 0:1], lv[:, :, 0:1], [t_rp], [t_bb])
                for j in range(1, 4):
                    TT("dve", bv[:, :, j:j + 1], bv[:, :, j - 1:j], lv[:, :, j:j + 1], ALU.add, [t_rp, t_bb], [t_bb])
            ACT(Pt[:, hd, 0:nt], bb[:, 0:nt], AF.Exp, [t_bb], [t_P[hd]])
            ACT(rp[:, 0:nt], bb[:, 0:nt], AF.Exp, [t_bb], [t_rp], scale=-1.0)
            TT("pool", ke[:, hd, 0:nt], kk[:, 0:nt], rp[:, 0:nt], ALU.mult, [t_kk, t_rp], [t_ke[hd]])
            if not samp:
                ncq = nt // 64
                plb = Pt[:, hd, 0:nt].rearrange("p (c t) -> p c t", t=64)[:, :, 63:64].broadcast_to([128, ncq, 64])
                TT("dve", ke2[:, hd, 0:nt].rearrange("p (c t) -> p c t", t=64), ke[:, hd, 0:nt].rearrange("p (c t) -> p c t", t=64),
                   plb, ALU.mult, [t_ke[hd], t_P[hd]], [t_ke2])

        def proj_tm(h, t_h, tcol, ntk):
            pw, t_pw = PW()
            for k in range(8):
                MM(pw[0:ntk, 0:512], h[:, k, tcol:tcol + ntk], wtm[:, k, 0:512], k == 0, k == 7, [t_w["hi"], t_h], [t_pw])
            vt, t_vt = nxt(vts, "vt")
            CP("dve", vt[0:ntk, 0:512], pw[0:ntk, 0:512], [t_pw], [t_vt])
            return vt, t_vt

        def proj_sv(h, t_h, tcol, ntk, tile_idx, want32):
            pn, t_pn = PN()
            for k in range(8):
                MM(pn[0:ntk, 0:128], h[:, k, tcol:tcol + ntk], wtm[:, k, 512:640], k == 0, k == 7, [t_w["sv"], t_h], [t_pn])
            CP("dve", vaug[0:ntk, tile_idx, :, 0:64], pn[0:ntk, 0:128].rearrange("p (j c) -> p j c", j=2), [t_pn], [t_vaug[tile_idx]])
            v32 = None
            if want32:
                v32t, t_v32 = nxt(v32s, "v32")
                CP("act", v32t[0:ntk, :], pn[0:ntk, 0:128], [t_pn], [t_v32])
                v32 = (v32t, t_v32)
            return v32

        def state_U(ket, t_ket, vt, t_vt):
            pus = []
            for c in range(2):
                pu, t_pu = PW()
                for hd in range(4):
                    MM(pu[:, hd * 128:(hd + 1) * 128], ket[c * 64:(c + 1) * 64, hd, :], vt[c * 64:(c + 1) * 64, hd * 128:(hd + 1) * 128],
                       True, True, [t_ket, t_vt], [t_pu])
                pus.append((pu, t_pu))
            return pus

        def state_chain(c, pus, Pt, t_P, tcol, cast):
            n = chunk_ctr["n"]
            chunk_ctr["n"] = n + 1
            pu, t_pu = pus[c]
            for hd in range(4):
                col = tcol + c * 64 + 63
                STT(S32[:, hd, :], S32[:, hd, :], Pt[:, hd, col:col + 1], pu[:, hd * 128:(hd + 1) * 128], ALU.mult, ALU.add,
                    [t_S32[hd], t_P[hd], t_pu], [t_S32[hd]])
            if cast:
                p_ = (n + 1) % 2
                CP("dve", Sbf2[p_][:, :, :], S32[:, :, :], t_S32, t_Sbf2[p_])

        def ke_transpose(ke, t_ke, tcol, ntk):
            pw, t_pw = PW()
            pwb = pw[:].bitcast(BF16)
            for hd in range(4):
                TR(pwb[0:ntk, hd * 128:(hd + 1) * 128], ke[:, hd, tcol:tcol + ntk], id16[:],
                   [t_ke[hd] if isinstance(t_ke, list) else t_ke, t_id16], [t_pw])
            ket, t_ket = nxt(kets, "ket")
            CP("dve", ket[0:ntk, :, :], pwb[0:ntk, 0:512].rearrange("p (h d) -> p h d", h=4), [t_pw], [t_ket])
            return ket, t_ket

        def rotary_chunk(h, t_h, ch_a, ch_b, wa, cst, t_cs, c0, nt, out_ap, out_toks, hsl=None):
            hv = h if hsl is None else h[:, :, hsl[0]:hsl[1]]
            pa, t_pa = proj_fm(hv, t_h, ch_a, nt, [t_w[wa]])
            zb, t_zb = nxt(zbs, "zb")
            CP("dve", zb[:, 0:nt], pa[:, 0:nt], [t_pa], [t_zb])
            pb, t_pb = PN()
            MM(pb[:, 0:nt], perm16[:], zb[:, 0:nt], True, True, [t_perm, t_zb], [t_pb])
            a, t_a = nxt(rt1, "rt1")
            b, t_b = nxt(rt2, "rt2")
            TT("dve", a[:, 0:nt], pa[:, 0:nt], cst[:, 0, c0:c0 + nt], ALU.mult, [t_pa, t_cs], [t_a])
            TT("dve", b[:, 0:nt], pb[:, 0:nt], cst[:, 1, c0:c0 + nt], ALU.mult, [t_pb, t_cs], [t_b])
            if isinstance(out_ap, tuple):
                TT("pool", out_ap[0], a[0:64, 0:nt], b[0:64, 0:nt], ALU.add, [t_a, t_b], out_toks)
                TT("pool", out_ap[1], a[64:128, 0:nt], b[64:128, 0:nt], ALU.add, [t_a, t_b], out_toks)
            else:
                TT("pool", out_ap, a[:, 0:nt], b[:, 0:nt], ALU.add, [t_a, t_b], out_toks)

        def load_cs(col0, nt):
            cst, t_cs = nxt(css, "cs")
            DMA("sp", cst[:, :, 0:nt], cs_d[:, :, col0:col0 + nt], [], [t_cs], "cs%d" % (rr["cs"] % 2))
            return cst, t_cs

        def k_chunk(h, t_h, cst, t_cs, nt, kcol0, tiles, j, hsl=None):
            kr, t_kr = kr32[0]
            rotary_chunk(h, t_h, 16 + j, 22 + j, "k", cst, t_cs, 0, nt, kr[:, j, 0:nt], [t_kr], hsl)
            CP("pool", k2t[:, j, kcol0:kcol0 + nt], kr[:, j, 0:nt], [t_kr], [t_k2t[t] for t in tiles])
            return kr, t_kr

        def attention(qr, t_qr, qcol, nq, blocks, obT, t_obT, ocol, po_bank=None, po_bank2=None, preloaded=None):
            ob, _ = nxt(obt, "obt")
            t_obh = t_ob_all[(rr["obt"] - 1) % 2]
            den, _ = nxt(dens, "den")
            t_den2 = t_den_all[(rr["den"] - 1) % 2]
            nb = len(blocks)
            pos = [po_bank if po_bank is not None else plong]
            if po_bank2 is not None:
                pos.append(po_bank2)
                items = [(j, bi) for bi in range(nb) for j in range(2)]
            else:
                pos.append(pos[0])
                items = [(j, bi) for j in range(2) for bi in range(nb)]
            loaded = dict(preloaded or {})

            def ensure_loaded(idx):
                if idx < len(items):
                    b0 = items[idx][1]
                    for bi in range(b0, min(nb, b0 + 3)):
                        if bi not in loaded:
                            loaded[bi] = blocks[bi][0]()

            def scores(idx):
                j, bi = items[idx]
                _, mask_ap, nk = blocks[bi]
                kaps, vaps, btoks = loaded[bi]
                ps_, t_ps = PW()
                MM(ps_[0:nk, 0:4 * nq].rearrange("p (g q) -> p g q", g=4), id16[0:nk, 0:nk], mask_ap, True, False,
                   [t_id16, t_negm, t_negsc, t_negsn], [t_ps], skip=True)
                MM(ps_[0:nk, 0:4 * nq].rearrange("p (g q) -> p g q", g=4), kaps[j],
                   qr[:, 2 * j:2 * j + 2, :, qcol:qcol + nq].rearrange("p c h q -> p (c h) q"),
                   False, True, btoks + [t_qr], [t_ps], skip=True)
                pT, t_pT = nxt(pTs, "pT")
                ACT(pT[0:nk, 0:4 * nq], ps_[0:nk, 0:4 * nq], AF.Exp, [t_ps], [t_pT], scale=0.125)
                return pT, t_pT, vaps[j], btoks, nk

            ensure_loaded(0)
            nxt_s = scores(0)
            yield
            for idx, (j, bi) in enumerate(items):
                pT, t_pT, vap, btoks, nk = nxt_s
                if idx + 1 < len(items):
                    nxt_s = scores(idx + 1)
                po, t_po = pos[j]
                pov = po[0:nq, 0:260].rearrange("p (g c) -> p g c", c=65)
                for gq in range(4):
                    MM(po[0:nq, gq * 65:(gq + 1) * 65], pT[0:nk, gq * nq:(gq + 1) * nq], vap, bi == 0 and gq == 0, bi == nb - 1,
                       [t_pT] + btoks, [t_po], skip=True)
                ensure_loaded(idx + 1)
                if bi == nb - 1:
                    TT("dve", den[0:nq, 4 * j:4 * j + 4], pov[:, :, 64], esink[0:nq, 4 * j:4 * j + 4], ALU.add, [t_po, t_esink], [t_den2[j]])
                    S.add("dve", lambda e, j=j, den=den: e.reciprocal(out=den[0:nq, 8 + 4 * j:12 + 4 * j], in_=den[0:nq, 4 * j:4 * j + 4]),
                          [t_den2[j]], [t_den2[j]])
                    TT("dve", ob[0:nq, 4 * j * 64:(4 * j + 4) * 64].rearrange("p (g c) -> p g c", c=64), pov[:, :, 0:64],
                       den[0:nq, 8 + 4 * j:12 + 4 * j].unsqueeze(2).broadcast_to([nq, 4, 64]), ALU.mult,
                       [t_po, t_den2[j]], [t_obh[4 * j + g_] for g_ in range(4)])
                yield
            pw, t_pw = PW()
            pwb = pw[:].bitcast(BF16)
            for c in range(4):
                TR(pwb[:, c * nq:(c + 1) * nq], ob[0:nq, c * 128:(c + 1) * 128], id16[0:nq, 0:nq], [t_obh[2 * c], t_obh[2 * c + 1], t_id16], [t_pw])
            CP("act", obT[:, :, ocol:ocol + nq], pwb[:, 0:4 * nq].rearrange("p (c q) -> p c q", c=4), [t_pw], [t_obT])
            yield

        def hgrn_out(o32, t_o32, osq, t_osq, sg, t_sg, og, t_og, tcol, ntk, pso, t_pso):
            pov = pso[:, 0:4 * ntk].rearrange("p (h t) -> p h t", h=4)
            ACT(o32[:, :, 0:ntk], pov, AF.Copy, [t_pso], [t_o32])
            ACT(osq[:, :, 0:ntk], pov, AF.Square, [t_pso], [t_osq])
            yield
            pss, t_pss = PW()
            MM(pss[:, 0:4 * ntk].rearrange("p (h t) -> p h t", h=4), ones16[:], osq[:, :, 0:ntk], True, True, [t_ones, t_osq], [t_pss])
            r, t_r = nxt(rso, "rso")
            rstd_from(pss[:, 0:4 * ntk], t_pss, r[:, 0:4 * ntk], t_r, 1.0 / 128, 4 * ntk)
            TT("dve", o32[:, :, 0:ntk], o32[:, :, 0:ntk], r[:, 0:4 * ntk].rearrange("p (h t) -> p h t", h=4), ALU.mult, [t_o32, t_r], [t_o32])
            TT("pool", og[:, :, tcol:tcol + ntk], o32[:, :, 0:ntk], sg[:, :, tcol:tcol + ntk], ALU.mult, [t_o32, t_sg], [t_og])

        def out_kv(kr, t_kr, v32, tcol, ntk, samp):
            v32t, t_v32 = v32
            pw, t_pw = PW()
            for j in range(2):
                TR(pw[0:ntk, j * 128:(j + 1) * 128], kr[:, j, tcol:tcol + ntk], id32[:], [t_kr, t_id32], [t_pw])
            stk, t_stk = st_k[0]
            CP("dve", stk[0:ntk, :].rearrange("p (j c) -> p j c", j=2), pw[0:ntk, 0:256].rearrange("p (j c) -> p j c", j=2)[:, :, 0:64],
               [t_pw], [t_stk])
            if not samp:
                DMA("sp", nk_d, stk[:, :], [t_stk], [], "o_nk")
                DMA("sp", nv_d, v32t[:, :], [t_v32], [], "o_nv")
            else:
                for s in range(NSEQ):
                    DMA("sp", nks_d[s, 124:128, :], stk[4 * s:4 * s + 4, :], [t_stk], [], "o_nk")
                    DMA("sp", nvs_d[s, 124:128, :], v32t[4 * s:4 * s + 4, :], [t_v32], [], "o_nv")

        def sample_states(pso, t_pso, qd, t_qd, ket, t_ket, vt, t_vt, Pt, t_P, deferred=()):
            deferred = list(deferred)
            sfl = dict(sample_pre["sf"])

            def load_s(s_):
                if s_ < NSEQ and s_ not in sfl:
                    sf_, t_sf_ = nxt(s0s, "s0f")
                    DMA("sp", sf_[:], s0_d[s_].rearrange("h d e -> d h e"), [], [t_sf_], "s0f%d" % (rr["s0f"] % 3))
                    sfl[s_] = (sf_, t_sf_)
            load_s(0)
            load_s(1)
            for s in range(NSEQ):
                if deferred and s % 3 == 2:
                    deferred.pop(0)()
                sf, t_sf = sfl.pop(s)
                for hd in range(4):
                    MM(pso[:, hd * 64 + s * 4:hd * 64 + s * 4 + 4], sf[:, hd, :], qd32[0][:, hd, s * 4:s * 4 + 4], False, s == NSEQ - 1,
                       [t_sf, qd32[1]], [t_pso], skip=True)
                km, t_km = nxt(kms, "km")
                TS("dve", km[:, :, :], ket[0:64, :, :], ind[:, s:s + 1], None, ALU.mult, ALU.bypass, [t_ket, t_ind], [t_km])
                pu, t_pu = PW()
                for hd in range(4):
                    MM(pu[:, hd * 128:(hd + 1) * 128], km[:, hd, :], vt[0:64, hd * 128:(hd + 1) * 128], True, True, [t_km, t_vt], [t_pu])
                sn, t_sn = nxt(sns, "sn")
                for hd in range(4):
                    pl = Pt[:, hd, s * 4 + 3:s * 4 + 4]
                    ACT(sn[:, hd, :], pu[:, hd * 128:(hd + 1) * 128], AF.Copy, [t_pu, t_P[hd]], [t_sn], scale=pl)
                    STT(sn[:, hd, :], sf[:, hd, :], pl, sn[:, hd, :], ALU.mult, ALU.add, [t_sf, t_P[hd], t_sn], [t_sn])
                DMA("act", nss_d[s].rearrange("h d e -> d h e"), sn[:], [t_sn], [], "o_nss%d" % (rr["sn"] % 2))
                load_s(s + 2)
                yield
            while deferred:
                deferred.pop(0)()

        sample_pre = {"kv": {}, "sf": {}}

        def sample_preload():
            for s_ in range(3):
                sample_pre["kv"][s_] = cached_loader(s_)()
            for s_ in range(2):
                sf_, t_sf_ = nxt(s0s, "s0f")
                DMA("sp", sf_[:], s0_d[s_].rearrange("h d e -> d h e"), [], [t_sf_], "s0f%d" % (rr["s0f"] % 3))
                sample_pre["sf"][s_] = (sf_, t_sf_)

        def cached_loader(s):
            def load():
                kc, t_kc = nxt(kcs, "kc")
                vc, t_vc = nxt(vcs, "vc")
                key = "kc%d" % (rr["kc"] % 3)
                for hf_ in range(2):
                    DMA("pool", kc[hf_ * 64:(hf_ + 1) * 64, :, :], ckT_d[s].rearrange("j c k -> c j k"), [], [t_kc], key)
                DMA("pool", vc[:, :, 0:64], cv_d[s].rearrange("k (j c) -> k j c", j=2), [], [t_vc], "vc%d" % (rr["vc"] % 3))
                return [kc[:, j_, :] for j_ in range(2)], [vc[:, j_, :] for j_ in range(2)], [t_kc, t_vc]
            return load

        def sample_attention(qr, t_qr, obT, t_obT, po_bank, po_bank2):
            blocks = []

            def cached_loader_unused(s):
                def load():
                    kc, t_kc = nxt(kcs, "kc")
                    vc, t_vc = nxt(vcs, "vc")
                    key = "kc%d" % (rr["kc"] % 3)
                    for hf_ in range(2):
                        DMA("pool", kc[hf_ * 64:(hf_ + 1) * 64, :, :], ckT_d[s].rearrange("j c k -> c j k"), [], [t_kc], key)
                    DMA("pool", vc[:, :, 0:64], cv_d[s].rearrange("k (j c) -> k j c", j=2), [], [t_vc], "vc%d" % (rr["vc"] % 3))
                    return [kc[:, j_, :] for j_ in range(2)], [vc[:, j_, :] for j_ in range(2)], [t_kc, t_vc]
                return load
            for s in range(NSEQ):
                blocks.append((cached_loader(s), negsc[:, s, :].unsqueeze(1).broadcast_to([128, 4, 64]), 128))
                DMA("sp", nks_d[s, 0:124, :], ck_d[s, 4:128, :], [], [], "o_nkc")
                DMA("sp", nvs_d[s, 0:124, :], cv_d[s, 4:128, :], [], [], "o_nkc")
            blocks.append((lambda: ([k2t[:, j_, 128 + NMAIN:128 + NMAIN + 64] for j_ in range(2)], [vaug[0:64, 17, j_, :] for j_ in range(2)],
                                    [t_k2t[17], t_vaug[17]]), negsn[:, :, :], 64))
            yield from attention(qr, t_qr, 0, 64, blocks, obT, t_obT, 0, po_bank, po_bank2, sample_pre["kv"])

        NG = NPRE // G
        NGM = NMAIN // G
        jobs = [("pre", g) for g in range(NG)] + [("main", g) for g in range(NGM)] + [("samp", 0)]

        def job_x(job):
            kind, g = job
            if kind == "pre":
                return (g * G, G)
            if kind == "main":
                return (NPRE + g * G, G)
            return (NPRE + NMAIN, NSAMP)

        def stage1(ji, ctx):
            kind, gi = jobs[ji]
            samp = kind == "samp"
            nt = NSAMP if samp else G
            xt, t_x = xq.pop(0)
            h, t_h = norm_h(xt, t_x, nt, 0)
            if ji + 1 < len(jobs):
                xq.append(load_x(xT, *job_x(jobs[ji + 1]), []))
            ctx.update(h=h, t_h=t_h, nt=nt, samp=samp, kind=kind, gi=gi)
            if samp:
                sample_preload()
            yield
            Pt, _ = nxt(Pbs, "Pb")
            ke, _ = nxt(kes, "ke")
            t_P = t_P_all[(rr["Pb"] - 1) % 2]
            t_ke = t_ke_all[(rr["ke"] - 1) % 2]
            ke2, t_ke2 = nxt(ke2s, "ke2")
            ctx.update(Pt=Pt, t_P=t_P, ke=ke, t_ke=t_ke, ke2=ke2, t_ke2=t_ke2)
            fks = []
            sg = None
            if kind != "pre":
                sg, t_sg = nxt(sgs, "sg")
            for hd in range(4):
                fks.append(gates_a(h, t_h, nt, hd))
            if kind != "pre":
                for hd in range(4):
                    pg, t_pg = proj_fm(h, t_h, 8 + hd, nt, [t_w["hg"]])
                    sgt, t_sgt = nxt(sgts, "sgt")
                    ACT(sgt[:, 0:nt], pg[:, 0:nt], AF.Sigmoid, [t_pg], [t_sgt])
                    STT(sg[:, hd, 0:nt], pg[:, 0:nt], hgnw[:, hd:hd + 1], sgt[:, 0:nt], ALU.mult, ALU.mult, [t_pg, t_hgnw, t_sgt], [t_sg])
            yield
            for hd in range(4):
                gates_b(fks[hd], nt, samp, hd, Pt, t_P, ke, t_ke, ke2, t_ke2)
                yield
            if kind == "pre":
                if gi == NG - 1:
                    cst, t_cs = load_cs(0, 128)
                    for j in range(2):
                        k_chunk(h, t_h, cst, t_cs, 128, 0, [0], j, hsl=(G - 128, G))
                        yield
                    proj_sv(h, t_h, G - 128, 128, 0, False)
                    yield
                return
            cscol0 = 128 + (NMAIN if samp else gi * G)
            cst, t_cs = load_cs(cscol0, nt)
            qd, t_qd = nxt(qds, "qd")
            for hd in range(4):
                pb, t_pb = proj_fm(h, t_h, hd, nt, [t_w["hq"]])
                TT("dve", qd[:, hd, 0:nt], pb[:, 0:nt], Pt[:, hd, 0:nt], ALU.mult, [t_pb, t_P[hd]], [t_qd])
                if samp:
                    TT("dve", qd32[0][:, hd, 0:nt], pb[:, 0:nt], Pt[:, hd, 0:nt], ALU.mult, [t_pb, t_P[hd]], [qd32[1]])
                if hd % 2 == 1:
                    yield
            qr, t_qr = nxt(qrs, "qr")
            for c in range(4):
                rotary_chunk(h, t_h, 12 + c, 18 + c, "sq", cst, t_cs, 0, nt, (qr[0:64, c, 0, 0:nt], qr[64:128, c, 1, 0:nt]), [t_qr])
                yield
            tiles = [17] if samp else [1 + 2 * gi, 2 + 2 * gi]
            for j in range(2):
                kr, t_kr = k_chunk(h, t_h, cst, t_cs, nt, cscol0, tiles, j)
                yield
            v32 = None
            lastg_ = (not samp) and gi == NGM - 1
            for t in range(1 if samp else G // 128):
                ntk = nt if samp else 128
                w32 = samp or (lastg_ and t == G // 128 - 1)
                r_ = proj_sv(h, t_h, t * 128, ntk, tiles[t], w32)
                if r_ is not None:
                    v32 = r_
                yield
            ctx.update(qd=qd, t_qd=t_qd, sg=sg, t_sg=t_sg, qr=qr, t_qr=t_qr, kr=kr, t_kr=t_kr, tiles=tiles, cscol0=cscol0, v32=v32)

        po_att = (banks[7], nt_[3])
        narrow[:] = [e for e in narrow if e[1] is not nt_[3]]

        def stage2h(ctx):
            kind, gi, samp, nt = ctx["kind"], ctx["gi"], ctx["samp"], ctx["nt"]
            h, t_h, Pt, t_P, ke, t_ke = ctx["h"], ctx["t_h"], ctx["Pt"], ctx["t_P"], ctx["ke"], ctx["t_ke"]
            if kind == "pre":
                for t in range(G // 128):
                    halo = (gi == NG - 1) and t == G // 128 - 1
                    vt, t_vt = proj_tm(h, t_h, t * 128, 128)
                    ket, t_ket = ke_transpose(ctx["ke2"], ctx["t_ke2"], t * 128, 128)
                    yield
                    pus = state_U(ket, t_ket, vt, t_vt)
                    for c in range(2):
                        state_chain(c, pus, Pt, t_P, t * 128, halo and c == 1)
                    yield
                if gi == NG - 1:
                    assert chunk_ctr["n"] % 2 == 0
                return
            if ABL == "nohg" and not samp:
                return
            qd, t_qd, sg, t_sg = ctx["qd"], ctx["t_qd"], ctx["sg"], ctx["t_sg"]
            ocol0 = NMAIN if samp else gi * G
            lastg = (not samp) and gi == NGM - 1
            if samp:
                A2v = Arena(big, BIGN)
                wg_v = A2v.alloc([128, 8, 2048], BF16)
                wua_v = A2v.alloc([128, 4, 1024], BF16)
                wub_v = A2v.alloc([128, 4, 1024], BF16)
                assert A2v.off <= 8 * NFM
                wdead = [t_w[n_] for n_ in ("hq", "hf", "hg", "sq", "k", "rot")]
                w2pre = [
                    lambda: DMA("pool", wg_v[:, :, 0:1024], w_in_v[:, :, 2816:3840], [], wdead, "w2pre"),
                    lambda: DMA("pool", wua_v, w_ua.rearrange("(k p) c -> p k c", p=128), [], wdead, "w2pre"),
                    lambda: DMA("pool", wub_v, w_ub.rearrange("(k p) c -> p k c", p=128), [], wdead, "w2pre"),
                    lambda: DMA("pool", wg_v[:, :, 1024:2048], w_in_v[:, :, 3840:4864], [], wdead, "w2pre"),
                ]
            og, t_og = nxt(ogs, "og")
            ntile = 1 if samp else G // 128
            for t in range(ntile):
                ntk = nt if samp else 128
                tcol = t * 128
                vt, t_vt = proj_tm(h, t_h, tcol, ntk)
                yield
                ket, t_ket = ke_transpose(ke if samp else ctx["ke2"], t_ke if samp else ctx["t_ke2"], tcol, ntk)
                pa, t_pa = PW()
                for hd in range(4):
                    MM(pa[0:ntk, hd * ntk:(hd + 1) * ntk], ke[:, hd, tcol:tcol + ntk], qd[:, hd, tcol:tcol + ntk], True, True, [t_ke[hd], t_qd], [t_pa])
                am, t_am = nxt(ams, "am")
                msk = mhgs[:, :, :] if samp else mhg[:, :, :]
                TT("dve", am[0:ntk, :, 0:ntk], pa[0:ntk, 0:4 * ntk].rearrange("p (h t) -> p h t", h=4), msk, ALU.mult,
                   [t_pa, t_mhg, t_mhgs], [t_am])
                yield
                if not samp:
                    pus = state_U(ket, t_ket, vt, t_vt)
                pso, t_pso = plong
                for hd in range(4):
                    MM(pso[:, hd * ntk:(hd + 1) * ntk], vt[0:ntk, hd * 128:(hd + 1) * 128], am[0:ntk, hd, 0:ntk], hd == 0, False, [t_vt, t_am], [t_pso], skip=True)
                if not samp:
                    for c in range(2):
                        p_ = chunk_ctr["n"] % 2
                        for hd in range(4):
                            MM(pso[:, hd * ntk + c * 64:hd * ntk + (c + 1) * 64], Sbf2[p_][:, hd, :], qd[:, hd, tcol + c * 64:tcol + (c + 1) * 64],
                               False, True, [t_Sbf2[p_][hd], t_qd], [t_pso], skip=True)
                        state_chain(c, pus, Pt, t_P, tcol, True)
                    yield
                else:
                    yield from sample_states(pso, t_pso, qd, t_qd, ket, t_ket, vt, t_vt, Pt, t_P, w2pre)
                o32, t_o32 = nxt(o32s, "o32")
                osq, t_osq = nxt(osqs, "osq")
                yield from hgrn_out(o32, t_o32, osq, t_osq, sg, t_sg, og, t_og, tcol, ntk, pso, t_pso)
                yield
            DMA("sp", ogs_d[:, ocol0:ocol0 + nt].rearrange("(k p) n -> p k n", p=128), og[:, :, 0:nt], [t_og], [], "st_og%d" % (rr["og"] % 2))
            if lastg:
                DMA("sp", ns_d.rearrange("h d e -> d h e"), S32[:], t_S32, [], "o_ns")

        def stage2a(ctx):
            kind, gi, samp, nt = ctx["kind"], ctx["gi"], ctx["samp"], ctx["nt"]
            if kind == "pre" or (ABL == "noatt" and not samp):
                return
            qr, t_qr, kr, t_kr, tiles, cscol0 = ctx["qr"], ctx["t_qr"], ctx["kr"], ctx["t_kr"], ctx["tiles"], ctx["cscol0"]
            ocol0 = NMAIN if samp else gi * G
            lastg = (not samp) and gi == NGM - 1
            obT, t_obT = nxt(obTs, "obT")
            ntile = 1 if samp else G // 128
            for t in range(ntile):
                ntk = nt if samp else 128
                tcol = t * 128
                tidx = tiles[t]
                want32 = samp or (lastg and t == ntile - 1)
                if not samp:
                    kcol = cscol0 + tcol

                    def mk(kc, ti, mi):
                        return (lambda kc=kc, ti=ti: ([k2t[:, j_, kc:kc + 128] for j_ in range(2)], [vaug[:, ti, j_, :] for j_ in range(2)],
                                                      [t_k2t[ti], t_vaug[ti]]),
                                negm[:, mi, :, :], 128)
                    blocks = [mk(kcol - 128, tidx - 1, 2 if tidx == 1 else 1), mk(kcol, tidx, 0)]
                    yield from attention(qr, t_qr, tcol, 128, blocks, obT, t_obT, tcol, po_att)
                else:
                    po_att2 = (banks[6], nt_[2])
                    narrow[:] = [e for e in narrow if e[1] is not nt_[2]]
                    yield from sample_attention(qr, t_qr, obT, t_obT, po_att, po_att2)
                if want32:
                    out_kv(kr, t_kr, ctx["v32"], tcol, ntk, samp)
                    yield
            DMA("sp", obs_d[:, ocol0:ocol0 + nt].rearrange("(k p) n -> p k n", p=128), obT[:, :, 0:nt], [t_obT], [], "st_ob%d" % (rr["obT"] % 2))

        step_counts = {}

        def interleave(named_gens, speed=None):
            gens = [(nm, g) for nm, g in named_gens if g is not None]
            done = {nm: 0 for nm, _ in gens}
            tot = {nm: (counts or {}).get(nm, 1) / (speed or {}).get(nm.split("_")[0], 1.0) for nm, _ in gens}
            live = dict(gens)
            while live:
                nm = min(live, key=lambda n: done[n] / max(tot[n], 1))
                try:
                    next(live[nm])
                    done[nm] += 1
                except StopIteration:
                    del live[nm]
            for nm, _ in gens:
                step_counts[nm] = done[nm]

        def run_pipelined():
            ctxs = [dict() for _ in jobs]
            xq.append(load_x(xT, *job_x(jobs[0]), []))
            for _ in stage1(0, ctxs[0]):
                pass
            for ji in range(len(jobs)):
                g2h = stage2h(ctxs[ji])
                g2a = stage2a(ctxs[ji])
                g1 = stage1(ji + 1, ctxs[ji + 1]) if ji + 1 < len(jobs) else None
                interleave([("a1s2h_%d" % ji, g2h), ("a1s2a_%d" % ji, g2a), ("a1s1_%d" % (ji + 1), g1)], speed={"a1s1": 0.9, "a1s2h": 0.55, "a1s2a": 0.8})
                if stop == "pre" and jobs[ji] == ("pre", NG - 1):
                    return True
            return False

        xq = []
        if run_pipelined():
            finish()
            return nc
        if stop == "A1":
            finish()
            return nc

        S.barrier()
        A.reset()
        rr.clear()
        narrow[:] = narrow_all
        wg = A.alloc([128, 8, 2048], BF16)
        wua = A.alloc([128, 4, 1024], BF16)
        wub = A.alloc([128, 4, 1024], BF16)
        wo = A.alloc([128, 8, 1024], BF16)
        t_w2 = {n: Tok("w2_" + n) for n in ("g", "ua", "ub", "wo")}
        DMA("pool", wo, w_o.rearrange("(k p) c -> p k c", p=128), [], [t_w2["wo"]], "w2_wo")
        xs = slots("xs", [128, 8, G], F32, 2)
        xsq = slots("xsq", [128, G], BF16, 2)
        rstd = slots("rstd", [128, G], F32, 2)
        hTs = slots("hT", [128, 8, G], BF16, 2)
        og2 = slots("og2", [128, 4, G], BF16, 2)
        ob2 = slots("ob2", [128, 4, G], BF16, 2)
        sga = slots("sga", [128, G], F32, 2)
        sgb = slots("sgb", [128, G], F32, 2)
        mt1 = slots("mt1", [128, G], F32, 2)
        mt2 = slots("mt2", [128, G], F32, 2)
        mTs = slots("mT", [128, 8, G], BF16, 2)
        t_mT_all = [[Tok("mT%d_%d" % (s_, i)) for i in range(8)] for s_ in range(2)]
        x1o = slots("x1o", [128, G], F32, 3)
        assert A.off <= BIGN - 8 * 4096, ("phase A2 working set overlaps the wf1 prefetch region", A.off)
        wf1_pre = big[:, BIGN - 8 * 4096:BIGN].rearrange("p (k c) -> p k c", k=8)
        f1v = w_f1.rearrange("(k p) c -> p k c", p=128)
        for q in range(4):
            DMA("pool", wf1_pre[:, :, q * 1024:(q + 1) * 1024], f1v[:, :, q * 1024:(q + 1) * 1024], [], [], "wB%d" % q)

        def load_a2(col0, nt):
            xt, t_x = load_x(xT, NPRE + col0, nt, [])
            og, t_og = nxt(og2, "og2")
            ob, t_ob = nxt(ob2, "ob2")
            DMA("sp", og[:, :, 0:nt], ogs_d[:, col0:col0 + nt].rearrange("(k p) n -> p k n", p=128), [], [t_og], "og2%d" % (rr["og2"] % 2))
            DMA("sp", ob[:, :, 0:nt], obs_d[:, col0:col0 + nt].rearrange("(k p) n -> p k n", p=128), [], [t_ob], "ob2%d" % (rr["ob2"] % 2))
            return (xt, t_x, og, t_og, ob, t_ob)

        def a2_s1(ld, nt, res):
            xt, t_x = ld[0], ld[1]
            yield from norm_h_gen(xt, t_x, nt, 0, res)

        def a2_s2(ld, res, nt, ocol0, nxt_ld, out):
            xt, t_x, og, t_og, obT, t_obT = ld
            h, t_h = res["h"], res["t_h"]
            mT, _ = nxt(mTs, "mT")
            t_mTc = t_mT_all[(rr["mT"] - 1) % 2]
            for jc in range(8):
                pga, t_pga = PN()
                for k in range(8):
                    MM(pga[:, 0:nt], wg[:, k, jc * 128:(jc + 1) * 128], h[:, k, 0:nt], k == 0, k == 7, [t_w2["g"], t_h], [t_pga])
                a, t_a = nxt(sga, "sga")
                ACT(a[:, 0:nt], pga[:, 0:nt], AF.Sigmoid, [t_pga], [t_a])
                pgb, t_pgb = PN()
                for k in range(8):
                    MM(pgb[:, 0:nt], wg[:, k, 1024 + jc * 128:1024 + (jc + 1) * 128], h[:, k, 0:nt], k == 0, k == 7, [t_w2["g"], t_h], [t_pgb])
                b, t_b = nxt(sgb, "sgb")
                ACT(b[:, 0:nt], pgb[:, 0:nt], AF.Sigmoid, [t_pgb], [t_b])
                pya, t_pya = PN()
                for k in range(4):
                    MM(pya[:, 0:nt], wua[:, k, jc * 128:(jc + 1) * 128], og[:, k, 0:nt], k == 0, k == 3, [t_w2["ua"], t_og], [t_pya])
                pyb, t_pyb = PN()
                for k in range(4):
                    MM(pyb[:, 0:nt], wub[:, k, jc * 128:(jc + 1) * 128], obT[:, k, 0:nt], k == 0, k == 3, [t_w2["ub"], t_obT], [t_pyb])
                m1, t_m1 = nxt(mt1, "mt1")
                m2, t_m2 = nxt(mt2, "mt2")
                TT("dve", m1[:, 0:nt], pya[:, 0:nt], a[:, 0:nt], ALU.mult, [t_pya, t_a], [t_m1])
                TT("dve", m2[:, 0:nt], pyb[:, 0:nt], b[:, 0:nt], ALU.mult, [t_pyb, t_b], [t_m2])
                TT("pool", mT[:, jc, 0:nt], m1[:, 0:nt], m2[:, 0:nt], ALU.add, [t_m1, t_m2], [t_mTc[jc]])
                yield
            for jc in range(8):
                px, t_px = PN()
                for k in range(8):
                    MM(px[:, 0:nt], wo[:, k, jc * 128:(jc + 1) * 128], mT[:, k, 0:nt], k == 0, k == 7, [t_w2["wo"], t_mTc[k]], [t_px])
                xo, t_xo = nxt(x1o, "x1o")
                TT("dve", xo[:, 0:nt], px[:, 0:nt], xt[:, jc, 0:nt], ALU.add, [t_px, t_x], [t_xo])
                DMA("sp", x1s[jc * 128:(jc + 1) * 128, ocol0:ocol0 + nt], xo[:, 0:nt], [t_xo], [], "x1o%d" % (rr["x1o"] % 3))
                yield
            out["ld"] = load_a2(*nxt_ld) if nxt_ld is not None else None

        a2_jobs = [(gi * G, G) for gi in range(NGM)] + [(NMAIN, NSAMP)]
        lds = {0: load_a2(*a2_jobs[0])}
        if len(a2_jobs) > 1:
            lds[1] = load_a2(*a2_jobs[1])
        ress = [dict() for _ in a2_jobs]
        for _ in a2_s1(lds[0], a2_jobs[0][1], ress[0]):
            pass
        for ji in range(len(a2_jobs)):
            out = {}
            nl = a2_jobs[ji + 2] if ji + 2 < len(a2_jobs) else None
            g2 = a2_s2(lds[ji], ress[ji], a2_jobs[ji][1], a2_jobs[ji][0], nl, out)
            g1 = a2_s1(lds[ji + 1], a2_jobs[ji + 1][1], ress[ji + 1]) if ji + 1 < len(a2_jobs) else None
            interleave([("a2s2_%d" % ji, g2), ("a2s1_%d" % (ji + 1), g1)], speed={"a2s1": 0.6})
            if out.get("ld") is not None:
                lds[ji + 2] = out["ld"]
        if stop == "A2":
            finish()
            return nc

        S.barrier()
        A.reset()
        rr.clear()
        wf2 = A.alloc([128, 32, 1024], BF16)
        t_wB = [Tok("wB%d" % i) for i in range(8)]
        f2v = w_f2.rearrange("(k p) c -> p k c", p=128)
        for q in range(4):
            DMA("pool", wf2[:, q * 8:(q + 1) * 8, :], f2v[:, q * 8:(q + 1) * 8, :], [], [t_wB[4 + q]], "wB%d" % (4 + q))
        xs = slots("xs", [128, 8, G], F32, 2)
        xsq = slots("xsq", [128, G], BF16, 4)
        rstd = slots("rstd", [128, G], F32, 2)
        hTs = slots("hT", [128, 8, G], BF16, 2)
        aTs = slots("aT", [128, 32, G], BF16, 1)
        t_aTc = [Tok("aT%d" % i) for i in range(32)]
        rl = slots("rl", [128, G], F32, 3)
        x2s = slots("x2", [128, 8, G], F32, 1)
        t_x2c = [Tok("x2_%d" % i) for i in range(8)]
        yts = slots("yt", [128, 1024], F32, 2)
        assert A.off <= BIGN - 8 * 4096, ("phase B working set overlaps wf1", A.off)
        print("B arena bytes", A.off * 2)
        wf1 = wf1_pre

        plong2 = wide[2]
        del wide[2]

        def b_s1(ld, nt, res):
            yield from norm_h_gen(ld[0], ld[1], nt, 8, res)

        def b_s2(ld, res, nt, nxt_ld, out):
            xt, t_x = ld
            h, t_h = res["h"], res["t_h"]
            aT, _ = aTs[0]
            for c in range(32):
                pb, t_pb = PN()
                for k in range(8):
                    MM(pb[:, 0:nt], wf1[:, k, c * 128:(c + 1) * 128], h[:, k, 0:nt], k == 0, k == 7, [t_wB[c // 8], t_h], [t_pb])
                r, t_r = nxt(rl, "rl")
                ACT(r[:, 0:nt], pb[:, 0:nt], AF.Relu, [t_pb], [t_r])
                TT("pool" if c % 2 else "dve", aT[:, c, 0:nt], r[:, 0:nt], r[:, 0:nt], ALU.mult, [t_r], [t_aTc[c]])
                yield
            x2, _ = x2s[0]
            pss, t_pss = plong2
            pend = None
            for jc in range(8):
                pb, t_pb = PN()
                for k in range(32):
                    MM(pb[:, 0:nt], wf2[:, k, jc * 128:(jc + 1) * 128], aT[:, k, 0:nt], k == 0, k == 31, [t_wB[4 + k // 8], t_aTc[k]], [t_pb])
                TT("dve", x2[:, jc, 0:nt], pb[:, 0:nt], xt[:, jc, 0:nt], ALU.add, [t_pb, t_x], [t_x2c[jc]])
                q, t_q = nxt(xsq, "xsq")
                ACT(q[:, 0:nt], x2[:, jc, 0:nt], AF.Square, [t_x2c[jc]], [t_q])
                if pend is not None:
                    MM(pss[:, 0:nt], ones16[:], pend[0][:, 0:nt], pend[2] == 0, False, [t_ones, pend[1]], [t_pss])
                pend = (q, t_q, jc)
                yield
            MM(pss[:, 0:nt], ones16[:], pend[0][:, 0:nt], False, True, [t_ones, pend[1]], [t_pss])
            out["ld"] = load_x(x1s, nxt_ld[0], nxt_ld[1], []) if nxt_ld is not None else None

        def b_s3(col0, nt):
            x2, _ = x2s[0]
            pss, t_pss = plong2
            r, t_r = nxt(rstd, "rstd")
            rstd_from(pss[:, 0:nt], t_pss, r[:, 0:nt], t_r, 1.0 / D, nt)
            yield
            for jc in range(8):
                STT(x2[:, jc, 0:nt], x2[:, jc, 0:nt], ncol[:, 16 + jc:17 + jc], r[:, 0:nt], ALU.mult, ALU.mult, [t_x2c[jc], t_ncol, t_r], [t_x2c[jc]])
                if jc % 4 == 3:
                    yield
            for t in range((nt + 127) // 128):
                ntk = min(128, nt - t * 128)
                yt, t_yt = nxt(yts, "yt")
                for half in range(2):
                    pw, t_pw = PW()
                    for jj in range(4):
                        jc = half * 4 + jj
                        TR(pw[0:ntk, jj * 128:(jj + 1) * 128], x2[:, jc, t * 128:t * 128 + ntk], id32[:], [t_x2c[jc], t_id32], [t_pw])
                    CP("act" if half else "dve", yt[0:ntk, half * 512:(half + 1) * 512], pw[0:ntk, 0:512], [t_pw], [t_yt])
                    yield
                DMA("sp", y_d[col0 + t * 128:col0 + t * 128 + ntk, :], yt[0:ntk, :], [t_yt], [], "o_y%d" % (rr["yt"] % 2))

        b_jobs = [(gi * G, G) for gi in range(NGM)] + [(NMAIN, NSAMP)]
        ldsB = {0: load_x(x1s, b_jobs[0][0], b_jobs[0][1], []), 1: load_x(x1s, b_jobs[1][0], b_jobs[1][1], [])}
        resB = [dict() for _ in b_jobs]
        for _ in b_s1(ldsB[0], b_jobs[0][1], resB[0]):
            pass
        for ji in range(len(b_jobs)):
            out = {}
            nl = b_jobs[ji + 2] if ji + 2 < len(b_jobs) else None
            g2 = b_s2(ldsB[ji], resB[ji], b_jobs[ji][1], nl, out)
            g1 = b_s1(ldsB[ji + 1], b_jobs[ji + 1][1], resB[ji + 1]) if ji + 1 < len(b_jobs) else None
            g3 = b_s3(b_jobs[ji - 1][0], b_jobs[ji - 1][1]) if ji > 0 else None
            interleave([("bs2_%d" % ji, g2), ("bs1_%d" % (ji + 1), g1), ("bs3_%d" % (ji - 1), g3)], speed={"bs3": 0.35, "bs1": 0.6})
            if out.get("ld") is not None:
                ldsB[ji + 2] = out["ld"]
        for _ in b_s3(b_jobs[-1][0], b_jobs[-1][1]):
            pass

        finish()
    if count_only:
        return step_counts
    return nc


def build_two_pass(stop=None):
    counts = build_program(None, None, True)
    return build_program(stop, counts, False)


_NC_CACHE = {}


def _consts():
    half = 8
    inv = np.exp(-math.log(500000.0) * np.arange(half, dtype=np.float32) * np.float32(2.0 / 16)).astype(np.float32)
    s = np.arange(128)
    m_hg = ((s[:, None] <= s[None, :]) & ((s[:, None] // 64) == (s[None, :] // 64))).astype(np.float32)
    s64 = np.arange(64)
    m_hg_s = ((s64[:, None] <= s64[None, :]) & ((s64[:, None] // 4) == (s64[None, :] // 4))).astype(np.float32)
    own = np.where(s[:, None] <= s[None, :], 0.0, NEG).astype(np.float32)
    prev = np.where(s[:, None] > s[None, :], 0.0, NEG).astype(np.float32)
    allneg = np.full((128, 128), NEG, np.float32)
    negsc = np.full((128, NSEQ, 64), NEG, np.float32)
    for q in range(64):
        sq, tq = q // 4, q % 4
        negsc[tq + 1:, sq, q] = 0.0
    negsn = np.where(((s64[:, None] // 4) == (s64[None, :] // 4)) & (s64[:, None] <= s64[None, :]), 0.0, NEG).astype(np.float32)
    ind = (s64[:, None] // 4 == np.arange(NSEQ)[None, :]).astype(np.float32)
    return inv, m_hg, m_hg_s, own, prev, allneg, negsc, negsn, ind


def _perm():
    p = np.zeros((128, 128), np.float32)
    for d in range(128):
        i = d % 64
        s = d + 8 if i < 8 else (d - 8 if i < 16 else d)
        p[s, d] = 1.0
    return p


def _cs_table(pos, inv):
    ang = pos.astype(np.float32)[None, :] * inv[:, None]
    c = np.cos(ang.astype(np.float64)).astype(np.float32)
    sn = np.sin(ang.astype(np.float64)).astype(np.float32)
    n = pos.shape[0]
    C = np.ones((64, n), np.float32)
    Sg = np.zeros((64, n), np.float32)
    C[0:8] = c
    C[8:16] = c
    Sg[0:8] = -sn
    Sg[8:16] = sn
    out = np.stack([np.concatenate([C, C], 0), np.concatenate([Sg, Sg], 0)], axis=1)
    return np.ascontiguousarray(out)


def prepare(x_prompt, x_sample, cache_swa_k, cache_swa_v, state_hgrn, w_in, hgrn_lb_logits, hgrn_norm_w,
            sinks, w_up_a, w_up_b, w_o, norm1_w, norm2_w, w_ff1, w_ff2, normf_w, cores=range(8)):
    f = lambda a: np.ascontiguousarray(np.asarray(a, dtype=np.float32))
    x_prompt, x_sample = f(x_prompt), f(x_sample)
    ck_all, cv_all, s0_all = f(cache_swa_k)[0], f(cache_swa_v)[0], f(state_hgrn)[0]
    inv, m_hg, m_hg_s, own, prev, allneg, negsc, negsn, ind = _consts()
    col = lambda v: np.ascontiguousarray(f(v).reshape(8, 128).T)
    ncols = np.concatenate([col(norm1_w[0]), col(norm2_w[0]), col(normf_w)], axis=1)
    lbl = np.ascontiguousarray(f(hgrn_lb_logits).reshape(2, 4, 128).transpose(2, 0, 1).reshape(128, 8))
    hgnw = np.ascontiguousarray(f(hgrn_norm_w)[0].T)
    shared = {
        "w_in": f(w_in)[0], "w_up_a": f(w_up_a)[0], "w_up_b": f(w_up_b)[0], "w_o": f(w_o)[0],
        "w_ff1": f(w_ff1)[0], "w_ff2": f(w_ff2)[0], "lbl": lbl, "hgnw": hgnw, "ncols": ncols,
        "sinks": f(sinks).reshape(1, 8), "m_hg": m_hg, "m_hg_s": m_hg_s, "negm_sc": negsc, "negm_sn": negsn,
        "ind16": ind, "ident": np.eye(128, dtype=np.float32), "perm": _perm(),
    }
    in_maps = []
    for c in cores:
        b, hf = c // 2, c % 2
        xT = np.zeros((D, NX), np.float32)
        if hf == 1:
            xT[:, 0:NPRE] = x_prompt[b, 0:2048].T
        xT[:, NPRE:NPRE + NMAIN] = x_prompt[b, hf * 2048:(hf + 1) * 2048].T
        xT[:, NPRE + NMAIN:] = x_sample[16 * c:16 * c + 16].reshape(64, D).T
        base = hf * 2048
        pos = np.concatenate([np.arange(base - 128, base + 2048), PAST_LEN + np.tile(np.arange(4), NSEQ)])
        negm = np.stack([own, prev, prev if hf == 1 else allneg], axis=1)
        m = dict(shared)
        m["xT"] = xT
        m["cs"] = _cs_table(pos, inv)
        m["negm"] = np.ascontiguousarray(negm)
        ck = ck_all[16 * c:16 * c + 16]
        m["ckT"] = np.ascontiguousarray(ck.transpose(0, 2, 3, 1))
        m["ck"] = np.ascontiguousarray(ck.reshape(16, 128, 128))
        m["cv"] = np.ascontiguousarray(cv_all[16 * c:16 * c + 16].reshape(16, 128, 128))
        m["s0"] = np.ascontiguousarray(s0_all[16 * c:16 * c + 16])
        in_maps.append(m)
    return in_maps


def assemble(R):
    y_prompt = np.zeros((4, 4096, D), np.float32)
    y_sample = np.zeros((128, 4, D), np.float32)
    nkp = np.zeros((1, 4, 128, 2, 64), np.float32)
    nvp = np.zeros((1, 4, 128, 2, 64), np.float32)
    nsp = np.zeros((1, 4, 4, 128, 128), np.float32)
    nks = np.zeros((1, 128, 128, 2, 64), np.float32)
    nvs = np.zeros((1, 128, 128, 2, 64), np.float32)
    nss = np.zeros((1, 128, 4, 128, 128), np.float32)
    for c in range(8):
        b, hf = c // 2, c % 2
        r = R[c]
        y_prompt[b, hf * 2048:(hf + 1) * 2048] = r["y"][0:2048]
        y_sample[16 * c:16 * c + 16] = r["y"][2048:2112].reshape(16, 4, D)
        if hf == 1:
            nkp[0, b] = r["nk"].reshape(128, 2, 64)
            nvp[0, b] = r["nv"].reshape(128, 2, 64)
            nsp[0, b] = r["ns"]
        nks[0, 16 * c:16 * c + 16] = r["nks"].reshape(16, 128, 2, 64)
        nvs[0, 16 * c:16 * c + 16] = r["nvs"].reshape(16, 128, 2, 64)
        nss[0, 16 * c:16 * c + 16] = r["nss"]
    return (y_prompt, y_sample, nkp, nvp, nsp, nks, nvs, nss)


def kernel(**inputs):
    in_maps = prepare(**inputs)
    if "nc" not in _NC_CACHE:
        _NC_CACHE["nc"] = build_two_pass()
    nc = _NC_CACHE["nc"]
    res = run_bass_kernel_spmd(nc, in_maps, core_ids=list(range(8)))
    return assemble(res.results)
```
